# Optimizing a Trainium2 kernel written in Bass

```python
import math
import jax, jax.numpy as jnp
from jax import lax
import numpy as np

D_MODEL = 1024
BATCH = 16
SEQ = 2048
DEPTH = 2

D_MIX = D_MODEL
D_A = D_MIX // 2
D_B = D_MIX - D_A
G_A = 8
DG_A = D_A // G_A
H_B = 8
DH_B = D_B // H_B
CHUNK = 128
Q_BLOCK = 128
PROJ_COLS = 2 * D_A + 3 * D_B
PEER_HEADS = 8
PEER_DK = 128
PEER_HALF = PEER_DK // 2
N_KEYS = 128
N_EXPERTS = N_KEYS * N_KEYS
PEER_TOPK = 16
PEER_TOK_BLOCK = 128
EPS = 1e-6

kernel_name = "hybrid_gmlp_stickbreak_peer_adaln"


def rms_norm(x, g):
    xf = x.astype(jnp.float32)
    y = xf * lax.rsqrt(jnp.mean(xf * xf, axis=-1, keepdims=True) + EPS)
    return (y * g.astype(jnp.float32)).astype(x.dtype)


def group_rms_norm(x, g, n_groups):
    shp = x.shape
    xg = x.reshape(shp[:-1] + (n_groups, shp[-1] // n_groups))
    xf = xg.astype(jnp.float32)
    y = xf * lax.rsqrt(jnp.mean(xf * xf, axis=-1, keepdims=True) + EPS)
    y = y.reshape(shp) * g.astype(jnp.float32)
    return y.astype(x.dtype)


def chunked_gmlp(u, v, w_s, b_s):
    B, T, _ = v.shape
    nc = T // CHUNK
    vg = v.reshape(B, nc, CHUNK, G_A, DG_A).astype(jnp.float32)
    mu = jnp.mean(vg, axis=-1, keepdims=True)
    var = jnp.mean(jnp.square(vg - mu), axis=-1, keepdims=True)
    vn = ((vg - mu) * lax.rsqrt(var + EPS)).astype(v.dtype)
    w_causal = jnp.tril(w_s)
    s = jnp.einsum("gts,bnsgd->bntgd", w_causal, vn)
    s = s + jnp.transpose(b_s)[None, None, :, :, None]
    return u * s.reshape(B, T, D_A)


def stick_breaking_attention(q, k, v):
    B, H, T, d = q.shape
    scale = 1.0 / math.sqrt(d)
    outs = []
    for i in range(T // Q_BLOCK):
        t0 = i * Q_BLOCK
        kend = t0 + Q_BLOCK
        qb = q[:, :, t0:kend]
        kb = k[:, :, :kend]
        vb = v[:, :, :kend]
        z = jnp.einsum("bhtd,bhsd->bhts", qb, kb).astype(jnp.float32) * scale
        t_idx = t0 + jnp.arange(Q_BLOCK)[:, None]
        s_idx = jnp.arange(kend)[None, :]
        mask = s_idx < t_idx
        log1m = jnp.where(mask, jax.nn.log_sigmoid(-z), 0.0)
        tail = lax.cumsum(log1m, axis=3, reverse=True) - log1m
        a = jnp.where(mask, jnp.exp(jax.nn.log_sigmoid(z) + tail), 0.0)
        outs.append(jnp.einsum("bhts,bhsd->bhtd", a.astype(v.dtype), vb))
    return jnp.concatenate(outs, axis=2)


def peer_layer(h, w_q, k1, k2, u_tab, v_tab):
    B, T, D = h.shape
    q = jnp.einsum("btd,dk->btk", h, w_q).reshape(B, T, PEER_HEADS, PEER_DK)
    q1, q2 = q[..., :PEER_HALF], q[..., PEER_HALF:]
    s1 = jnp.einsum("bthd,nd->bthn", q1, k1)
    s2 = jnp.einsum("bthd,nd->bthn", q2, k2)
    v1, i1 = lax.top_k(s1, PEER_TOPK)
    v2, i2 = lax.top_k(s2, PEER_TOPK)
    cand = (v1[..., :, None] + v2[..., None, :]).reshape(B, T, PEER_HEADS, PEER_TOPK * PEER_TOPK)
    cidx = (i1[..., :, None] * N_KEYS + i2[..., None, :]).reshape(B, T, PEER_HEADS, PEER_TOPK * PEER_TOPK)
    sc, pos = lax.top_k(cand, PEER_TOPK)
    eidx = jnp.take_along_axis(cidx, pos, axis=-1)
    g = jax.nn.softmax(sc.astype(jnp.float32), axis=-1).astype(h.dtype)
    n_blk = (B * T) // PEER_TOK_BLOCK
    HK = PEER_HEADS * PEER_TOPK
    h_blk = h.reshape(n_blk, PEER_TOK_BLOCK, D)
    idx_blk = eidx.reshape(n_blk, PEER_TOK_BLOCK, HK)
    g_blk = g.reshape(n_blk, PEER_TOK_BLOCK, HK)

    def eval_block(args):
        hb, ib, gb = args
        ue = jnp.take(u_tab, ib, axis=0)
        act = jax.nn.gelu(jnp.einsum("tkd,td->tk", ue, hb))
        ve = jnp.take(v_tab, ib, axis=0)
        return jnp.einsum("tk,tkd->td", gb * act, ve)

    y = lax.map(eval_block, (h_blk, idx_blk, g_blk))
    return y.reshape(B, T, D)


def setup_inputs(seed: int = 0) -> dict:
    key = jax.random.key(seed)
    ks = jax.random.split(key, 20)
    f = jnp.float32
    L = DEPTH
    x = jax.random.normal(ks[0], (BATCH, SEQ, D_MODEL), f)
    c = jax.random.normal(ks[1], (BATCH, D_MODEL), f)
    ada_w = jax.random.normal(ks[2], (L, D_MODEL, 6 * D_MODEL), f) * (0.5 * D_MODEL ** -0.5)
    ada_b = jax.random.normal(ks[3], (L, 6 * D_MODEL), f) * 0.02
    norm1_g = 1.0 + 0.02 * jax.random.normal(ks[4], (L, D_MODEL), f)
    norm2_g = 1.0 + 0.02 * jax.random.normal(ks[5], (L, D_MODEL), f)
    w_in = jax.random.normal(ks[6], (L, D_MODEL, PROJ_COLS), f) * D_MODEL ** -0.5
    sgu_w = jax.random.normal(ks[7], (L, G_A, CHUNK, CHUNK), f) * CHUNK ** -0.5
    sgu_b = 1.0 + 0.02 * jax.random.normal(ks[8], (L, G_A, CHUNK), f)
    out_norm_a = 1.0 + 0.02 * jax.random.normal(ks[9], (L, D_A), f)
    out_norm_b = 1.0 + 0.02 * jax.random.normal(ks[10], (L, D_B), f)
    w_out = jax.random.normal(ks[11], (L, D_MIX, D_MODEL), f) * D_MIX ** -0.5
    peer_wq = jax.random.normal(ks[12], (L, D_MODEL, PEER_HEADS * PEER_DK), f) * D_MODEL ** -0.5
    peer_k1 = jax.random.normal(ks[13], (L, N_KEYS, PEER_HALF), f) * PEER_HALF ** -0.5
    peer_k2 = jax.random.normal(ks[14], (L, N_KEYS, PEER_HALF), f) * PEER_HALF ** -0.5
    peer_u = jax.random.normal(ks[15], (L, N_EXPERTS, D_MODEL), f) * D_MODEL ** -0.5
    peer_v = jax.random.normal(ks[16], (L, N_EXPERTS, D_MODEL), f) * PEER_HEADS ** -0.5
    final_g = 1.0 + 0.02 * jax.random.normal(ks[17], (D_MODEL,), f)
    return {"x": x, "c": c, "ada_w": ada_w, "ada_b": ada_b, "norm1_g": norm1_g,
            "norm2_g": norm2_g, "w_in": w_in, "sgu_w": sgu_w, "sgu_b": sgu_b,
            "out_norm_a": out_norm_a, "out_norm_b": out_norm_b, "w_out": w_out,
            "peer_wq": peer_wq, "peer_k1": peer_k1, "peer_k2": peer_k2,
            "peer_u": peer_u, "peer_v": peer_v, "final_g": final_g}


def reference(x, c, ada_w, ada_b, norm1_g, norm2_g, w_in, sgu_w, sgu_b,
              out_norm_a, out_norm_b, w_out, peer_wq, peer_k1, peer_k2,
              peer_u, peer_v, final_g):
    B, T, D = x.shape
    c_act = jax.nn.silu(c)
    for l in range(DEPTH):
        mod = jnp.einsum("bd,de->be", c_act, ada_w[l]) + ada_b[l]
        sh1, sc1, gt1, sh2, sc2, gt2 = [m[:, None, :] for m in jnp.split(mod, 6, axis=-1)]

        h = rms_norm(x, norm1_g[l]) * (1.0 + sc1) + sh1
        proj = jnp.einsum("btd,dc->btc", h, w_in[l])
        u_a = jax.nn.gelu(proj[..., :D_A])
        v_a = jax.nn.gelu(proj[..., D_A:2 * D_A])
        q_b = proj[..., 2 * D_A:2 * D_A + D_B]
        k_b = proj[..., 2 * D_A + D_B:2 * D_A + 2 * D_B]
        v_b = proj[..., 2 * D_A + 2 * D_B:]
        y_a = chunked_gmlp(u_a, v_a, sgu_w[l], sgu_b[l])
        to_heads = lambda z: z.reshape(B, T, H_B, DH_B).transpose(0, 2, 1, 3)
        y_b = stick_breaking_attention(to_heads(q_b), to_heads(k_b), to_heads(v_b))
        y_b = y_b.transpose(0, 2, 1, 3).reshape(B, T, D_B)
        y_mix = jnp.concatenate([group_rms_norm(y_a, out_norm_a[l], G_A),
                                 group_rms_norm(y_b, out_norm_b[l], H_B)], axis=-1)
        x = x + gt1 * jnp.einsum("btc,cd->btd", y_mix, w_out[l])

        h2 = rms_norm(x, norm2_g[l]) * (1.0 + sc2) + sh2
        x = x + gt2 * peer_layer(h2, peer_wq[l], peer_k1[l], peer_k2[l], peer_u[l], peer_v[l])
    return rms_norm(x, final_g)
```

```python
import numpy as np
from contextlib import ExitStack
import concourse.bass as bass
import concourse.mybir as mybir
from concourse.bass_utils import run_bass_kernel_spmd

F32 = mybir.dt.float32
BF16 = mybir.dt.bfloat16
U32 = mybir.dt.uint32
ALU = mybir.AluOpType
AF = mybir.ActivationFunctionType
AX = mybir.AxisListType

NCORES = 8
T = 2048
D = 1024
L = 2
EPS = 1e-6
_ESZ = {F32: 4, BF16: 2, U32: 4}


def _esz(dt):
    return _ESZ.get(dt, 4)


class _Op:
    __slots__ = ("eng", "fn", "acc", "dma", "deps", "ms", "semval", "dsem", "dval", "prevd")


class Prog:
    ENGS = ("pe", "act", "dve", "pool", "sp")

    def __init__(self, nc, es):
        self.nc = nc
        self.e = {"pe": nc.tensor, "act": nc.scalar, "dve": nc.vector, "pool": nc.gpsimd, "sp": nc.sync}
        self.sem = {k: es.enter_context(nc.semaphore("pg_" + k)) for k in self.ENGS}
        self.cnt = {k: 0 for k in self.ENGS}
        self.NDS = 8
        self.dsems = {q: [es.enter_context(nc.semaphore("dq_%s%d" % (q, i))) for i in range(self.NDS)]
                      for q in ("sp", "act", "pool")}
        self.dcnt = {q: 0 for q in self.dsems}
        self.dlast = {q: [None] * self.NDS for q in self.dsems}
        self.waited = {k: {} for k in self.ENGS}
        self.pending = []
        self.hist = {}
        self.nins = 0

    @staticmethod
    def regions(ap):
        name = ap.tensor.name
        esz = _esz(ap.dtype)
        apl = ap.ap
        off = ap.offset
        sp = str(ap.space)
        if "SB" not in sp and "PSUM" not in sp:
            ext = sum((c - 1) * abs(s) for s, c in apl) + 1
            return [(name, 0, 1, off * esz, (off + ext) * esz)]
        pstep, pcnt = apl[0]
        if pstep > 0:
            p0 = off // pstep
            f0 = off % pstep
        else:
            p0, f0 = 0, off
        p1 = p0 + pcnt
        dims = [(abs(s), c) for s, c in apl[1:] if c > 1]
        if not dims:
            return [(name, p0, p1, f0 * esz, (f0 + 1) * esz)]
        dims.sort(key=lambda sc: -sc[0])
        s0, c0 = dims[0]
        inner = sum((c - 1) * s for s, c in dims[1:]) + 1
        if len(dims) > 1 and s0 >= inner and c0 <= 32:
            return [(name, p0, p1, (f0 + i * s0) * esz, (f0 + i * s0 + inner) * esz) for i in range(c0)]
        ext = sum((c - 1) * s for s, c in dims) + 1
        return [(name, p0, p1, f0 * esz, (f0 + ext) * esz)]

    def op(self, eng, fn, reads=(), writes=(), dma=False):
        o = _Op()
        o.eng = eng
        o.fn = fn
        o.dma = dma
        acc = []
        for ap in reads:
            if ap is None or isinstance(ap, (int, float)):
                continue
            for r in self.regions(ap):
                acc.append((r, False))
        for ap in writes:
            for r in self.regions(ap):
                acc.append((r, True))
        o.acc = acc
        o.deps = []
        o.ms = False
        o.semval = None
        o.dsem = None
        o.dval = None
        o.prevd = None
        self.pending.append(o)
        return o

    def flush(self, barrier=True):
        hist = self.hist
        pend = self.pending
        for o in pend:
            deps = set()
            for (name, p0, p1, lo, hi), w in o.acc:
                lst = hist.get(name)
                if not lst:
                    continue
                for rec in lst:
                    if rec[0] < p1 and p0 < rec[1] and rec[2] < hi and lo < rec[3]:
                        if w or rec[4]:
                            deps.add(rec[5])
            deps.discard(o)
            for d in deps:
                if (not d.dma) and d.eng == o.eng and o.eng == "pe" and not o.dma:
                    continue
                o.deps.append(d)
                if not d.dma:
                    d.ms = True
            if o.dma:
                q = o.eng
                k = self.dcnt[q]
                self.dcnt[q] += 1
                slot = k % self.NDS
                o.dsem = self.dsems[q][slot]
                o.dval = 16 * (k // self.NDS + 1)
                o.prevd = self.dlast[q][slot]
                self.dlast[q][slot] = o
            for (name, p0, p1, lo, hi), w in o.acc:
                lst = hist.setdefault(name, [])
                if w:
                    lst[:] = [r for r in lst if not (p0 <= r[0] and r[1] <= p1 and lo <= r[2] and r[3] <= hi)]
                    lst.append((p0, p1, lo, hi, True, o))
                else:
                    found = False
                    if not o.dma:
                        for i, r in enumerate(lst):
                            if (not r[4]) and r[0] == p0 and r[1] == p1 and r[2] == lo and r[3] == hi \
                                    and (not r[5].dma) and r[5].eng == o.eng:
                                lst[i] = (p0, p1, lo, hi, False, o)
                                found = True
                                break
                    if not found:
                        lst.append((p0, p1, lo, hi, False, o))
        if barrier:
            last = {}
            for o in pend:
                if not o.dma:
                    last[o.eng] = o
            for o in last.values():
                o.ms = True
        else:
            for lst in hist.values():
                for r in lst:
                    if (not r[5].dma) and r[5].semval is None:
                        r[5].ms = True
        for o in pend:
            eng = self.e[o.eng]
            wd = self.waited[o.eng]
            need = {}
            if o.dma and o.prevd is not None:
                need[id(o.prevd.dsem)] = (o.prevd.dsem, o.prevd.dval)
            for d in o.deps:
                if d.dma:
                    s, v = d.dsem, d.dval
                else:
                    s, v = self.sem[d.eng], d.semval
                cur = need.get(id(s))
                if cur is None or cur[1] < v:
                    need[id(s)] = (s, v)
            for sid, (s, v) in need.items():
                if wd.get(sid, 0) < v:
                    eng.wait_ge(s, v)
                    wd[sid] = v
                    self.nins += 1
            ins = o.fn()
            self.nins += 1
            if o.dma:
                ins.then_inc(o.dsem, 16)
            elif o.ms:
                self.cnt[o.eng] += 1
                o.semval = self.cnt[o.eng]
                ins.then_inc(self.sem[o.eng], 1)
            o.fn = None
            o.acc = None
        self.pending = []
        if barrier:
            for k in self.ENGS:
                eng = self.e[k]
                wd = self.waited[k]
                for k2 in self.ENGS:
                    if k2 == k:
                        continue
                    s = self.sem[k2]
                    v = self.cnt[k2]
                    if wd.get(id(s), 0) < v:
                        eng.wait_ge(s, v)
                        wd[id(s)] = v
                        self.nins += 1
                for q in self.dsems:
                    for slot in range(self.NDS):
                        d = self.dlast[q][slot]
                        if d is None:
                            continue
                        if wd.get(id(d.dsem), 0) < d.dval:
                            eng.wait_ge(d.dsem, d.dval)
                            wd[id(d.dsem)] = d.dval
                            self.nins += 1
            self.hist = {}

    def mm(self, out, lhsT, rhs, start=True, stop=True):
        nc = self.nc
        rd = [lhsT, rhs] + ([] if start else [out])
        return self.op("pe", lambda: nc.tensor.matmul(out, lhsT, rhs, start=start, stop=stop), rd, [out])

    def tr(self, out, in_, ident):
        nc = self.nc
        return self.op("pe", lambda: nc.tensor.transpose(out, in_, ident), [in_, ident], [out])

    def act(self, out, in_, func, bias=None, scale=None):
        nc = self.nc
        kw = {}
        if bias is not None:
            kw["bias"] = bias
        if scale is not None:
            kw["scale"] = scale
        rd = [in_] + [a for a in (bias, scale) if a is not None and not isinstance(a, (int, float))]
        return self.op("act", lambda: nc.scalar.activation(out=out, in_=in_, func=func, **kw), rd, [out])

    def tt(self, eng, out, in0, in1, op):
        e = self.e[eng]
        return self.op(eng, lambda: e.tensor_tensor(out=out, in0=in0, in1=in1, op=op), [in0, in1], [out])

    def ts(self, eng, out, in0, s1, s2, op0, op1=None):
        e = self.e[eng]
        rd = [in0] + [a for a in (s1, s2) if a is not None and not isinstance(a, (int, float))]
        if op1 is None:
            return self.op(eng, lambda: e.tensor_scalar(out=out, in0=in0, scalar1=s1, scalar2=None, op0=op0), rd, [out])
        return self.op(eng, lambda: e.tensor_scalar(out=out, in0=in0, scalar1=s1, scalar2=s2, op0=op0, op1=op1), rd, [out])

    def stt(self, eng, out, in0, scalar, in1, op0, op1):
        e = self.e[eng]
        rd = [in0, in1] + ([] if isinstance(scalar, (int, float)) else [scalar])
        return self.op(eng, lambda: e.scalar_tensor_tensor(out=out, in0=in0, scalar=scalar, in1=in1, op0=op0, op1=op1),
                       rd, [out])

    def copy(self, eng, out, in_):
        if eng == "act":
            return self.act(out, in_, AF.Copy)
        e = self.e[eng]
        return self.op(eng, lambda: e.tensor_copy(out=out, in_=in_), [in_], [out])

    def memset(self, eng, ap, val):
        e = self.e[eng]
        return self.op(eng, lambda: e.memset(ap, val), [], [ap])

    def reduce(self, eng, out, in_, op):
        e = self.e[eng]
        return self.op(eng, lambda: e.tensor_reduce(out=out, in_=in_, axis=AX.X, op=op), [in_], [out])

    def recip(self, out, in_):
        nc = self.nc
        return self.op("dve", lambda: nc.vector.reciprocal(out=out, in_=in_), [in_], [out])

    def vmax(self, out, in_):
        nc = self.nc
        return self.op("dve", lambda: nc.vector.max(out=out, in_=in_), [in_], [out])

    def vmax_index(self, out, in_max, in_values):
        nc = self.nc
        return self.op("dve", lambda: nc.vector.max_index(out=out, in_max=in_max, in_values=in_values),
                       [in_max, in_values], [out])

    def vmatch_replace(self, out, in_to_replace, in_values, imm):
        nc = self.nc
        return self.op("dve", lambda: nc.vector.match_replace(out=out, in_to_replace=in_to_replace,
                                                              in_values=in_values, imm_value=imm),
                       [in_to_replace, in_values], [out])

    def dma(self, q, out, in_):
        e = self.e[q]
        return self.op(q, lambda: e.dma_start(out=out, in_=in_), [in_], [out], dma=True)


C_ID = 0
C_TRI = 128
C_ONE = 256
C_BD = 384
C_SGM = 512
C_IOTA = 640
C_BM = 768
C_EPS = 776
C_AM = 784
NCST = C_AM + 4 * 512


def _consts():
    c = np.zeros((128, NCST), np.float32)
    i = np.arange(128)
    c[:, C_ID:C_ID + 128] = np.eye(128)
    c[:, C_TRI:C_TRI + 128] = (i[:, None] >= i[None, :])
    c[:, C_ONE:C_ONE + 128] = 1.0
    c[:, C_BD:C_BD + 128] = (i[:, None] // 64 == i[None, :] // 64)
    c[:, C_SGM:C_SGM + 128] = (i[:, None] <= i[None, :])
    c[:, C_IOTA:C_IOTA + 128] = i[None, :]
    c[:, C_BM:C_BM + 8] = (i[:, None] // 16 == np.arange(8)[None, :])
    c[:, C_EPS] = EPS
    t = np.arange(512)
    for r in range(4):
        c[:, C_AM + r * 512:C_AM + (r + 1) * 512] = (t[None, :] > r * 128 + i[:, None])
    return c


def build(dbg=(), stages="all", nseq=2):
    nc = bass.Bass("TRN2", target_bir_lowering=False)
    es = ExitStack()
    P = Prog(nc, es)
    dram_in = {}

    def din(name, shape, dt=F32):
        dram_in[name] = nc.dram_tensor(name, list(shape), dt, kind="ExternalInput").ap()
        return dram_in[name]

    _shapes = {"x": [2, T, D], "cT": [128, 8, 2], "ada_w": [L, D, 6 * D], "ada_bT": [128, L, 48],
               "n1gT": [128, L, 8], "n2gT": [128, L, 8], "w_in": [L, D, 2560], "sgu_wT": [128, L, 8, 128],
               "sgu_bB": [128, L, 4, 128], "onaT": [128, L, 4], "onbT": [128, L, 4], "w_out": [L, D, D],
               "peer_wq": [L, D, D], "k1T": [64, L, 128], "k2T": [64, L, 128], "peer_uT": [L, D, 16384],
               "peer_v": [L, 16384, D], "fgT": [128, 8], "consts": [128, NCST]}

    class _DI:
        def __getattr__(self, name):
            if name not in dram_in:
                din(name, _shapes[name])
            return dram_in[name]
    DI = _DI()
    out_d = nc.dram_tensor("out", [2, T, D], F32, kind="ExternalOutput").ap()
    dbg_out = {}

    _uid = [0]

    def sb(name, shape, dt=F32, stack=es):
        _uid[0] += 1
        return stack.enter_context(nc.sbuf_tensor("s%d_%s" % (_uid[0], name), list(shape), dt))

    def tap(name, ap, shape, dt):
        if name in dbg:
            o = nc.dram_tensor("dbg_" + name, list(shape), dt, kind="ExternalOutput").ap()
            dbg_out[name] = o
            P.dma("sp", o, ap)

    ps = es.enter_context(nc.psum_tensor("ps", [128, 8, 512], F32))
    bank_i = [0]

    def nb():
        b = bank_i[0]
        bank_i[0] = (b + 1) % 8
        return b

    cst = sb("cst", [128, NCST])
    cb = sb("cb", [128, 784], BF16)
    xT = sb("xT", [128, 8, T])
    hT = sb("hT", [128, 8, T], BF16)
    modT = sb("modT", [128, L, 6, 8, 2])
    gsc = sb("gsc", [128, L, 2, 2, 8])
    small = sb("small", [128, 64])
    P.dma("sp", cst[:], DI.consts[:, :])
    ident = cst[:, C_ID:C_ID + 128]
    ones_f = cst[:, C_ONE:C_ONE + 128]
    eps_c = cst[:, C_EPS:C_EPS + 1]
    P.copy("dve", cb[:, 0:784], cst[:, 0:784])
    ident_b = cb[:, 0:128]
    tri_b = cb[:, 128:256]
    ones_b = cb[:, 256:384]
    bd_b = cb[:, 384:512]
    iota_b = cb[:, C_IOTA:C_IOTA + 128]
    bm_b = cb[:, C_BM:C_BM + 8]

    with ExitStack() as st:
        cT = sb("cT", [128, 8, 2], F32, st)
        cact = sb("cact", [128, 8, 2], F32, st)
        adab = sb("adab", [128, L, 48], F32, st)
        n1g = sb("n1g", [128, L, 8], F32, st)
        n2g = sb("n2g", [128, L, 8], F32, st)
        wada = [sb("wada%d" % i, [128, 8, 1024], F32, st) for i in range(2)]
        P.dma("sp", cT[:], DI.cT[:, :, :])
        P.dma("sp", adab[:], DI.ada_bT[:, :, :])
        P.dma("sp", n1g[:], DI.n1gT[:, :, :])
        P.dma("sp", n2g[:], DI.n2gT[:, :, :])
        P.act(cact[:], cT[:], AF.Silu)
        it = 0
        for l in range(L):
            for js in range(6):
                w = wada[it % 2]
                it += 1
                for k in range(8):
                    P.dma("sp", w[:, k, :], DI.ada_w[l, k * 128:(k + 1) * 128, js * 1024:(js + 1) * 1024])
                for ec in range(8):
                    b = nb()
                    for k in range(8):
                        P.mm(ps[:, b, 0:2], w[:, k, ec * 128:(ec + 1) * 128], cact[:, k, :], start=(k == 0), stop=(k == 7))
                    P.ts("dve", modT[:, l, js, ec, :], ps[:, b, 0:2], adab[:, l, js * 8 + ec:js * 8 + ec + 1], None, ALU.add)
        for l in range(L):
            for s in range(2):
                for w, (j, g) in enumerate(((1, n1g), (4, n2g))):
                    P.ts("dve", small[:, 0:8], modT[:, l, j, :, s], 1.0, None, ALU.add)
                    P.tt("dve", gsc[:, l, w, s, :], small[:, 0:8], g[:, l, :], ALU.mult)
        tap("modT", modT[:], [128, L, 6, 8, 2], F32)
        P.flush()

    def load_x(s):
        with ExitStack() as st:
            xin = [sb("xin%d" % i, [128, D], F32, st) for i in range(2)]
            for tb in range(16):
                xi = xin[tb % 2]
                P.dma("sp", xi[:], DI.x[s, tb * 128:(tb + 1) * 128, :])
                for kh in range(2):
                    b = nb()
                    for kk in range(4):
                        k = kh * 4 + kk
                        P.tr(ps[:, b, kk * 128:(kk + 1) * 128], xi[:, k * 128:(k + 1) * 128], ident)
                    P.copy("act" if kh == 0 else "dve", xT[:, kh * 4:(kh + 1) * 4, tb * 128:(tb + 1) * 128],
                           ps[:, b, :].rearrange("p (a b) -> p a b", a=4))
            P.flush()

    def rms_bcast(st, tq, tagsrc):
        sq = [sb("sq%d" % i, [128, 512], F32, st) for i in range(2)]
        rs = sb("rs", [128, 512], F32, st)
        return sq, rs

    def norm_mod(l, s, w):
        jsh = 0 if w == 0 else 3
        with ExitStack() as st:
            sq = [sb("sq%d" % i, [128, 512], F32, st) for i in range(2)]
            rs = [sb("rs%d" % i, [128, 512], F32, st) for i in range(2)]
            tmp = [sb("nt%d" % i, [128, 512], F32, st) for i in range(2)]
            for tq in range(4):
                tsl = slice(tq * 512, (tq + 1) * 512)
                b = nb()
                for k in range(8):
                    q = sq[k % 2]
                    P.tt("pool", q[:], xT[:, k, tsl], xT[:, k, tsl], ALU.mult)
                    P.mm(ps[:, b, :], ones_f, q[:], start=(k == 0), stop=(k == 7))
                r = rs[tq % 2]
                P.act(r[:], ps[:, b, :], AF.Sqrt, bias=eps_c, scale=1.0 / D)
                P.recip(r[:], r[:])
                for k in range(8):
                    tm = tmp[k % 2]
                    P.tt("dve", tm[:], xT[:, k, tsl], r[:], ALU.mult)
                    P.act(hT[:, k, tsl], tm[:], AF.Identity, bias=modT[:, l, jsh, k, s:s + 1],
                          scale=gsc[:, l, w, s, k:k + 1])
            P.flush()

    def bcast(ap, dims):
        return bass.AP(ap.tensor, ap.offset, [list(ap.ap[0])] + [list(d) for d in dims])

    one_c = cst[:, C_ONE:C_ONE + 1]

    def load_wslice(dst, src2d, c0, ncols):
        for k in range(8):
            P.dma("pool", dst[:, k, :], src2d[k * 128:(k + 1) * 128, c0:c0 + ncols])

    def proj_fm(w, dstfn, evac):
        for cc in range(4):
            for tq in range(4):
                tsl = slice(tq * 512, (tq + 1) * 512)
                b = nb()
                for k in range(8):
                    P.mm(ps[:, b, :], w[:, k, cc * 128:(cc + 1) * 128], hT[:, k, tsl], start=(k == 0), stop=(k == 7))
                evac(dstfn(cc, tsl), ps[:, b, :], cc * 4 + tq)

    def group_norm(yT, gain, l):
        with ExitStack() as st:
            sqb = [sb("gq%d" % i, [128, 512], BF16, st) for i in range(2)]
            rs = [sb("gr%d" % i, [128, 512], F32, st) for i in range(2)]
            tf = [sb("gt%d" % i, [128, 512], F32, st) for i in range(2)]
            i = 0
            for cc in range(4):
                for tq in range(4):
                    tsl = slice(tq * 512, (tq + 1) * 512)
                    q, r, t_ = sqb[i % 2], rs[i % 2], tf[i % 2]
                    i += 1
                    P.tt("pool", q[:], yT[:, cc, tsl], yT[:, cc, tsl], ALU.mult)
                    b = nb()
                    P.mm(ps[:, b, :], bd_b, q[:])
                    P.act(r[:], ps[:, b, :], AF.Sqrt, bias=eps_c, scale=1.0 / 64)
                    P.recip(r[:], r[:])
                    P.tt("dve", t_[:], yT[:, cc, tsl], r[:], ALU.mult)
                    P.act(yT[:, cc, tsl], t_[:], AF.Identity, scale=gain[:, l, cc:cc + 1])
            P.flush()

    def gmlp(l, s, yaT):
        with ExitStack() as st:
            wu = sb("wu", [128, 8, 512], BF16, st)
            wv = sb("wv", [128, 8, 512], BF16, st)
            sw = sb("sw", [128, 8, 128], F32, st)
            swb = sb("swb", [128, 8, 128], BF16, st)
            bsB = sb("bsB", [128, 4, 128], F32, st)
            st8 = sb("st8", [128, 32], F32, st)
            vv = [sb("vv%d" % i, [128, 512], F32, st) for i in range(2)]
            cen = sb("cen", [128, 512], F32, st)
            sqv = sb("sqv", [128, 512], F32, st)
            vn = [sb("vn%d" % i, [128, 512], BF16, st) for i in range(2)]
            tmp = [sb("gtmp%d" % i, [128, 128], F32, st) for i in range(2)]
            load_wslice(wu, DI.w_in[l], 0, 512)
            load_wslice(wv, DI.w_in[l], 512, 512)
            P.dma("sp", sw[:], DI.sgu_wT[:, l, :, :])
            P.dma("sp", bsB[:], DI.sgu_bB[:, l, :, :])
            for g in range(8):
                P.tt("dve", swb[:, g, :], sw[:, g, :], cst[:, C_SGM:C_SGM + 128], ALU.mult)
            proj_fm(wu, lambda cc, tsl: yaT[:, cc, tsl], lambda o, p_, i: P.act(o, p_, AF.Gelu_apprx_tanh))
            it = 0
            for n in range(16):
                nsl = slice(n * 128, (n + 1) * 128)
                b = nb()
                for k in range(8):
                    P.mm(ps[:, b, :], hT[:, k, nsl], wv[:, k, :], start=(k == 0), stop=(k == 7))
                v = vv[n % 2]
                P.act(v[:], ps[:, b, :], AF.Gelu_apprx_tanh)
                v3 = v[:].rearrange("p (g d) -> p g d", g=8)
                cen3 = cen[:].rearrange("p (g d) -> p g d", g=8)
                sq3 = sqv[:].rearrange("p (g d) -> p g d", g=8)
                P.reduce("dve", st8[:, 0:8], v3, ALU.add)
                P.ts("dve", st8[:, 8:16], st8[:, 0:8], -1.0 / 64, None, ALU.mult)
                P.tt("dve", cen3, v3, bcast(st8[:, 8:16], [[1, 8], [0, 64]]), ALU.add)
                P.tt("pool", sqv[:], cen[:], cen[:], ALU.mult)
                P.reduce("dve", st8[:, 16:24], sq3, ALU.add)
                P.act(st8[:, 24:32], st8[:, 16:24], AF.Sqrt, bias=eps_c, scale=1.0 / 64)
                P.recip(st8[:, 24:32], st8[:, 24:32])
                vb_ = vn[n % 2]
                P.tt("dve", vb_[:].rearrange("p (g d) -> p g d", g=8), cen3, bcast(st8[:, 24:32], [[1, 8], [0, 64]]), ALU.mult)
                for cc in range(4):
                    for half in range(2):
                        g = 2 * cc + half
                        rows = slice(half * 64, half * 64 + 64)
                        b2 = nb()
                        P.mm(ps[:, b2, 0:128], vb_[:, cc * 128:(cc + 1) * 128], swb[:, g, :])
                        tm = tmp[it % 2]
                        it += 1
                        P.tt("dve", tm[rows, :], ps[rows, b2, 0:128], bsB[rows, cc, :], ALU.add)
                        P.tt("pool", yaT[rows, cc, nsl], tm[rows, :], yaT[rows, cc, nsl], ALU.mult)
            P.flush()

    def attention(l, s, qT):
        with ExitStack() as st:
            kTn = sb("kTn", [128, 4, T], BF16, st)
            vtok = sb("vtok", [128, 16, 512], BF16, st)
            with ExitStack() as st2:
                ws = [sb("wqk%d" % i, [128, 8, 512], BF16, st2) for i in range(2)]
                load_wslice(ws[0], DI.w_in[l], 1024, 512)
                load_wslice(ws[1], DI.w_in[l], 1536, 512)
                proj_fm(ws[0], lambda cc, tsl: qT[:, cc, tsl],
                        lambda o, p_, i: P.copy("act" if i % 2 else "dve", o, p_))
                proj_fm(ws[1], lambda cc, tsl: kTn[:, cc, tsl],
                        lambda o, p_, i: (P.act(o, p_, AF.Copy, scale=-0.125) if i % 2 else
                                          P.ts("dve", o, p_, -0.125, None, ALU.mult)))
                P.flush(barrier=False)
                load_wslice(ws[0], DI.w_in[l], 2048, 512)
                for n in range(16):
                    b = nb()
                    for k in range(8):
                        P.mm(ps[:, b, :], hT[:, k, n * 128:(n + 1) * 128], ws[0][:, k, :], start=(k == 0), stop=(k == 7))
                    P.copy("act" if n % 2 else "dve", vtok[:, n, :], ps[:, b, :])
                P.flush()
            tap("qT_%d_%d" % (l, s), qT[:], [128, 4, T], BF16)
            tap("kTn_%d_%d" % (l, s), kTn[:], [128, 4, T], BF16)
            tap("vtok_%d_%d" % (l, s), vtok[:], [128, 16, 512], BF16)
            with ExitStack() as st2:
                E = [sb("aE%d" % i, [128, 512], F32, st2) for i in range(2)]
                Lf = sb("aLf", [128, 512], F32, st2)
                Lb = [sb("aLb%d" % i, [128, 512], BF16, st2) for i in range(2)]
                SL = sb("aSL", [128, 512], F32, st2)
                SLb = [sb("aSLb%d" % i, [128, 512], BF16, st2) for i in range(2)]
                af = sb("aaf", [128, 512], F32, st2)
                aa = [sb("aa%d" % i, [128, 512], BF16, st2) for i in range(2)]
                it = 0
                io = 0
                for h in range(8):
                    cc = h // 2
                    rows = slice((h % 2) * 64, (h % 2) * 64 + 64)
                    for qb in range(4):
                        tsl = slice(qb * 512, (qb + 1) * 512)
                        nkb = 4 * (qb + 1)
                        bo = 6 + (io % 2)
                        io += 1
                        first = True
                        for kb in reversed(range(nkb)):
                            r = kb - 4 * qb
                            ksl = slice(kb * 128, (kb + 1) * 128)
                            bz = it % 3
                            bc_ = 3 + it % 3
                            e_, lb_, a_, slb_ = E[it % 2], Lb[it % 2], aa[it % 2], SLb[it % 2]
                            slb_prev = SLb[(it + 1) % 2]
                            it += 1
                            P.mm(ps[:, bz, :], kTn[rows, cc, ksl], qT[rows, cc, tsl])
                            P.act(e_[:], ps[:, bz, :], AF.Exp, scale=-1.0)
                            if r >= 0:
                                am = cst[:, C_AM + r * 512:C_AM + (r + 1) * 512]
                                P.act(Lf[:], e_[:], AF.Ln, bias=one_c)
                                P.tt("pool", lb_[:], Lf[:], am, ALU.mult)
                            else:
                                P.act(lb_[:], e_[:], AF.Ln, bias=one_c)
                            P.mm(ps[:, bc_, :], tri_b, lb_[:], start=True, stop=False)
                            if not first:
                                P.mm(ps[:, bc_, :], ones_b, slb_prev[:], start=False, stop=False)
                            P.mm(ps[:, bc_, :], kTn[rows, cc, ksl], qT[rows, cc, tsl], start=False, stop=True)
                            if r >= 0:
                                P.act(af[:], ps[:, bc_, :], AF.Exp, scale=-1.0)
                                P.tt("pool", a_[:], af[:], am, ALU.mult)
                            else:
                                P.act(a_[:], ps[:, bc_, :], AF.Exp, scale=-1.0)
                            if kb > 0:
                                if first:
                                    P.copy("dve", SL[:], lb_[:])
                                    P.copy("pool", slb_[:], lb_[:])
                                else:
                                    P.tt("dve", SL[:], SL[:], lb_[:], ALU.add)
                                    P.copy("pool", slb_[:], SL[:])
                            P.mm(ps[:, bo, :], vtok[:, kb, cc * 128:(cc + 1) * 128], a_[:], start=first, stop=(kb == 0))
                            first = False
                        P.copy("dve", qT[rows, cc, tsl], ps[rows, bo, :])
                    P.flush(barrier=False)
                P.flush()

    def out_proj(l, s, yaT, ybT):
        with ExitStack() as st:
            wo = sb("wo", [128, 8, D], BF16, st)
            load_wslice(wo, DI.w_out[l], 0, D)
            for dc in range(8):
                for tq in range(4):
                    tsl = slice(tq * 512, (tq + 1) * 512)
                    b = nb()
                    for c8 in range(8):
                        src = yaT[:, c8, tsl] if c8 < 4 else ybT[:, c8 - 4, tsl]
                        P.mm(ps[:, b, :], wo[:, c8, dc * 128:(dc + 1) * 128], src, start=(c8 == 0), stop=(c8 == 7))
                    P.stt("dve", xT[:, dc, tsl], ps[:, b, :], modT[:, l, 2, dc, s:s + 1], xT[:, dc, tsl], ALU.mult, ALU.add)
            P.flush()

    Wd = nc.dram_tensor("Wd_scratch", [32, 128, 128, 64], BF16, kind="Internal").ap()
    GA = 4
    NEG = 32

    def peer_wbuild(l, s):
        with ExitStack() as st:
            wq = sb("wq", [128, 8, D], BF16, st)
            kbd = sb("kbd", [128, 256], BF16, st)
            qp = [sb("qp%d" % i, [128, 8, 128], BF16, st) for i in range(2)]
            S = sb("pS", [128, 4, 256], F32, st)
            V12 = sb("pV12", [128, 8, 2, 16], F32, st)
            I12 = sb("pI12", [128, 8, 2, 16], U32, st)
            I12f = sb("pI12f", [128, 2, 8, 16], F32, st)
            wk = sb("pwk", [128, 256], F32, st)
            cand = sb("pcand", [128, 4, 256], F32, st)
            ce = sb("pce", [128, 4, 256], F32, st)
            c8 = sb("pc8", [128, 4, 16], F32, st)
            zs = sb("pzs", [128, 8], F32, st)
            Gt = sb("pG", [128, 16, 8, 16], F32, st)
            iT = sb("piT", [128, 2, 128], BF16, st)
            gJ = sb("pgJ", [128, 128, 16], BF16, st)
            NSUB = 16
            A = sb("pA", [128, NSUB, 128], BF16, st)
            P2 = sb("pP2", [128, NSUB, 128], BF16, st)
            Gbd = sb("pGbd", [128, NSUB, 8, 16], BF16, st)
            Bt = [sb("pBt%d" % i, [128, 4, 128], BF16, st) for i in range(2)]
            Wst = sb("pWst", [128, 128, 64], BF16, st)
            load_wslice(wq, DI.peer_wq[l], 0, D)
            P.memset("dve", kbd[:], 0.0)
            P.dma("pool", kbd[0:64, 0:128], DI.k1T[:, l, :])
            P.dma("pool", kbd[64:128, 128:256], DI.k2T[:, l, :])
            ev = 0
            for tb in range(16):
                tsl = slice(tb * 128, (tb + 1) * 128)
                q_ = qp[tb % 2]
                for h in range(8):
                    b = nb()
                    for k in range(8):
                        P.mm(ps[:, b, 0:128], wq[:, k, h * 128:(h + 1) * 128], hT[:, k, tsl], start=(k == 0), stop=(k == 7))
                    P.copy("act" if h % 2 else "dve", q_[:, h, :], ps[:, b, 0:128])
                for hh in range(2):
                    for h2 in range(2):
                        b = nb()
                        for h1 in range(2):
                            hl = h2 * 2 + h1
                            P.mm(ps[:, b, h1 * 256:(h1 + 1) * 256], q_[:, 4 * hh + hl, :], kbd[:])
                        P.copy("act", S[:, h2 * 2:h2 * 2 + 2, :], ps[:, b, :].rearrange("p (a c) -> p a c", a=2))
                    if tb == 0 and hh == 0:
                        tap("S_%d_%d" % (l, s), S[:], [128, 4, 256], F32)
                    for hl in range(4):
                        h = 4 * hh + hl
                        for hf in range(2):
                            src = S[:, hl, hf * 128:(hf + 1) * 128]
                            P.vmax(V12[:, h, hf, 0:8], src)
                            P.vmax_index(I12[:, h, hf, 0:8], V12[:, h, hf, 0:8], src)
                            P.vmatch_replace(wk[:, 0:128], V12[:, h, hf, 0:8], src, -1e30)
                            P.vmax(V12[:, h, hf, 8:16], wk[:, 0:128])
                            P.vmax_index(I12[:, h, hf, 8:16], V12[:, h, hf, 8:16], wk[:, 0:128])
                    v1 = V12[:, 4 * hh:4 * hh + 4, 0, :]
                    v2 = V12[:, 4 * hh:4 * hh + 4, 1, :]
                    cand4 = cand[:].rearrange("p h (i j) -> p h i j", i=16)
                    P.tt("dve", cand4, bcast(v1, [[32, 4], [1, 16], [0, 16]]), bcast(v2, [[32, 4], [0, 16], [1, 16]]), ALU.add)
                    for hl in range(4):
                        P.vmax(c8[:, hl, 0:8], cand[:, hl, :])
                        P.vmatch_replace(wk[:], c8[:, hl, 0:8], cand[:, hl, :], -1e30)
                        P.vmax(c8[:, hl, 8:16], wk[:])
                    P.tt("dve", ce[:], cand[:], bcast(c8[:, :, 0:1], [[16, 4], [0, 256]]), ALU.subtract)
                    P.act(ce[:], ce[:], AF.Exp)
                    P.tt("dve", S[:], cand[:], bcast(c8[:, :, 15:16], [[16, 4], [0, 256]]), ALU.is_ge)
                    P.tt("pool", ce[:], ce[:], S[:], ALU.mult)
                    P.reduce("dve", zs[:, 0:4], ce[:], ALU.add)
                    P.recip(zs[:, 4:8], zs[:, 0:4])
                    P.tt("dve", bcast(Gt[:, 0, 4 * hh:4 * hh + 4, :], [[16, 4], [128, 16], [1, 16]]),
                         ce[:].rearrange("p h (i j) -> p h i j", i=16),
                         bcast(zs[:, 4:8], [[1, 4], [0, 16], [0, 16]]), ALU.mult)
                if tb == 0:
                    tap("G_%d_%d" % (l, s), Gt[:], [128, 16, 8, 16], F32)
                    tap("I12_%d_%d" % (l, s), I12[:], [128, 8, 2, 16], U32)
                for hf in range(2):
                    P.copy("dve", I12f[:, hf, :, :], I12[:, :, hf, :])
                    b = nb()
                    P.tr(ps[:, b, 0:128], I12f[:, hf, :, :].rearrange("p h i -> p (h i)"), ident)
                    P.copy("act", iT[:, hf, :], ps[:, b, 0:128])
                for i4 in range(4):
                    b = nb()
                    for ii in range(4):
                        i = i4 * 4 + ii
                        P.tr(ps[:, b, ii * 128:(ii + 1) * 128], Gt[:, i, :, :].rearrange("p h j -> p (h j)"), ident)
                    P.copy("dve" if i4 % 2 else "act", bcast(gJ[:, 0, i4 * 4:i4 * 4 + 4], [[1, 4], [16, 128]]),
                           ps[:, b, :].rearrange("p (i t) -> p i t", i=4))
                for half in range(2):
                    for sub in range(64 // NSUB):
                        t0 = half * 64 + sub * NSUB
                        P.tt("dve", A[:], bcast(iota_b, [[0, NSUB], [1, 128]]),
                             bcast(iT[:, 0, t0:t0 + NSUB], [[1, NSUB], [0, 128]]), ALU.is_equal)
                        P.tt("dve", P2[:], bcast(iota_b, [[0, NSUB], [1, 128]]),
                             bcast(iT[:, 1, t0:t0 + NSUB], [[1, NSUB], [0, 128]]), ALU.is_equal)
                        P.tt("pool", Gbd[:], bcast(gJ[:, t0, :], [[16, NSUB], [0, 8], [1, 16]]),
                             bcast(bm_b, [[0, NSUB], [1, 8], [0, 16]]), ALU.mult)
                        for t4 in range(NSUB // 4):
                            bB = nb()
                            for tt_ in range(4):
                                tl = t4 * 4 + tt_
                                P.mm(ps[:, bB, tt_ * 128:(tt_ + 1) * 128],
                                     Gbd[:, tl, :, :].rearrange("p h i -> p (h i)"), P2[:, tl, :])
                            bt = Bt[ev % 2]
                            P.copy("act" if ev % 2 else "dve", bt[:].rearrange("p a b -> p (a b)"), ps[:, bB, :])
                            bW = nb()
                            for tt_ in range(4):
                                tl = t4 * 4 + tt_
                                P.mm(ps[:, bW, tt_ * 128:(tt_ + 1) * 128], bt[:, tt_, :], A[:, tl, :])
                            tw = sub * NSUB + t4 * 4
                            P.copy("dve" if ev % 2 else "act", bcast(Wst[:, 0, tw:tw + 4], [[1, 4], [64, 128]]),
                                   ps[:, bW, :].rearrange("p (t a) -> p t a", t=4))
                            ev += 1
                    P.dma("sp", Wd[tb * 2 + half], Wst[:])
                    P.flush(barrier=False)
            P.flush()

    def peer_dense(l, s):
        with ExitStack() as st:
            UTs = [sb("dU%d" % i, [128, 8, GA * 128], BF16, st) for i in range(2)]
            Vs = [sb("dV%d" % i, [128, GA, D], BF16, st) for i in range(2)]
            Wt = [sb("dW%d" % i, [128, 8, GA, 64], BF16, st) for i in range(2)]
            actT = [sb("dA%d" % i, [128, 512], BF16, st) for i in range(2)]
            Zt = [sb("dZ%d" % i, [128, GA, 512], BF16, st) for i in range(2)]

            def load_tables(eg):
                a0 = eg * GA
                for k in range(8):
                    P.dma("pool", UTs[eg % 2][:, k, :], DI.peer_uT[l, k * 128:(k + 1) * 128, a0 * 128:(a0 + GA) * 128])
                for ga in range(GA):
                    P.dma("pool", Vs[eg % 2][:, ga, :], DI.peer_v[l, (a0 + ga) * 128:(a0 + ga + 1) * 128, :])

            steps = [(eg, tq) for eg in range(NEG) for tq in range(4)]
            ai = [0]
            yi = [0]

            def stageA(i):
                eg, tq = steps[i]
                tsl = slice(tq * 512, (tq + 1) * 512)
                a0 = eg * GA
                if tq == 1 and eg + 1 < NEG:
                    load_tables(eg + 1)
                w_ = Wt[i % 2]
                P.dma("sp", w_[:], Wd[tq * 8:(tq + 1) * 8, :, a0:a0 + GA, :].rearrange("k b a t -> b k a t"))
                for ga in range(GA):
                    b = ai[0] % 2
                    a_ = actT[ai[0] % 2]
                    ai[0] += 1
                    for k in range(8):
                        P.mm(ps[:, b, :], UTs[eg % 2][:, k, ga * 128:(ga + 1) * 128], hT[:, k, tsl], start=(k == 0), stop=(k == 7))
                    P.act(a_[:], ps[:, b, :], AF.Gelu_apprx_tanh)
                    P.tt("dve", Zt[i % 2][:, ga, :].rearrange("p (k t) -> p k t", k=8),
                         a_[:].rearrange("p (k t) -> p k t", k=8), w_[:, :, ga, :], ALU.mult)

            def stageY(i):
                eg, tq = steps[i]
                tsl = slice(tq * 512, (tq + 1) * 512)
                for dc in range(8):
                    b = 2 + yi[0] % 6
                    yi[0] += 1
                    for ga in range(GA):
                        P.mm(ps[:, b, :], Vs[eg % 2][:, ga, dc * 128:(dc + 1) * 128], Zt[i % 2][:, ga, :],
                             start=(ga == 0), stop=(ga == GA - 1))
                    P.stt("dve", xT[:, dc, tsl], ps[:, b, :], modT[:, l, 5, dc, s:s + 1], xT[:, dc, tsl], ALU.mult, ALU.add)

            load_tables(0)
            stageA(0)
            for i in range(len(steps)):
                if i + 1 < len(steps):
                    stageA(i + 1)
                stageY(i)
                if i % 4 == 3:
                    P.flush(barrier=False)
            P.flush()

    def final_out(s):
        with ExitStack() as st:
            fg = sb("fg", [128, 8], F32, st)
            sq = [sb("fsq%d" % i, [128, 512], F32, st) for i in range(2)]
            rs = [sb("frs%d" % i, [128, 512], F32, st) for i in range(2)]
            tmp = [sb("ftm%d" % i, [128, 512], F32, st) for i in range(2)]
            yk = sb("fyk", [128, 8, 512], F32, st)
            ot = [sb("fot%d" % i, [128, D], F32, st) for i in range(2)]
            P.dma("sp", fg[:], DI.fgT[:, :])
            io = 0
            for tq in range(4):
                tsl = slice(tq * 512, (tq + 1) * 512)
                b = nb()
                for k in range(8):
                    q = sq[k % 2]
                    P.tt("pool", q[:], xT[:, k, tsl], xT[:, k, tsl], ALU.mult)
                    P.mm(ps[:, b, :], ones_f, q[:], start=(k == 0), stop=(k == 7))
                r = rs[tq % 2]
                P.act(r[:], ps[:, b, :], AF.Sqrt, bias=eps_c, scale=1.0 / D)
                P.recip(r[:], r[:])
                for k in range(8):
                    tm = tmp[k % 2]
                    P.tt("dve", tm[:], xT[:, k, tsl], r[:], ALU.mult)
                    P.act(yk[:, k, :], tm[:], AF.Identity, scale=fg[:, k:k + 1])
                for t4 in range(4):
                    o = ot[io % 2]
                    io += 1
                    for kh in range(2):
                        b = nb()
                        for kk in range(4):
                            k = kh * 4 + kk
                            P.tr(ps[:, b, kk * 128:(kk + 1) * 128], yk[:, k, t4 * 128:(t4 + 1) * 128], ident)
                        P.copy("act" if kh else "dve", o[:, kh * 512:(kh + 1) * 512], ps[:, b, :])
                    P.dma("sp", out_d[s, tq * 512 + t4 * 128:tq * 512 + (t4 + 1) * 128, :], o[:])
            P.flush()

    ona = sb("ona", [128, L, 4])
    onb = sb("onb", [128, L, 4])
    P.dma("sp", ona[:], DI.onaT[:, :, :])
    P.dma("sp", onb[:], DI.onbT[:, :, :])

    done = False
    for s in range(nseq):
        load_x(s)
        tap("xT%d" % s, xT[:], [128, 8, T], F32)
        for l in range(L):
            if not stages.startswith("peeronly"):
                norm_mod(l, s, 0)
                tap("h1_%d_%d" % (l, s), hT[:], [128, 8, T], BF16)
                if stages == "norm1":
                    done = True
                    break
                with ExitStack() as stl:
                    yaT = sb("yaT", [128, 4, T], BF16, stl)
                    qT = sb("qT", [128, 4, T], BF16, stl)
                    gmlp(l, s, yaT)
                    tap("ya_%d_%d" % (l, s), yaT[:], [128, 4, T], BF16)
                    if stages == "gmlp":
                        done = True
                        P.flush()
                        break
                    attention(l, s, qT)
                    tap("yb_%d_%d" % (l, s), qT[:], [128, 4, T], BF16)
                    if stages == "attn":
                        done = True
                        P.flush()
                        break
                    group_norm(yaT, ona, l)
                    group_norm(qT, onb, l)
                    tap("yan_%d_%d" % (l, s), yaT[:], [128, 4, T], BF16)
                    tap("ybn_%d_%d" % (l, s), qT[:], [128, 4, T], BF16)
                    out_proj(l, s, yaT, qT)
                    tap("xmix_%d_%d" % (l, s), xT[:], [128, 8, T], F32)
                    P.flush()
                if stages == "mix":
                    done = True
                    break
            norm_mod(l, s, 1)
            tap("h2_%d_%d" % (l, s), hT[:], [128, 8, T], BF16)
            peer_wbuild(l, s)
            if "Wd_%d_%d" % (l, s) in dbg:
                o = nc.dram_tensor("dbg_Wd_%d_%d" % (l, s), [32, 128, 128, 64], BF16, kind="ExternalOutput").ap()
                P.dma("sp", o, Wd)
                P.flush()
            if stages in ("wbuild", "peeronly_wbuild"):
                done = True
                break
            peer_dense(l, s)
            tap("xl_%d_%d" % (l, s), xT[:], [128, 8, T], F32)
            if stages in ("layer0", "peeronly"):
                done = True
                break
        if done:
            break
        final_out(s)

    P.flush()
    return nc, P, dbg_out, list(dram_in.keys())


def host_inputs(inputs, core):
    f = np.float32
    b0 = 2 * core
    m = {}
    m["x"] = np.ascontiguousarray(inputs["x"][b0:b0 + 2])
    c = inputs["c"][b0:b0 + 2]
    m["cT"] = np.ascontiguousarray(c.reshape(2, 8, 128).transpose(2, 1, 0))
    m["ada_w"] = inputs["ada_w"]
    m["ada_bT"] = np.ascontiguousarray(inputs["ada_b"].reshape(L, 48, 128).transpose(2, 0, 1))
    m["n1gT"] = np.ascontiguousarray(inputs["norm1_g"].reshape(L, 8, 128).transpose(2, 0, 1))
    m["n2gT"] = np.ascontiguousarray(inputs["norm2_g"].reshape(L, 8, 128).transpose(2, 0, 1))
    m["w_in"] = inputs["w_in"]
    m["sgu_wT"] = np.ascontiguousarray(inputs["sgu_w"].transpose(3, 0, 1, 2))
    sb_ = inputs["sgu_b"]
    m["sgu_bB"] = np.ascontiguousarray(np.repeat(sb_.reshape(L, 4, 2, 1, 128), 64, axis=3)
                                       .reshape(L, 4, 128, 128).transpose(2, 0, 1, 3))
    m["onaT"] = np.ascontiguousarray(inputs["out_norm_a"].reshape(L, 4, 128).transpose(2, 0, 1))
    m["onbT"] = np.ascontiguousarray(inputs["out_norm_b"].reshape(L, 4, 128).transpose(2, 0, 1))
    m["w_out"] = inputs["w_out"]
    m["peer_wq"] = inputs["peer_wq"]
    m["k1T"] = np.ascontiguousarray(inputs["peer_k1"].transpose(2, 0, 1))
    m["k2T"] = np.ascontiguousarray(inputs["peer_k2"].transpose(2, 0, 1))
    m["peer_uT"] = inputs["_peer_uT"]
    m["peer_v"] = inputs["peer_v"]
    m["fgT"] = np.ascontiguousarray(inputs["final_g"].reshape(8, 128).T)
    m["consts"] = inputs["_consts"]
    return {k: np.asarray(v, dtype=f) for k, v in m.items()}


def kernel(**inputs):
    inputs = {k: np.asarray(v) for k, v in inputs.items()}
    inputs["_peer_uT"] = np.ascontiguousarray(inputs["peer_u"].transpose(0, 2, 1))
    inputs["_consts"] = _consts()
    nc, P, _, used = build()
    in_maps = [{k: v for k, v in host_inputs(inputs, c).items() if k in used} for c in range(NCORES)]
    res = run_bass_kernel_spmd(nc, in_maps, core_ids=list(range(NCORES)))
    out = np.concatenate([r["out"] for r in res.results], axis=0)
    return out.astype(np.float32)
```

```python
import numpy as np
from contextlib import ExitStack
import concourse.bass as bass
import concourse.mybir as mybir
from concourse.bass_utils import run_bass_kernel_spmd

F32 = mybir.dt.float32
BF16 = mybir.dt.bfloat16
U32 = mybir.dt.uint32
ALU = mybir.AluOpType
AF = mybir.ActivationFunctionType
AX = mybir.AxisListType

NCORES = 8
T = 2048
D = 1024
L = 2
EPS = 1e-6
_ESZ = {F32: 4, BF16: 2, U32: 4}


def _esz(dt):
    return _ESZ.get(dt, 4)


class _Op:
    __slots__ = ("eng", "fn", "acc", "dma", "deps", "ms", "semval", "dsem", "dval", "prevd")


class Prog:
    ENGS = ("pe", "act", "dve", "pool", "sp")

    def __init__(self, nc, es):
        self.nc = nc
        self.e = {"pe": nc.tensor, "act": nc.scalar, "dve": nc.vector, "pool": nc.gpsimd, "sp": nc.sync}
        self.sem = {k: es.enter_context(nc.semaphore("pg_" + k)) for k in self.ENGS}
        self.cnt = {k: 0 for k in self.ENGS}
        self.NDS = 8
        self.dsems = {q: [es.enter_context(nc.semaphore("dq_%s%d" % (q, i))) for i in range(self.NDS)]
                      for q in ("sp", "act", "pool")}
        self.dcnt = {q: 0 for q in self.dsems}
        self.dlast = {q: [None] * self.NDS for q in self.dsems}
        self.waited = {k: {} for k in self.ENGS}
        self.pending = []
        self.hist = {}
        self.nins = 0

    @staticmethod
    def regions(ap):
        name = ap.tensor.name
        esz = _esz(ap.dtype)
        apl = ap.ap
        off = ap.offset
        sp = str(ap.space)
        if "SB" not in sp and "PSUM" not in sp:
            ext = sum((c - 1) * abs(s) for s, c in apl) + 1
            return [(name, 0, 1, off * esz, (off + ext) * esz)]
        pstep, pcnt = apl[0]
        if pstep > 0:
            p0 = off // pstep
            f0 = off % pstep
        else:
            p0, f0 = 0, off
        p1 = p0 + pcnt
        dims = [(abs(s), c) for s, c in apl[1:] if c > 1]
        if not dims:
            return [(name, p0, p1, f0 * esz, (f0 + 1) * esz)]
        dims.sort(key=lambda sc: -sc[0])
        s0, c0 = dims[0]
        inner = sum((c - 1) * s for s, c in dims[1:]) + 1
        if len(dims) > 1 and s0 >= inner and c0 <= 32:
            return [(name, p0, p1, (f0 + i * s0) * esz, (f0 + i * s0 + inner) * esz) for i in range(c0)]
        ext = sum((c - 1) * s for s, c in dims) + 1
        return [(name, p0, p1, f0 * esz, (f0 + ext) * esz)]

    def op(self, eng, fn, reads=(), writes=(), dma=False):
        o = _Op()
        o.eng = eng
        o.fn = fn
        o.dma = dma
        acc = []
        for ap in reads:
            if ap is None or isinstance(ap, (int, float)):
                continue
            for r in self.regions(ap):
                acc.append((r, False))
        for ap in writes:
            for r in self.regions(ap):
                acc.append((r, True))
        o.acc = acc
        o.deps = []
        o.ms = False
        o.semval = None
        o.dsem = None
        o.dval = None
        o.prevd = None
        self.pending.append(o)
        return o

    def flush(self, barrier=True):
        hist = self.hist
        pend = self.pending
        for o in pend:
            deps = set()
            for (name, p0, p1, lo, hi), w in o.acc:
                lst = hist.get(name)
                if not lst:
                    continue
                for rec in lst:
                    if rec[0] < p1 and p0 < rec[1] and rec[2] < hi and lo < rec[3]:
                        if w or rec[4]:
                            deps.add(rec[5])
            deps.discard(o)
            for d in deps:
                if (not d.dma) and d.eng == o.eng and o.eng == "pe" and not o.dma:
                    continue
                o.deps.append(d)
                if not d.dma:
                    d.ms = True
            if o.dma:
                q = o.eng
                k = self.dcnt[q]
                self.dcnt[q] += 1
                slot = k % self.NDS
                o.dsem = self.dsems[q][slot]
                o.dval = 16 * (k // self.NDS + 1)
                o.prevd = self.dlast[q][slot]
                self.dlast[q][slot] = o
            for (name, p0, p1, lo, hi), w in o.acc:
                lst = hist.setdefault(name, [])
                if w:
                    lst[:] = [r for r in lst if not (p0 <= r[0] and r[1] <= p1 and lo <= r[2] and r[3] <= hi)]
                    lst.append((p0, p1, lo, hi, True, o))
                else:
                    found = False
                    if not o.dma:
                        for i, r in enumerate(lst):
                            if (not r[4]) and r[0] == p0 and r[1] == p1 and r[2] == lo and r[3] == hi \
                                    and (not r[5].dma) and r[5].eng == o.eng:
                                lst[i] = (p0, p1, lo, hi, False, o)
                                found = True
                                break
                    if not found:
                        lst.append((p0, p1, lo, hi, False, o))
        if barrier:
            last = {}
            for o in pend:
                if not o.dma:
                    last[o.eng] = o
            for o in last.values():
                o.ms = True
        else:
            for lst in hist.values():
                for r in lst:
                    if (not r[5].dma) and r[5].semval is None:
                        r[5].ms = True
        for o in pend:
            eng = self.e[o.eng]
            wd = self.waited[o.eng]
            need = {}
            if o.dma and o.prevd is not None:
                need[id(o.prevd.dsem)] = (o.prevd.dsem, o.prevd.dval)
            for d in o.deps:
                if d.dma:
                    s, v = d.dsem, d.dval
                else:
                    s, v = self.sem[d.eng], d.semval
                cur = need.get(id(s))
                if cur is None or cur[1] < v:
                    need[id(s)] = (s, v)
            for sid, (s, v) in need.items():
                if wd.get(sid, 0) < v:
                    eng.wait_ge(s, v)
                    wd[sid] = v
                    self.nins += 1
            ins = o.fn()
            self.nins += 1
            if o.dma:
                ins.then_inc(o.dsem, 16)
            elif o.ms:
                self.cnt[o.eng] += 1
                o.semval = self.cnt[o.eng]
                ins.then_inc(self.sem[o.eng], 1)
            o.fn = None
            o.acc = None
        self.pending = []
        if barrier:
            for k in self.ENGS:
                eng = self.e[k]
                wd = self.waited[k]
                for k2 in self.ENGS:
                    if k2 == k:
                        continue
                    s = self.sem[k2]
                    v = self.cnt[k2]
                    if wd.get(id(s), 0) < v:
                        eng.wait_ge(s, v)
                        wd[id(s)] = v
                        self.nins += 1
                for q in self.dsems:
                    for slot in range(self.NDS):
                        d = self.dlast[q][slot]
                        if d is None:
                            continue
                        if wd.get(id(d.dsem), 0) < d.dval:
                            eng.wait_ge(d.dsem, d.dval)
                            wd[id(d.dsem)] = d.dval
                            self.nins += 1
            self.hist = {}

    def mm(self, out, lhsT, rhs, start=True, stop=True):
        nc = self.nc
        rd = [lhsT, rhs] + ([] if start else [out])
        return self.op("pe", lambda: nc.tensor.matmul(out, lhsT, rhs, start=start, stop=stop), rd, [out])

    def tr(self, out, in_, ident):
        nc = self.nc
        return self.op("pe", lambda: nc.tensor.transpose(out, in_, ident), [in_, ident], [out])

    def act(self, out, in_, func, bias=None, scale=None):
        nc = self.nc
        kw = {}
        if bias is not None:
            kw["bias"] = bias
        if scale is not None:
            kw["scale"] = scale
        rd = [in_] + [a for a in (bias, scale) if a is not None and not isinstance(a, (int, float))]
        return self.op("act", lambda: nc.scalar.activation(out=out, in_=in_, func=func, **kw), rd, [out])

    def tt(self, eng, out, in0, in1, op):
        e = self.e[eng]
        return self.op(eng, lambda: e.tensor_tensor(out=out, in0=in0, in1=in1, op=op), [in0, in1], [out])

    def ts(self, eng, out, in0, s1, s2, op0, op1=None):
        e = self.e[eng]
        rd = [in0] + [a for a in (s1, s2) if a is not None and not isinstance(a, (int, float))]
        if op1 is None:
            return self.op(eng, lambda: e.tensor_scalar(out=out, in0=in0, scalar1=s1, scalar2=None, op0=op0), rd, [out])
        return self.op(eng, lambda: e.tensor_scalar(out=out, in0=in0, scalar1=s1, scalar2=s2, op0=op0, op1=op1), rd, [out])

    def stt(self, eng, out, in0, scalar, in1, op0, op1):
        e = self.e[eng]
        rd = [in0, in1] + ([] if isinstance(scalar, (int, float)) else [scalar])
        return self.op(eng, lambda: e.scalar_tensor_tensor(out=out, in0=in0, scalar=scalar, in1=in1, op0=op0, op1=op1),
                       rd, [out])

    def copy(self, eng, out, in_):
        if eng == "act":
            return self.act(out, in_, AF.Copy)
        e = self.e[eng]
        return self.op(eng, lambda: e.tensor_copy(out=out, in_=in_), [in_], [out])

    def memset(self, eng, ap, val):
        e = self.e[eng]
        return self.op(eng, lambda: e.memset(ap, val), [], [ap])

    def reduce(self, eng, out, in_, op):
        e = self.e[eng]
        return self.op(eng, lambda: e.tensor_reduce(out=out, in_=in_, axis=AX.X, op=op), [in_], [out])

    def recip(self, out, in_):
        nc = self.nc
        return self.op("dve", lambda: nc.vector.reciprocal(out=out, in_=in_), [in_], [out])

    def vmax(self, out, in_):
        nc = self.nc
        return self.op("dve", lambda: nc.vector.max(out=out, in_=in_), [in_], [out])

    def vmax_index(self, out, in_max, in_values):
        nc = self.nc
        return self.op("dve", lambda: nc.vector.max_index(out=out, in_max=in_max, in_values=in_values),
                       [in_max, in_values], [out])

    def vmatch_replace(self, out, in_to_replace, in_values, imm):
        nc = self.nc
        return self.op("dve", lambda: nc.vector.match_replace(out=out, in_to_replace=in_to_replace,
                                                              in_values=in_values, imm_value=imm),
                       [in_to_replace, in_values], [out])

    def dma(self, q, out, in_):
        e = self.e[q]
        return self.op(q, lambda: e.dma_start(out=out, in_=in_), [in_], [out], dma=True)


C_ID = 0
C_TRI = 128
C_ONE = 256
C_BD = 384
C_SGM = 512
C_IOTA = 640
C_BM = 768
C_EPS = 776
C_AM = 784
NCST = C_AM + 4 * 512


def _consts():
    c = np.zeros((128, NCST), np.float32)
    i = np.arange(128)
    c[:, C_ID:C_ID + 128] = np.eye(128)
    c[:, C_TRI:C_TRI + 128] = (i[:, None] >= i[None, :])
    c[:, C_ONE:C_ONE + 128] = 1.0
    c[:, C_BD:C_BD + 128] = (i[:, None] // 64 == i[None, :] // 64)
    c[:, C_SGM:C_SGM + 128] = (i[:, None] <= i[None, :])
    c[:, C_IOTA:C_IOTA + 128] = i[None, :]
    c[:, C_BM:C_BM + 8] = (i[:, None] // 16 == np.arange(8)[None, :])
    c[:, C_EPS] = EPS
    t = np.arange(512)
    for r in range(4):
        c[:, C_AM + r * 512:C_AM + (r + 1) * 512] = (t[None, :] > r * 128 + i[:, None])
    return c


def build(dbg=(), stages="all", nseq=2):
    nc = bass.Bass("TRN2", target_bir_lowering=False)
    es = ExitStack()
    P = Prog(nc, es)
    dram_in = {}

    def din(name, shape, dt=F32):
        dram_in[name] = nc.dram_tensor(name, list(shape), dt, kind="ExternalInput").ap()
        return dram_in[name]

    _shapes = {"x": [2, T, D], "cT": [128, 8, 2], "ada_w": [L, D, 6 * D], "ada_bT": [128, L, 48],
               "n1gT": [128, L, 8], "n2gT": [128, L, 8], "w_in": [L, D, 2560], "sgu_wT": [128, L, 8, 128],
               "sgu_bB": [128, L, 4, 128], "onaT": [128, L, 4], "onbT": [128, L, 4], "w_out": [L, D, D],
               "peer_wq": [L, D, D], "k1T": [64, L, 128], "k2T": [64, L, 128], "peer_uT": [L, D, 16384],
               "peer_v": [L, 16384, D], "fgT": [128, 8], "consts": [128, NCST]}

    class _DI:
        def __getattr__(self, name):
            if name not in dram_in:
                din(name, _shapes[name])
            return dram_in[name]
    DI = _DI()
    out_d = nc.dram_tensor("out", [2, T, D], F32, kind="ExternalOutput").ap()
    dbg_out = {}

    _uid = [0]

    def sb(name, shape, dt=F32, stack=es):
        _uid[0] += 1
        return stack.enter_context(nc.sbuf_tensor("s%d_%s" % (_uid[0], name), list(shape), dt))

    def tap(name, ap, shape, dt):
        if name in dbg:
            o = nc.dram_tensor("dbg_" + name, list(shape), dt, kind="ExternalOutput").ap()
            dbg_out[name] = o
            P.dma("sp", o, ap)

    ps = es.enter_context(nc.psum_tensor("ps", [128, 8, 512], F32))
    psb = ps[:].bitcast(BF16)
    bank_i = [0]

    def nb():
        b = bank_i[0]
        bank_i[0] = (b + 1) % 8
        return b

    cst = sb("cst", [128, NCST])
    cb = sb("cb", [128, 784], BF16)
    xT = sb("xT", [128, 8, T])
    hT = sb("hT", [128, 8, T], BF16)
    modT = sb("modT", [128, L, 6, 8, 2])
    gsc = sb("gsc", [128, L, 2, 2, 8])
    small = sb("small", [128, 64])
    P.dma("sp", cst[:], DI.consts[:, :])
    ident = cst[:, C_ID:C_ID + 128]
    ones_f = cst[:, C_ONE:C_ONE + 128]
    eps_c = cst[:, C_EPS:C_EPS + 1]
    P.copy("dve", cb[:, 0:784], cst[:, 0:784])
    ident_b = cb[:, 0:128]
    tri_b = cb[:, 128:256]
    ones_b = cb[:, 256:384]
    bd_b = cb[:, 384:512]
    iota_b = cb[:, C_IOTA:C_IOTA + 128]
    bm_b = cb[:, C_BM:C_BM + 8]

    with ExitStack() as st:
        cT = sb("cT", [128, 8, 2], F32, st)
        cact = sb("cact", [128, 8, 2], F32, st)
        adab = sb("adab", [128, L, 48], F32, st)
        n1g = sb("n1g", [128, L, 8], F32, st)
        n2g = sb("n2g", [128, L, 8], F32, st)
        wada = [sb("wada%d" % i, [128, 8, 1024], F32, st) for i in range(2)]
        P.dma("sp", cT[:], DI.cT[:, :, :])
        P.dma("sp", adab[:], DI.ada_bT[:, :, :])
        P.dma("sp", n1g[:], DI.n1gT[:, :, :])
        P.dma("sp", n2g[:], DI.n2gT[:, :, :])
        P.act(cact[:], cT[:], AF.Silu)
        it = 0
        for l in range(L):
            for js in range(6):
                w = wada[it % 2]
                it += 1
                for k in range(8):
                    P.dma("sp", w[:, k, :], DI.ada_w[l, k * 128:(k + 1) * 128, js * 1024:(js + 1) * 1024])
                for ec in range(8):
                    b = nb()
                    for k in range(8):
                        P.mm(ps[:, b, 0:2], w[:, k, ec * 128:(ec + 1) * 128], cact[:, k, :], start=(k == 0), stop=(k == 7))
                    P.ts("dve", modT[:, l, js, ec, :], ps[:, b, 0:2], adab[:, l, js * 8 + ec:js * 8 + ec + 1], None, ALU.add)
        for l in range(L):
            for s in range(2):
                for w, (j, g) in enumerate(((1, n1g), (4, n2g))):
                    P.ts("dve", small[:, 0:8], modT[:, l, j, :, s], 1.0, None, ALU.add)
                    P.tt("dve", gsc[:, l, w, s, :], small[:, 0:8], g[:, l, :], ALU.mult)
        tap("modT", modT[:], [128, L, 6, 8, 2], F32)
        P.flush()

    def load_x(s):
        with ExitStack() as st:
            xin = [sb("xin%d" % i, [128, D], F32, st) for i in range(2)]
            for tb in range(16):
                xi = xin[tb % 2]
                P.dma("sp", xi[:], DI.x[s, tb * 128:(tb + 1) * 128, :])
                for kh in range(2):
                    b = nb()
                    for kk in range(4):
                        k = kh * 4 + kk
                        P.tr(ps[:, b, kk * 128:(kk + 1) * 128], xi[:, k * 128:(k + 1) * 128], ident)
                    P.copy("act" if kh == 0 else "dve", xT[:, kh * 4:(kh + 1) * 4, tb * 128:(tb + 1) * 128],
                           ps[:, b, :].rearrange("p (a b) -> p a b", a=4))
            P.flush()

    def rms_bcast(st, tq, tagsrc):
        sq = [sb("sq%d" % i, [128, 512], F32, st) for i in range(2)]
        rs = sb("rs", [128, 512], F32, st)
        return sq, rs

    def norm_mod(l, s, w):
        jsh = 0 if w == 0 else 3
        with ExitStack() as st:
            sq = [sb("sq%d" % i, [128, 512], F32, st) for i in range(2)]
            rs = [sb("rs%d" % i, [128, 512], F32, st) for i in range(2)]
            tmp = [sb("nt%d" % i, [128, 512], F32, st) for i in range(2)]
            for tq in range(4):
                tsl = slice(tq * 512, (tq + 1) * 512)
                b = nb()
                for k in range(8):
                    q = sq[k % 2]
                    P.tt("pool", q[:], xT[:, k, tsl], xT[:, k, tsl], ALU.mult)
                    P.mm(ps[:, b, :], ones_f, q[:], start=(k == 0), stop=(k == 7))
                r = rs[tq % 2]
                P.act(r[:], ps[:, b, :], AF.Sqrt, bias=eps_c, scale=1.0 / D)
                P.recip(r[:], r[:])
                for k in range(8):
                    tm = tmp[k % 2]
                    P.tt("dve", tm[:], xT[:, k, tsl], r[:], ALU.mult)
                    P.act(hT[:, k, tsl], tm[:], AF.Identity, bias=modT[:, l, jsh, k, s:s + 1],
                          scale=gsc[:, l, w, s, k:k + 1])
            P.flush()

    def bcast(ap, dims):
        return bass.AP(ap.tensor, ap.offset, [list(ap.ap[0])] + [list(d) for d in dims])

    one_c = cst[:, C_ONE:C_ONE + 1]

    def load_wslice(dst, src2d, c0, ncols):
        for k in range(8):
            P.dma("pool", dst[:, k, :], src2d[k * 128:(k + 1) * 128, c0:c0 + ncols])

    def proj_fm(w, dstfn, evac):
        for cc in range(4):
            for tq in range(4):
                tsl = slice(tq * 512, (tq + 1) * 512)
                b = nb()
                for k in range(8):
                    P.mm(ps[:, b, :], w[:, k, cc * 128:(cc + 1) * 128], hT[:, k, tsl], start=(k == 0), stop=(k == 7))
                evac(dstfn(cc, tsl), ps[:, b, :], cc * 4 + tq)

    def group_norm(yT, gain, l):
        with ExitStack() as st:
            sqb = [sb("gq%d" % i, [128, 512], BF16, st) for i in range(2)]
            rs = [sb("gr%d" % i, [128, 512], F32, st) for i in range(2)]
            tf = [sb("gt%d" % i, [128, 512], F32, st) for i in range(2)]
            i = 0
            for cc in range(4):
                for tq in range(4):
                    tsl = slice(tq * 512, (tq + 1) * 512)
                    q, r, t_ = sqb[i % 2], rs[i % 2], tf[i % 2]
                    i += 1
                    P.tt("pool", q[:], yT[:, cc, tsl], yT[:, cc, tsl], ALU.mult)
                    b = nb()
                    P.mm(ps[:, b, :], bd_b, q[:])
                    P.act(r[:], ps[:, b, :], AF.Sqrt, bias=eps_c, scale=1.0 / 64)
                    P.recip(r[:], r[:])
                    P.tt("dve", t_[:], yT[:, cc, tsl], r[:], ALU.mult)
                    P.act(yT[:, cc, tsl], t_[:], AF.Identity, scale=gain[:, l, cc:cc + 1])
            P.flush()

    def gmlp(l, s, yaT):
        with ExitStack() as st:
            wu = sb("wu", [128, 8, 512], BF16, st)
            wv = sb("wv", [128, 8, 512], BF16, st)
            sw = sb("sw", [128, 8, 128], F32, st)
            swb = sb("swb", [128, 8, 128], BF16, st)
            bsB = sb("bsB", [128, 4, 128], F32, st)
            st8 = sb("st8", [128, 32], F32, st)
            vv = [sb("vv%d" % i, [128, 512], F32, st) for i in range(2)]
            cen = sb("cen", [128, 512], F32, st)
            sqv = sb("sqv", [128, 512], F32, st)
            vn = [sb("vn%d" % i, [128, 512], BF16, st) for i in range(2)]
            tmp = [sb("gtmp%d" % i, [128, 128], F32, st) for i in range(2)]
            load_wslice(wu, DI.w_in[l], 0, 512)
            load_wslice(wv, DI.w_in[l], 512, 512)
            P.dma("sp", sw[:], DI.sgu_wT[:, l, :, :])
            P.dma("sp", bsB[:], DI.sgu_bB[:, l, :, :])
            for g in range(8):
                P.tt("dve", swb[:, g, :], sw[:, g, :], cst[:, C_SGM:C_SGM + 128], ALU.mult)
            proj_fm(wu, lambda cc, tsl: yaT[:, cc, tsl], lambda o, p_, i: P.act(o, p_, AF.Gelu_apprx_tanh))
            it = 0
            for n in range(16):
                nsl = slice(n * 128, (n + 1) * 128)
                b = nb()
                for k in range(8):
                    P.mm(ps[:, b, :], hT[:, k, nsl], wv[:, k, :], start=(k == 0), stop=(k == 7))
                v = vv[n % 2]
                P.act(v[:], ps[:, b, :], AF.Gelu_apprx_tanh)
                v3 = v[:].rearrange("p (g d) -> p g d", g=8)
                cen3 = cen[:].rearrange("p (g d) -> p g d", g=8)
                sq3 = sqv[:].rearrange("p (g d) -> p g d", g=8)
                P.reduce("dve", st8[:, 0:8], v3, ALU.add)
                P.ts("dve", st8[:, 8:16], st8[:, 0:8], -1.0 / 64, None, ALU.mult)
                P.tt("dve", cen3, v3, bcast(st8[:, 8:16], [[1, 8], [0, 64]]), ALU.add)
                P.tt("pool", sqv[:], cen[:], cen[:], ALU.mult)
                P.reduce("dve", st8[:, 16:24], sq3, ALU.add)
                P.act(st8[:, 24:32], st8[:, 16:24], AF.Sqrt, bias=eps_c, scale=1.0 / 64)
                P.recip(st8[:, 24:32], st8[:, 24:32])
                vb_ = vn[n % 2]
                P.tt("dve", vb_[:].rearrange("p (g d) -> p g d", g=8), cen3, bcast(st8[:, 24:32], [[1, 8], [0, 64]]), ALU.mult)
                for cc in range(4):
                    for half in range(2):
                        g = 2 * cc + half
                        rows = slice(half * 64, half * 64 + 64)
                        b2 = nb()
                        P.mm(ps[:, b2, 0:128], vb_[:, cc * 128:(cc + 1) * 128], swb[:, g, :])
                        tm = tmp[it % 2]
                        it += 1
                        P.tt("dve", tm[rows, :], ps[rows, b2, 0:128], bsB[rows, cc, :], ALU.add)
                        P.tt("pool", yaT[rows, cc, nsl], tm[rows, :], yaT[rows, cc, nsl], ALU.mult)
            P.flush()

    def attention(l, s, qT):
        with ExitStack() as st:
            kTn = sb("kTn", [128, 4, T], BF16, st)
            vtok = sb("vtok", [128, 16, 512], BF16, st)
            with ExitStack() as st2:
                ws = [sb("wqk%d" % i, [128, 8, 512], BF16, st2) for i in range(2)]
                load_wslice(ws[0], DI.w_in[l], 1024, 512)
                load_wslice(ws[1], DI.w_in[l], 1536, 512)
                proj_fm(ws[0], lambda cc, tsl: qT[:, cc, tsl],
                        lambda o, p_, i: P.copy("act" if i % 2 else "dve", o, p_))
                proj_fm(ws[1], lambda cc, tsl: kTn[:, cc, tsl],
                        lambda o, p_, i: (P.act(o, p_, AF.Copy, scale=-0.125) if i % 2 else
                                          P.ts("dve", o, p_, -0.125, None, ALU.mult)))
                P.flush(barrier=False)
                load_wslice(ws[0], DI.w_in[l], 2048, 512)
                for n in range(16):
                    b = nb()
                    for k in range(8):
                        P.mm(ps[:, b, :], hT[:, k, n * 128:(n + 1) * 128], ws[0][:, k, :], start=(k == 0), stop=(k == 7))
                    P.copy("act" if n % 2 else "dve", vtok[:, n, :], ps[:, b, :])
                P.flush()
            tap("qT_%d_%d" % (l, s), qT[:], [128, 4, T], BF16)
            tap("kTn_%d_%d" % (l, s), kTn[:], [128, 4, T], BF16)
            tap("vtok_%d_%d" % (l, s), vtok[:], [128, 16, 512], BF16)
            with ExitStack() as st2:
                NB_ = 3
                E = [sb("aE%d" % i, [128, 512], F32, st2) for i in range(NB_)]
                Lf = [sb("aLf%d" % i, [128, 512], F32, st2) for i in range(2)]
                Lb = [sb("aLb%d" % i, [128, 512], BF16, st2) for i in range(NB_)]
                SL = sb("aSL", [128, 512], F32, st2)
                SLb = [sb("aSLb%d" % i, [128, 512], BF16, st2) for i in range(NB_)]
                af = [sb("aaf%d" % i, [128, 512], F32, st2) for i in range(2)]
                aa = [sb("aa%d" % i, [128, 512], BF16, st2) for i in range(NB_)]
                blocks = []
                gi = 0
                for h in range(8):
                    for qb in range(4):
                        nkb = 4 * (qb + 1)
                        for kb in reversed(range(nkb)):
                            blocks.append((h, qb, kb, kb == nkb - 1, kb == 0, 6 + gi % 2))
                        gi += 1
                nd = [0]

                def S1(i):
                    h, qb, kb, first, last, bo = blocks[i]
                    cc = h // 2
                    rows = slice((h % 2) * 64, (h % 2) * 64 + 64)
                    tsl = slice(qb * 512, (qb + 1) * 512)
                    ksl = slice(kb * 128, (kb + 1) * 128)
                    r = kb - 4 * qb
                    bz = i % 3
                    e_, lb_ = E[i % NB_], Lb[i % NB_]
                    P.mm(ps[:, bz, :], kTn[rows, cc, ksl], qT[rows, cc, tsl])
                    P.act(e_[:], ps[:, bz, :], AF.Exp, scale=-1.0)
                    if r >= 0:
                        am = cst[:, C_AM + r * 512:C_AM + (r + 1) * 512]
                        lf_ = Lf[nd[0] % 2]
                        nd[0] += 1
                        P.act(lf_[:], e_[:], AF.Ln, bias=one_c)
                        P.tt("pool", lb_[:], lf_[:], am, ALU.mult)
                    else:
                        P.act(lb_[:], e_[:], AF.Ln, bias=one_c)
                    if not last:
                        if first:
                            P.copy("dve", SL[:], lb_[:])
                            P.copy("dve", SLb[i % NB_][:], lb_[:])
                        else:
                            P.tt("dve", SL[:], SL[:], lb_[:], ALU.add)
                            P.copy("dve", SLb[i % NB_][:], SL[:])

                def S2(i):
                    h, qb, kb, first, last, bo = blocks[i]
                    cc = h // 2
                    rows = slice((h % 2) * 64, (h % 2) * 64 + 64)
                    tsl = slice(qb * 512, (qb + 1) * 512)
                    ksl = slice(kb * 128, (kb + 1) * 128)
                    r = kb - 4 * qb
                    bc_ = 3 + i % 3
                    lb_, a_ = Lb[i % NB_], aa[i % NB_]
                    P.mm(ps[:, bc_, :], tri_b, lb_[:], start=True, stop=False)
                    if not first:
                        P.mm(ps[:, bc_, :], ones_b, SLb[(i - 1) % NB_][:], start=False, stop=False)
                    P.mm(ps[:, bc_, :], kTn[rows, cc, ksl], qT[rows, cc, tsl], start=False, stop=True)
                    if r >= 0:
                        am = cst[:, C_AM + r * 512:C_AM + (r + 1) * 512]
                        af_ = af[i % 2]
                        P.act(af_[:], ps[:, bc_, :], AF.Exp, scale=-1.0)
                        P.tt("pool", a_[:], af_[:], am, ALU.mult)
                    else:
                        P.act(a_[:], ps[:, bc_, :], AF.Exp, scale=-1.0)

                def S3(i):
                    h, qb, kb, first, last, bo = blocks[i]
                    cc = h // 2
                    rows = slice((h % 2) * 64, (h % 2) * 64 + 64)
                    tsl = slice(qb * 512, (qb + 1) * 512)
                    P.mm(ps[:, bo, :], vtok[:, kb, cc * 128:(cc + 1) * 128], aa[i % NB_][:], start=first, stop=last)
                    if last:
                        P.copy("dve", qT[rows, cc, tsl], ps[rows, bo, :])

                nblk = len(blocks)
                for n in range(nblk + 2):
                    if n < nblk:
                        S1(n)
                    if 0 <= n - 1 < nblk:
                        S2(n - 1)
                    if 0 <= n - 2 < nblk:
                        S3(n - 2)
                    if n % 16 == 15:
                        P.flush(barrier=False)
                P.flush()

    def out_proj(l, s, yaT, ybT):
        with ExitStack() as st:
            wo = sb("wo", [128, 8, D], BF16, st)
            load_wslice(wo, DI.w_out[l], 0, D)
            for dc in range(8):
                for tq in range(4):
                    tsl = slice(tq * 512, (tq + 1) * 512)
                    b = nb()
                    for c8 in range(8):
                        src = yaT[:, c8, tsl] if c8 < 4 else ybT[:, c8 - 4, tsl]
                        P.mm(ps[:, b, :], wo[:, c8, dc * 128:(dc + 1) * 128], src, start=(c8 == 0), stop=(c8 == 7))
                    P.stt("dve", xT[:, dc, tsl], ps[:, b, :], modT[:, l, 2, dc, s:s + 1], xT[:, dc, tsl], ALU.mult, ALU.add)
            P.flush()

    Wd = nc.dram_tensor("Wd_scratch", [32, 128, 128, 64], BF16, kind="Internal").ap()
    GA = 4
    NEG = 32

    def peer_wbuild(l, s):
        with ExitStack() as st:
            wq = sb("wq", [128, 8, D], BF16, st)
            kbd = sb("kbd", [128, 256], BF16, st)
            qp = sb("qp", [128, 8, 128], BF16, st)
            S = sb("pS", [128, 8, 256], F32, st)
            V12 = sb("pV12", [128, 8, 2, 16], F32, st)
            I12 = sb("pI12", [128, 8, 2, 16], U32, st)
            I12f = sb("pI12f", [128, 2, 8, 16], F32, st)
            wk = sb("pwk", [128, 16, 128], F32, st)
            cand = sb("pcand", [128, 4, 256], F32, st)
            ce = sb("pce", [128, 4, 256], F32, st)
            c8 = sb("pc8", [128, 4, 16], F32, st)
            zs = sb("pzs", [128, 8], F32, st)
            Gt = sb("pG", [128, 16, 8, 16], F32, st)
            iTs = [sb("piT%d" % i, [128, 2, 128], BF16, st) for i in range(2)]
            gJs = [sb("pgJ%d" % i, [128, 128, 16], BF16, st) for i in range(2)]
            NSUB = 8
            iom = sb("piom", [128, 128, NSUB], BF16, st)
            As = [sb("pA%d" % i, [128, 128, NSUB], BF16, st) for i in range(2)]
            P2s = [sb("pP2%d" % i, [128, 128, NSUB], BF16, st) for i in range(2)]
            Gbds = [sb("pGbd%d" % i, [128, NSUB, 8, 16], BF16, st) for i in range(2)]
            Bts = [sb("pBt%d" % i, [128, 4, 128], BF16, st) for i in range(3)]
            Wst = sb("pWst", [128, 128, 64], BF16, st)
            load_wslice(wq, DI.peer_wq[l], 0, D)
            P.memset("dve", kbd[:], 0.0)
            P.dma("pool", kbd[0:64, 0:128], DI.k1T[:, l, :])
            P.dma("pool", kbd[64:128, 128:256], DI.k2T[:, l, :])
            P.copy("dve", iom[:], bcast(iota_b, [[1, 128], [0, NSUB]]))

            def qs(tb):
                tsl = slice(tb * 128, (tb + 1) * 128)
                for h in range(8):
                    b = nb()
                    for k in range(8):
                        P.mm(ps[:, b, 0:128], wq[:, k, h * 128:(h + 1) * 128], hT[:, k, tsl], start=(k == 0), stop=(k == 7))
                    P.copy("act", qp[:, h, :], ps[:, b, 0:128])
                for h2 in range(4):
                    b = nb()
                    for h1 in range(2):
                        P.mm(ps[:, b, h1 * 256:(h1 + 1) * 256], qp[:, 2 * h2 + h1, :], kbd[:])
                    P.copy("act", S[:, h2 * 2:h2 * 2 + 2, :], ps[:, b, :].rearrange("p (a c) -> p a c", a=2))

            def topk(tb):
                iT, gJ = iTs[tb % 2], gJs[tb % 2]
                grp = [(h, hf) for h in range(8) for hf in range(2)]
                for h, hf in grp:
                    P.vmax(V12[:, h, hf, 0:8], S[:, h, hf * 128:(hf + 1) * 128])
                    if hf: yield
                for h, hf in grp:
                    P.vmax_index(I12[:, h, hf, 0:8], V12[:, h, hf, 0:8], S[:, h, hf * 128:(hf + 1) * 128])
                    if hf and h % 2: yield
                for gi, (h, hf) in enumerate(grp):
                    P.vmatch_replace(wk[:, gi, :], V12[:, h, hf, 0:8], S[:, h, hf * 128:(hf + 1) * 128], -1e30)
                    if hf: yield
                for gi, (h, hf) in enumerate(grp):
                    P.vmax(V12[:, h, hf, 8:16], wk[:, gi, :])
                    if hf: yield
                for gi, (h, hf) in enumerate(grp):
                    P.vmax_index(I12[:, h, hf, 8:16], V12[:, h, hf, 8:16], wk[:, gi, :])
                    if hf and h % 2: yield
                for hf in range(2):
                    P.copy("pool", I12f[:, hf, :, :], I12[:, :, hf, :])
                    b = nb()
                    P.tr(ps[:, b, 0:128], I12f[:, hf, :, :].rearrange("p h i -> p (h i)"), ident)
                    P.copy("act", iT[:, hf, :], ps[:, b, 0:128])
                    yield
                wk2 = wk[:].rearrange("p (a c) d -> p a (c d)", c=2)
                msk = wk2[:, 4:8, :]
                for hh in range(2):
                    v1 = V12[:, 4 * hh:4 * hh + 4, 0, :]
                    v2 = V12[:, 4 * hh:4 * hh + 4, 1, :]
                    cand4 = cand[:].rearrange("p h (i j) -> p h i j", i=16)
                    P.tt("dve", cand4, bcast(v1, [[32, 4], [1, 16], [0, 16]]), bcast(v2, [[32, 4], [0, 16], [1, 16]]), ALU.add)
                    yield
                    for hl in range(4):
                        P.vmax(c8[:, hl, 0:8], cand[:, hl, :])
                    yield
                    for hl in range(4):
                        P.vmatch_replace(wk2[:, hl, :], c8[:, hl, 0:8], cand[:, hl, :], -1e30)
                    yield
                    for hl in range(4):
                        P.vmax(c8[:, hl, 8:16], wk2[:, hl, :])
                    yield
                    P.tt("dve", ce[:], cand[:], bcast(c8[:, :, 0:1], [[16, 4], [0, 256]]), ALU.subtract)
                    P.act(ce[:], ce[:], AF.Exp)
                    yield
                    P.tt("dve", msk, cand[:], bcast(c8[:, :, 15:16], [[16, 4], [0, 256]]), ALU.is_ge)
                    P.tt("pool", ce[:], ce[:], msk, ALU.mult)
                    P.reduce("dve", zs[:, 0:4], ce[:], ALU.add)
                    P.recip(zs[:, 4:8], zs[:, 0:4])
                    yield
                    P.tt("pool", bcast(Gt[:, 0, 4 * hh:4 * hh + 4, :], [[16, 4], [128, 16], [1, 16]]),
                         ce[:].rearrange("p h (i j) -> p h i j", i=16),
                         bcast(zs[:, 4:8], [[1, 4], [0, 16], [0, 16]]), ALU.mult)
                for i4 in range(4):
                    b = nb()
                    for ii in range(4):
                        i = i4 * 4 + ii
                        P.tr(ps[:, b, ii * 128:(ii + 1) * 128], Gt[:, i, :, :].rearrange("p h j -> p (h j)"), ident)
                    P.copy("act", bcast(gJ[:, 0, i4 * 4:i4 * 4 + 4], [[1, 4], [16, 128]]),
                           ps[:, b, :].rearrange("p (i t) -> p i t", i=4))
                    yield

            ctr = {"sub": 0, "bt": 0, "ev": 0}

            def pertoken(tb, bg=None):
                iT, gJ = iTs[tb % 2], gJs[tb % 2]
                groups = []
                for half in range(2):
                    pend = []

                    def emitW(item):
                        A_, bt, t0l, tw = item
                        bW = 4 + ctr["ev"] % 4
                        pw = ps[:, bW, :]
                        for tt_ in range(4):
                            P.mm(bass.AP(pw.tensor, pw.offset + tt_, [list(pw.ap[0]), [4, 128]]), bt[:, tt_, :], A_[:, :, t0l + tt_])
                        P.copy("act", bcast(Wst[:, 0, tw:tw + 4], [[64, 128], [1, 4]]),
                               pw.rearrange("p (a t) -> p a t", t=4))
                        ctr["ev"] += 1

                    for sub in range(64 // NSUB):
                        t0 = half * 64 + sub * NSUB
                        sbi = ctr["sub"] % 2
                        ctr["sub"] += 1
                        A_, P2_, Gbd_ = As[sbi], P2s[sbi], Gbds[sbi]
                        P.tt("dve", A_[:], iom[:], bcast(iT[:, 0, t0:t0 + NSUB], [[0, 128], [1, NSUB]]), ALU.is_equal)
                        P.tt("dve", P2_[:], iom[:], bcast(iT[:, 1, t0:t0 + NSUB], [[0, 128], [1, NSUB]]), ALU.is_equal)
                        P.tt("pool", Gbd_[:], bcast(gJ[:, t0, :], [[16, NSUB], [0, 8], [1, 16]]),
                             bcast(bm_b, [[0, NSUB], [1, 8], [0, 16]]), ALU.mult)
                        for t4 in range(NSUB // 4):
                            bB = ctr["bt"] % 4
                            bt = Bts[ctr["bt"] % 3]
                            ctr["bt"] += 1
                            for tt_ in range(4):
                                tl = t4 * 4 + tt_
                                P.mm(ps[:, bB, tt_ * 128:(tt_ + 1) * 128],
                                     Gbd_[:, tl, :, :].rearrange("p h i -> p (h i)"), P2_[:, :, tl])
                            P.copy("act", bt[:].rearrange("p a b -> p (a b)"), ps[:, bB, :])
                            pend.append((A_, bt, t4 * 4, sub * NSUB + t4 * 4))
                            if len(pend) > 1:
                                emitW(pend.pop(0))
                        if bg is not None:
                            for _ in range(4):
                                if next(bg, "end") == "end":
                                    bg = None
                                    break
                    while pend:
                        emitW(pend.pop(0))
                    P.dma("sp", Wd[tb * 2 + half], Wst[:])
                    P.flush(barrier=False)
                if bg is not None:
                    for _ in bg:
                        pass

            qs(0)
            for _ in topk(0):
                pass
            for tb in range(16):
                if tb + 1 < 16:
                    qs(tb + 1)
                pertoken(tb, topk(tb + 1) if tb + 1 < 16 else None)
            P.flush()

    def peer_dense(l, s):
        with ExitStack() as st:
            UTs = [sb("dU%d" % i, [128, 8, GA * 128], BF16, st) for i in range(2)]
            Vs = [sb("dV%d" % i, [128, GA, D], BF16, st) for i in range(2)]
            Wt = [sb("dW%d" % i, [128, 8, GA, 64], BF16, st) for i in range(2)]
            actT = [sb("dA%d" % i, [128, 512], BF16, st) for i in range(2)]
            Zt = [sb("dZ%d" % i, [128, GA, 512], BF16, st) for i in range(2)]

            def load_tables(eg):
                a0 = eg * GA
                for k in range(8):
                    P.dma("pool", UTs[eg % 2][:, k, :], DI.peer_uT[l, k * 128:(k + 1) * 128, a0 * 128:(a0 + GA) * 128])
                for ga in range(GA):
                    P.dma("pool", Vs[eg % 2][:, ga, :], DI.peer_v[l, (a0 + ga) * 128:(a0 + ga + 1) * 128, :])

            steps = [(eg, tq) for eg in range(NEG) for tq in range(4)]
            ai = [0]
            yi = [0]

            def stageA(i):
                eg, tq = steps[i]
                tsl = slice(tq * 512, (tq + 1) * 512)
                a0 = eg * GA
                if tq == 1 and eg + 1 < NEG:
                    load_tables(eg + 1)
                w_ = Wt[i % 2]
                P.dma("sp", w_[:], Wd[tq * 8:(tq + 1) * 8, :, a0:a0 + GA, :].rearrange("k b a t -> b k a t"))
                for ga in range(GA):
                    b = ai[0] % 2
                    a_ = actT[ai[0] % 2]
                    ai[0] += 1
                    for k in range(8):
                        P.mm(ps[:, b, :], UTs[eg % 2][:, k, ga * 128:(ga + 1) * 128], hT[:, k, tsl], start=(k == 0), stop=(k == 7))
                    P.act(a_[:], ps[:, b, :], AF.Gelu_apprx_tanh)
                    P.tt("dve", Zt[i % 2][:, ga, :].rearrange("p (k t) -> p k t", k=8),
                         a_[:].rearrange("p (k t) -> p k t", k=8), w_[:, :, ga, :], ALU.mult)

            def stageY(i):
                eg, tq = steps[i]
                tsl = slice(tq * 512, (tq + 1) * 512)
                for dc in range(8):
                    b = 2 + yi[0] % 6
                    yi[0] += 1
                    for ga in range(GA):
                        P.mm(ps[:, b, :], Vs[eg % 2][:, ga, dc * 128:(dc + 1) * 128], Zt[i % 2][:, ga, :],
                             start=(ga == 0), stop=(ga == GA - 1))
                    P.stt("dve", xT[:, dc, tsl], ps[:, b, :], modT[:, l, 5, dc, s:s + 1], xT[:, dc, tsl], ALU.mult, ALU.add)

            load_tables(0)
            stageA(0)
            for i in range(len(steps)):
                if i + 1 < len(steps):
                    stageA(i + 1)
                stageY(i)
                if i % 4 == 3:
                    P.flush(barrier=False)
            P.flush()

    def final_out(s):
        with ExitStack() as st:
            fg = sb("fg", [128, 8], F32, st)
            sq = [sb("fsq%d" % i, [128, 512], F32, st) for i in range(2)]
            rs = [sb("frs%d" % i, [128, 512], F32, st) for i in range(2)]
            tmp = [sb("ftm%d" % i, [128, 512], F32, st) for i in range(2)]
            yk = sb("fyk", [128, 8, 512], F32, st)
            ot = [sb("fot%d" % i, [128, D], F32, st) for i in range(2)]
            P.dma("sp", fg[:], DI.fgT[:, :])
            io = 0
            for tq in range(4):
                tsl = slice(tq * 512, (tq + 1) * 512)
                b = nb()
                for k in range(8):
                    q = sq[k % 2]
                    P.tt("pool", q[:], xT[:, k, tsl], xT[:, k, tsl], ALU.mult)
                    P.mm(ps[:, b, :], ones_f, q[:], start=(k == 0), stop=(k == 7))
                r = rs[tq % 2]
                P.act(r[:], ps[:, b, :], AF.Sqrt, bias=eps_c, scale=1.0 / D)
                P.recip(r[:], r[:])
                for k in range(8):
                    tm = tmp[k % 2]
                    P.tt("dve", tm[:], xT[:, k, tsl], r[:], ALU.mult)
                    P.act(yk[:, k, :], tm[:], AF.Identity, scale=fg[:, k:k + 1])
                for t4 in range(4):
                    o = ot[io % 2]
                    io += 1
                    for kh in range(2):
                        b = nb()
                        for kk in range(4):
                            k = kh * 4 + kk
                            P.tr(ps[:, b, kk * 128:(kk + 1) * 128], yk[:, k, t4 * 128:(t4 + 1) * 128], ident)
                        P.copy("act" if kh else "dve", o[:, kh * 512:(kh + 1) * 512], ps[:, b, :])
                    P.dma("sp", out_d[s, tq * 512 + t4 * 128:tq * 512 + (t4 + 1) * 128, :], o[:])
            P.flush()

    ona = sb("ona", [128, L, 4])
    onb = sb("onb", [128, L, 4])
    P.dma("sp", ona[:], DI.onaT[:, :, :])
    P.dma("sp", onb[:], DI.onbT[:, :, :])

    done = False
    for s in range(nseq):
        load_x(s)
        tap("xT%d" % s, xT[:], [128, 8, T], F32)
        for l in range(L):
            if not stages.startswith("peeronly"):
                norm_mod(l, s, 0)
                tap("h1_%d_%d" % (l, s), hT[:], [128, 8, T], BF16)
                if stages == "norm1":
                    done = True
                    break
                with ExitStack() as stl:
                    yaT = sb("yaT", [128, 4, T], BF16, stl)
                    qT = sb("qT", [128, 4, T], BF16, stl)
                    gmlp(l, s, yaT)
                    tap("ya_%d_%d" % (l, s), yaT[:], [128, 4, T], BF16)
                    if stages == "gmlp":
                        done = True
                        P.flush()
                        break
                    attention(l, s, qT)
                    tap("yb_%d_%d" % (l, s), qT[:], [128, 4, T], BF16)
                    if stages == "attn":
                        done = True
                        P.flush()
                        break
                    group_norm(yaT, ona, l)
                    group_norm(qT, onb, l)
                    tap("yan_%d_%d" % (l, s), yaT[:], [128, 4, T], BF16)
                    tap("ybn_%d_%d" % (l, s), qT[:], [128, 4, T], BF16)
                    out_proj(l, s, yaT, qT)
                    tap("xmix_%d_%d" % (l, s), xT[:], [128, 8, T], F32)
                    P.flush()
                if stages == "mix":
                    done = True
                    break
            norm_mod(l, s, 1)
            tap("h2_%d_%d" % (l, s), hT[:], [128, 8, T], BF16)
            peer_wbuild(l, s)
            if "Wd_%d_%d" % (l, s) in dbg:
                o = nc.dram_tensor("dbg_Wd_%d_%d" % (l, s), [32, 128, 128, 64], BF16, kind="ExternalOutput").ap()
                P.dma("sp", o, Wd)
                P.flush()
            if stages in ("wbuild", "peeronly_wbuild"):
                done = True
                break
            peer_dense(l, s)
            tap("xl_%d_%d" % (l, s), xT[:], [128, 8, T], F32)
            if stages in ("layer0", "peeronly"):
                done = True
                break
        if done:
            break
        final_out(s)

    P.flush()
    return nc, P, dbg_out, list(dram_in.keys())


def host_inputs(inputs, core):
    f = np.float32
    b0 = 2 * core
    m = {}
    m["x"] = np.ascontiguousarray(inputs["x"][b0:b0 + 2])
    c = inputs["c"][b0:b0 + 2]
    m["cT"] = np.ascontiguousarray(c.reshape(2, 8, 128).transpose(2, 1, 0))
    m["ada_w"] = inputs["ada_w"]
    m["ada_bT"] = np.ascontiguousarray(inputs["ada_b"].reshape(L, 48, 128).transpose(2, 0, 1))
    m["n1gT"] = np.ascontiguousarray(inputs["norm1_g"].reshape(L, 8, 128).transpose(2, 0, 1))
    m["n2gT"] = np.ascontiguousarray(inputs["norm2_g"].reshape(L, 8, 128).transpose(2, 0, 1))
    m["w_in"] = inputs["w_in"]
    m["sgu_wT"] = np.ascontiguousarray(inputs["sgu_w"].transpose(3, 0, 1, 2))
    sb_ = inputs["sgu_b"]
    m["sgu_bB"] = np.ascontiguousarray(np.repeat(sb_.reshape(L, 4, 2, 1, 128), 64, axis=3)
                                       .reshape(L, 4, 128, 128).transpose(2, 0, 1, 3))
    m["onaT"] = np.ascontiguousarray(inputs["out_norm_a"].reshape(L, 4, 128).transpose(2, 0, 1))
    m["onbT"] = np.ascontiguousarray(inputs["out_norm_b"].reshape(L, 4, 128).transpose(2, 0, 1))
    m["w_out"] = inputs["w_out"]
    m["peer_wq"] = inputs["peer_wq"]
    m["k1T"] = np.ascontiguousarray(inputs["peer_k1"].transpose(2, 0, 1))
    m["k2T"] = np.ascontiguousarray(inputs["peer_k2"].transpose(2, 0, 1))
    m["peer_uT"] = inputs["_peer_uT"]
    m["peer_v"] = inputs["peer_v"]
    m["fgT"] = np.ascontiguousarray(inputs["final_g"].reshape(8, 128).T)
    m["consts"] = inputs["_consts"]
    return {k: np.asarray(v, dtype=f) for k, v in m.items()}


def kernel(**inputs):
    inputs = {k: np.asarray(v) for k, v in inputs.items()}
    inputs["_peer_uT"] = np.ascontiguousarray(inputs["peer_u"].transpose(0, 2, 1))
    inputs["_consts"] = _consts()
    nc, P, _, used = build()
    in_maps = [{k: v for k, v in host_inputs(inputs, c).items() if k in used} for c in range(NCORES)]
    res = run_bass_kernel_spmd(nc, in_maps, core_ids=list(range(NCORES)))
    out = np.concatenate([r["out"] for r in res.results], axis=0)
    return out.astype(np.float32)
```

```python
import numpy as np
from contextlib import ExitStack
import concourse.bass as bass
import concourse.mybir as mybir
from concourse.bass_utils import run_bass_kernel_spmd

F32 = mybir.dt.float32
BF16 = mybir.dt.bfloat16
U32 = mybir.dt.uint32
ALU = mybir.AluOpType
AF = mybir.ActivationFunctionType
AX = mybir.AxisListType

NCORES = 8
T = 2048
D = 1024
L = 2
EPS = 1e-6
_ESZ = {F32: 4, BF16: 2, U32: 4}


def _esz(dt):
    return _ESZ.get(dt, 4)


class _Op:
    __slots__ = ("eng", "fn", "acc", "dma", "deps", "ms", "semval", "dsem", "dval", "prevd")


class Prog:
    ENGS = ("pe", "act", "dve", "pool", "sp")

    def __init__(self, nc, es):
        self.nc = nc
        self.e = {"pe": nc.tensor, "act": nc.scalar, "dve": nc.vector, "pool": nc.gpsimd, "sp": nc.sync}
        self.sem = {k: es.enter_context(nc.semaphore("pg_" + k)) for k in self.ENGS}
        self.cnt = {k: 0 for k in self.ENGS}
        self.NDS = 8
        self.dsems = {q: [es.enter_context(nc.semaphore("dq_%s%d" % (q, i))) for i in range(self.NDS)]
                      for q in ("sp", "act", "pool")}
        self.dcnt = {q: 0 for q in self.dsems}
        self.dlast = {q: [None] * self.NDS for q in self.dsems}
        self.waited = {k: {} for k in self.ENGS}
        self.pending = []
        self.hist = {}
        self.nins = 0

    @staticmethod
    def regions(ap):
        name = ap.tensor.name
        esz = _esz(ap.dtype)
        apl = ap.ap
        off = ap.offset
        sp = str(ap.space)
        if "SB" not in sp and "PSUM" not in sp:
            ext = sum((c - 1) * abs(s) for s, c in apl) + 1
            return [(name, 0, 1, off * esz, (off + ext) * esz)]
        pstep, pcnt = apl[0]
        if pstep > 0:
            p0 = off // pstep
            f0 = off % pstep
        else:
            p0, f0 = 0, off
        p1 = p0 + pcnt
        dims = [(abs(s), c) for s, c in apl[1:] if c > 1]
        if not dims:
            return [(name, p0, p1, f0 * esz, (f0 + 1) * esz)]
        dims.sort(key=lambda sc: -sc[0])
        s0, c0 = dims[0]
        inner = sum((c - 1) * s for s, c in dims[1:]) + 1
        if len(dims) > 1 and s0 >= inner and c0 <= 32:
            return [(name, p0, p1, (f0 + i * s0) * esz, (f0 + i * s0 + inner) * esz) for i in range(c0)]
        ext = sum((c - 1) * s for s, c in dims) + 1
        return [(name, p0, p1, f0 * esz, (f0 + ext) * esz)]

    def op(self, eng, fn, reads=(), writes=(), dma=False):
        o = _Op()
        o.eng = eng
        o.fn = fn
        o.dma = dma
        acc = []
        for ap in reads:
            if ap is None or isinstance(ap, (int, float)):
                continue
            for r in self.regions(ap):
                acc.append((r, False))
        for ap in writes:
            for r in self.regions(ap):
                acc.append((r, True))
        o.acc = acc
        o.deps = []
        o.ms = False
        o.semval = None
        o.dsem = None
        o.dval = None
        o.prevd = None
        self.pending.append(o)
        return o

    def flush(self, barrier=True):
        hist = self.hist
        pend = self.pending
        for o in pend:
            deps = set()
            for (name, p0, p1, lo, hi), w in o.acc:
                lst = hist.get(name)
                if not lst:
                    continue
                for rec in lst:
                    if rec[0] < p1 and p0 < rec[1] and rec[2] < hi and lo < rec[3]:
                        if w or rec[4]:
                            deps.add(rec[5])
            deps.discard(o)
            for d in deps:
                if (not d.dma) and d.eng == o.eng and o.eng == "pe" and not o.dma:
                    continue
                o.deps.append(d)
                if not d.dma:
                    d.ms = True
            if o.dma:
                q = o.eng
                k = self.dcnt[q]
                self.dcnt[q] += 1
                slot = k % self.NDS
                o.dsem = self.dsems[q][slot]
                o.dval = 16 * (k // self.NDS + 1)
                o.prevd = self.dlast[q][slot]
                self.dlast[q][slot] = o
            for (name, p0, p1, lo, hi), w in o.acc:
                lst = hist.setdefault(name, [])
                if w:
                    lst[:] = [r for r in lst if not (p0 <= r[0] and r[1] <= p1 and lo <= r[2] and r[3] <= hi)]
                    lst.append((p0, p1, lo, hi, True, o))
                else:
                    found = False
                    if not o.dma:
                        for i, r in enumerate(lst):
                            if (not r[4]) and r[0] == p0 and r[1] == p1 and r[2] == lo and r[3] == hi \
                                    and (not r[5].dma) and r[5].eng == o.eng:
                                lst[i] = (p0, p1, lo, hi, False, o)
                                found = True
                                break
                    if not found:
                        lst.append((p0, p1, lo, hi, False, o))
        if barrier:
            last = {}
            for o in pend:
                if not o.dma:
                    last[o.eng] = o
            for o in last.values():
                o.ms = True
        else:
            for lst in hist.values():
                for r in lst:
                    if (not r[5].dma) and r[5].semval is None:
                        r[5].ms = True
        for o in pend:
            eng = self.e[o.eng]
            wd = self.waited[o.eng]
            need = {}
            if o.dma and o.prevd is not None:
                need[id(o.prevd.dsem)] = (o.prevd.dsem, o.prevd.dval)
            for d in o.deps:
                if d.dma:
                    s, v = d.dsem, d.dval
                else:
                    s, v = self.sem[d.eng], d.semval
                cur = need.get(id(s))
                if cur is None or cur[1] < v:
                    need[id(s)] = (s, v)
            for sid, (s, v) in need.items():
                if wd.get(sid, 0) < v:
                    eng.wait_ge(s, v)
                    wd[sid] = v
                    self.nins += 1
            ins = o.fn()
            self.nins += 1
            if o.dma:
                ins.then_inc(o.dsem, 16)
            elif o.ms:
                self.cnt[o.eng] += 1
                o.semval = self.cnt[o.eng]
                ins.then_inc(self.sem[o.eng], 1)
            o.fn = None
            o.acc = None
        self.pending = []
        if barrier:
            for k in self.ENGS:
                eng = self.e[k]
                wd = self.waited[k]
                for k2 in self.ENGS:
                    if k2 == k:
                        continue
                    s = self.sem[k2]
                    v = self.cnt[k2]
                    if wd.get(id(s), 0) < v:
                        eng.wait_ge(s, v)
                        wd[id(s)] = v
                        self.nins += 1
                for q in self.dsems:
                    for slot in range(self.NDS):
                        d = self.dlast[q][slot]
                        if d is None:
                            continue
                        if wd.get(id(d.dsem), 0) < d.dval:
                            eng.wait_ge(d.dsem, d.dval)
                            wd[id(d.dsem)] = d.dval
                            self.nins += 1
            self.hist = {}

    def mm(self, out, lhsT, rhs, start=True, stop=True):
        nc = self.nc
        rd = [lhsT, rhs] + ([] if start else [out])
        return self.op("pe", lambda: nc.tensor.matmul(out, lhsT, rhs, start=start, stop=stop), rd, [out])

    def tr(self, out, in_, ident):
        nc = self.nc
        return self.op("pe", lambda: nc.tensor.transpose(out, in_, ident), [in_, ident], [out])

    def act(self, out, in_, func, bias=None, scale=None):
        nc = self.nc
        kw = {}
        if bias is not None:
            kw["bias"] = bias
        if scale is not None:
            kw["scale"] = scale
        rd = [in_] + [a for a in (bias, scale) if a is not None and not isinstance(a, (int, float))]
        return self.op("act", lambda: nc.scalar.activation(out=out, in_=in_, func=func, **kw), rd, [out])

    def tt(self, eng, out, in0, in1, op):
        e = self.e[eng]
        return self.op(eng, lambda: e.tensor_tensor(out=out, in0=in0, in1=in1, op=op), [in0, in1], [out])

    def ts(self, eng, out, in0, s1, s2, op0, op1=None):
        e = self.e[eng]
        rd = [in0] + [a for a in (s1, s2) if a is not None and not isinstance(a, (int, float))]
        if op1 is None:
            return self.op(eng, lambda: e.tensor_scalar(out=out, in0=in0, scalar1=s1, scalar2=None, op0=op0), rd, [out])
        return self.op(eng, lambda: e.tensor_scalar(out=out, in0=in0, scalar1=s1, scalar2=s2, op0=op0, op1=op1), rd, [out])

    def stt(self, eng, out, in0, scalar, in1, op0, op1):
        e = self.e[eng]
        rd = [in0, in1] + ([] if isinstance(scalar, (int, float)) else [scalar])
        return self.op(eng, lambda: e.scalar_tensor_tensor(out=out, in0=in0, scalar=scalar, in1=in1, op0=op0, op1=op1),
                       rd, [out])

    def copy(self, eng, out, in_):
        if eng == "act":
            return self.act(out, in_, AF.Copy)
        e = self.e[eng]
        return self.op(eng, lambda: e.tensor_copy(out=out, in_=in_), [in_], [out])

    def memset(self, eng, ap, val):
        e = self.e[eng]
        return self.op(eng, lambda: e.memset(ap, val), [], [ap])

    def reduce(self, eng, out, in_, op):
        e = self.e[eng]
        return self.op(eng, lambda: e.tensor_reduce(out=out, in_=in_, axis=AX.X, op=op), [in_], [out])

    def recip(self, out, in_):
        nc = self.nc
        return self.op("dve", lambda: nc.vector.reciprocal(out=out, in_=in_), [in_], [out])

    def vmax(self, out, in_):
        nc = self.nc
        return self.op("dve", lambda: nc.vector.max(out=out, in_=in_), [in_], [out])

    def vmax_index(self, out, in_max, in_values):
        nc = self.nc
        return self.op("dve", lambda: nc.vector.max_index(out=out, in_max=in_max, in_values=in_values),
                       [in_max, in_values], [out])

    def vmatch_replace(self, out, in_to_replace, in_values, imm):
        nc = self.nc
        return self.op("dve", lambda: nc.vector.match_replace(out=out, in_to_replace=in_to_replace,
                                                              in_values=in_values, imm_value=imm),
                       [in_to_replace, in_values], [out])

    def dma(self, q, out, in_):
        e = self.e[q]
        return self.op(q, lambda: e.dma_start(out=out, in_=in_), [in_], [out], dma=True)


C_ID = 0
C_TRI = 128
C_ONE = 256
C_BD = 384
C_SGM = 512
C_IOTA = 640
C_BM = 768
C_EPS = 776
C_AM = 784
NCST = C_AM + 4 * 512


def _consts():
    c = np.zeros((128, NCST), np.float32)
    i = np.arange(128)
    c[:, C_ID:C_ID + 128] = np.eye(128)
    c[:, C_TRI:C_TRI + 128] = (i[:, None] >= i[None, :])
    c[:, C_ONE:C_ONE + 128] = 1.0
    c[:, C_BD:C_BD + 128] = (i[:, None] // 64 == i[None, :] // 64)
    c[:, C_SGM:C_SGM + 128] = (i[:, None] <= i[None, :])
    c[:, C_IOTA:C_IOTA + 128] = i[None, :]
    c[:, C_BM:C_BM + 8] = (i[:, None] // 16 == np.arange(8)[None, :])
    c[:, C_EPS] = EPS
    t = np.arange(512)
    for r in range(4):
        c[:, C_AM + r * 512:C_AM + (r + 1) * 512] = (t[None, :] > r * 128 + i[:, None])
    return c


NWARM = 2
NWARM2 = 0


def build(dbg=(), stages="all", nseq=2):
    nc = bass.Bass("TRN2", target_bir_lowering=False)
    es = ExitStack()
    P = Prog(nc, es)
    dram_in = {}

    def din(name, shape, dt=F32):
        dram_in[name] = nc.dram_tensor(name, list(shape), dt, kind="ExternalInput").ap()
        return dram_in[name]

    _shapes = {"x": [2, T, D], "cT": [128, 8, 2], "ada_w": [L, D, 6 * D], "ada_bT": [128, L, 48],
               "n1gT": [128, L, 8], "n2gT": [128, L, 8], "w_in": [L, D, 2560], "sgu_wT": [128, L, 8, 128],
               "sgu_bB": [128, L, 4, 128], "onaT": [128, L, 4], "onbT": [128, L, 4], "w_out": [L, D, D],
               "peer_wq": [L, D, D], "k1T": [64, L, 128], "k2T": [64, L, 128], "peer_uT": [L, D, 16384],
               "peer_v": [L, 16384, D], "fgT": [128, 8], "consts": [128, NCST]}

    class _DI:
        def __getattr__(self, name):
            if name not in dram_in:
                din(name, _shapes[name])
            return dram_in[name]
    DI = _DI()
    out_d = nc.dram_tensor("out", [2, T, D], F32, kind="ExternalOutput").ap()
    dbg_out = {}

    _uid = [0]

    def sb(name, shape, dt=F32, stack=es):
        _uid[0] += 1
        return stack.enter_context(nc.sbuf_tensor("s%d_%s" % (_uid[0], name), list(shape), dt))

    def tap(name, ap, shape, dt):
        if name in dbg:
            o = nc.dram_tensor("dbg_" + name, list(shape), dt, kind="ExternalOutput").ap()
            dbg_out[name] = o
            P.dma("sp", o, ap)

    ps = es.enter_context(nc.psum_tensor("ps", [128, 8, 512], F32))
    psb = ps[:].bitcast(BF16)
    bank_i = [0]

    def nb():
        b = bank_i[0]
        bank_i[0] = (b + 1) % 8
        return b

    cst = sb("cst", [128, NCST])
    cb = sb("cb", [128, 784], BF16)
    xT = sb("xT", [128, 8, T])
    hT = sb("hT", [128, 8, T], BF16)
    modT = sb("modT", [128, L, 6, 8, 2])
    gsc = sb("gsc", [128, L, 2, 2, 8])
    small = sb("small", [128, 64])
    P.dma("sp", cst[:], DI.consts[:, :])
    ident = cst[:, C_ID:C_ID + 128]
    ones_f = cst[:, C_ONE:C_ONE + 128]
    eps_c = cst[:, C_EPS:C_EPS + 1]
    P.copy("dve", cb[:, 0:784], cst[:, 0:784])
    ident_b = cb[:, 0:128]
    tri_b = cb[:, 128:256]
    ones_b = cb[:, 256:384]
    bd_b = cb[:, 384:512]
    iota_b = cb[:, C_IOTA:C_IOTA + 128]
    bm_b = cb[:, C_BM:C_BM + 8]

    with ExitStack() as st:
        cT = sb("cT", [128, 8, 2], F32, st)
        cact = sb("cact", [128, 8, 2], F32, st)
        adab = sb("adab", [128, L, 48], F32, st)
        n1g = sb("n1g", [128, L, 8], F32, st)
        n2g = sb("n2g", [128, L, 8], F32, st)
        wada = [sb("wada%d" % i, [128, 8, 1024], F32, st) for i in range(2)]
        P.dma("sp", cT[:], DI.cT[:, :, :])
        P.dma("sp", adab[:], DI.ada_bT[:, :, :])
        P.dma("sp", n1g[:], DI.n1gT[:, :, :])
        P.dma("sp", n2g[:], DI.n2gT[:, :, :])
        P.act(cact[:], cT[:], AF.Silu)
        it = 0
        for l in range(L):
            for js in range(6):
                w = wada[it % 2]
                it += 1
                for k in range(8):
                    P.dma("sp", w[:, k, :], DI.ada_w[l, k * 128:(k + 1) * 128, js * 1024:(js + 1) * 1024])
                for ec in range(8):
                    b = nb()
                    for k in range(8):
                        P.mm(ps[:, b, 0:2], w[:, k, ec * 128:(ec + 1) * 128], cact[:, k, :], start=(k == 0), stop=(k == 7))
                    P.ts("dve", modT[:, l, js, ec, :], ps[:, b, 0:2], adab[:, l, js * 8 + ec:js * 8 + ec + 1], None, ALU.add)
        for l in range(L):
            for s in range(2):
                for w, (j, g) in enumerate(((1, n1g), (4, n2g))):
                    P.ts("dve", small[:, 0:8], modT[:, l, j, :, s], 1.0, None, ALU.add)
                    P.tt("dve", gsc[:, l, w, s, :], small[:, 0:8], g[:, l, :], ALU.mult)
        tap("modT", modT[:], [128, L, 6, 8, 2], F32)
        P.flush()

    def load_x(s):
        with ExitStack() as st:
            xin = [sb("xin%d" % i, [128, D], F32, st) for i in range(2)]
            for tb in range(16):
                xi = xin[tb % 2]
                P.dma("sp", xi[:], DI.x[s, tb * 128:(tb + 1) * 128, :])
                for kh in range(2):
                    b = nb()
                    for kk in range(4):
                        k = kh * 4 + kk
                        P.tr(ps[:, b, kk * 128:(kk + 1) * 128], xi[:, k * 128:(k + 1) * 128], ident)
                    P.copy("act" if kh == 0 else "dve", xT[:, kh * 4:(kh + 1) * 4, tb * 128:(tb + 1) * 128],
                           ps[:, b, :].rearrange("p (a b) -> p a b", a=4))
            P.flush()

    def rms_bcast(st, tq, tagsrc):
        sq = [sb("sq%d" % i, [128, 512], F32, st) for i in range(2)]
        rs = sb("rs", [128, 512], F32, st)
        return sq, rs

    def norm_mod(l, s, w):
        jsh = 0 if w == 0 else 3
        with ExitStack() as st:
            sq = [sb("sq%d" % i, [128, 512], F32, st) for i in range(2)]
            rs = [sb("rs%d" % i, [128, 512], F32, st) for i in range(2)]
            tmp = [sb("nt%d" % i, [128, 512], F32, st) for i in range(2)]
            for tq in range(4):
                tsl = slice(tq * 512, (tq + 1) * 512)
                b = nb()
                for k in range(8):
                    q = sq[k % 2]
                    P.tt("pool", q[:], xT[:, k, tsl], xT[:, k, tsl], ALU.mult)
                    P.mm(ps[:, b, :], ones_f, q[:], start=(k == 0), stop=(k == 7))
                r = rs[tq % 2]
                P.act(r[:], ps[:, b, :], AF.Sqrt, bias=eps_c, scale=1.0 / D)
                P.recip(r[:], r[:])
                for k in range(8):
                    tm = tmp[k % 2]
                    P.tt("dve", tm[:], xT[:, k, tsl], r[:], ALU.mult)
                    P.act(hT[:, k, tsl], tm[:], AF.Identity, bias=modT[:, l, jsh, k, s:s + 1],
                          scale=gsc[:, l, w, s, k:k + 1])
            P.flush()

    def bcast(ap, dims):
        return bass.AP(ap.tensor, ap.offset, [list(ap.ap[0])] + [list(d) for d in dims])

    one_c = cst[:, C_ONE:C_ONE + 1]

    def load_wslice(dst, src2d, c0, ncols):
        for k in range(8):
            P.dma("pool", dst[:, k, :], src2d[k * 128:(k + 1) * 128, c0:c0 + ncols])

    def proj_fm(w, dstfn, evac):
        for cc in range(4):
            for tq in range(4):
                tsl = slice(tq * 512, (tq + 1) * 512)
                b = nb()
                for k in range(8):
                    P.mm(ps[:, b, :], w[:, k, cc * 128:(cc + 1) * 128], hT[:, k, tsl], start=(k == 0), stop=(k == 7))
                evac(dstfn(cc, tsl), ps[:, b, :], cc * 4 + tq)

    def group_norm(yT, gain, l):
        with ExitStack() as st:
            sqb = [sb("gq%d" % i, [128, 512], BF16, st) for i in range(2)]
            rs = [sb("gr%d" % i, [128, 512], F32, st) for i in range(2)]
            tf = [sb("gt%d" % i, [128, 512], F32, st) for i in range(2)]
            i = 0
            for cc in range(4):
                for tq in range(4):
                    tsl = slice(tq * 512, (tq + 1) * 512)
                    q, r, t_ = sqb[i % 2], rs[i % 2], tf[i % 2]
                    i += 1
                    P.tt("pool", q[:], yT[:, cc, tsl], yT[:, cc, tsl], ALU.mult)
                    b = nb()
                    P.mm(ps[:, b, :], bd_b, q[:])
                    P.act(r[:], ps[:, b, :], AF.Sqrt, bias=eps_c, scale=1.0 / 64)
                    P.recip(r[:], r[:])
                    P.tt("dve", t_[:], yT[:, cc, tsl], r[:], ALU.mult)
                    P.act(yT[:, cc, tsl], t_[:], AF.Identity, scale=gain[:, l, cc:cc + 1])
            P.flush()

    def gmlp(l, s, yaT):
        with ExitStack() as st:
            wu = sb("wu", [128, 8, 512], BF16, st)
            wv = sb("wv", [128, 8, 512], BF16, st)
            sw = sb("sw", [128, 8, 128], F32, st)
            swb = sb("swb", [128, 8, 128], BF16, st)
            bsB = sb("bsB", [128, 4, 128], F32, st)
            st8 = sb("st8", [128, 32], F32, st)
            vv = [sb("vv%d" % i, [128, 512], F32, st) for i in range(2)]
            cen = sb("cen", [128, 512], F32, st)
            sqv = sb("sqv", [128, 512], F32, st)
            vn = [sb("vn%d" % i, [128, 512], BF16, st) for i in range(2)]
            tmp = [sb("gtmp%d" % i, [128, 128], F32, st) for i in range(2)]
            load_wslice(wu, DI.w_in[l], 0, 512)
            load_wslice(wv, DI.w_in[l], 512, 512)
            P.dma("sp", sw[:], DI.sgu_wT[:, l, :, :])
            P.dma("sp", bsB[:], DI.sgu_bB[:, l, :, :])
            for g in range(8):
                P.tt("dve", swb[:, g, :], sw[:, g, :], cst[:, C_SGM:C_SGM + 128], ALU.mult)
            proj_fm(wu, lambda cc, tsl: yaT[:, cc, tsl], lambda o, p_, i: P.act(o, p_, AF.Gelu_apprx_tanh))
            it = 0
            for n in range(16):
                nsl = slice(n * 128, (n + 1) * 128)
                b = nb()
                for k in range(8):
                    P.mm(ps[:, b, :], hT[:, k, nsl], wv[:, k, :], start=(k == 0), stop=(k == 7))
                v = vv[n % 2]
                P.act(v[:], ps[:, b, :], AF.Gelu_apprx_tanh)
                v3 = v[:].rearrange("p (g d) -> p g d", g=8)
                cen3 = cen[:].rearrange("p (g d) -> p g d", g=8)
                sq3 = sqv[:].rearrange("p (g d) -> p g d", g=8)
                P.reduce("dve", st8[:, 0:8], v3, ALU.add)
                P.ts("dve", st8[:, 8:16], st8[:, 0:8], -1.0 / 64, None, ALU.mult)
                P.tt("dve", cen3, v3, bcast(st8[:, 8:16], [[1, 8], [0, 64]]), ALU.add)
                P.tt("pool", sqv[:], cen[:], cen[:], ALU.mult)
                P.reduce("dve", st8[:, 16:24], sq3, ALU.add)
                P.act(st8[:, 24:32], st8[:, 16:24], AF.Sqrt, bias=eps_c, scale=1.0 / 64)
                P.recip(st8[:, 24:32], st8[:, 24:32])
                vb_ = vn[n % 2]
                P.tt("dve", vb_[:].rearrange("p (g d) -> p g d", g=8), cen3, bcast(st8[:, 24:32], [[1, 8], [0, 64]]), ALU.mult)
                for cc in range(4):
                    for half in range(2):
                        g = 2 * cc + half
                        rows = slice(half * 64, half * 64 + 64)
                        b2 = nb()
                        P.mm(ps[:, b2, 0:128], vb_[:, cc * 128:(cc + 1) * 128], swb[:, g, :])
                        tm = tmp[it % 2]
                        it += 1
                        P.tt("dve", tm[rows, :], ps[rows, b2, 0:128], bsB[rows, cc, :], ALU.add)
                        P.tt("pool", yaT[rows, cc, nsl], tm[rows, :], yaT[rows, cc, nsl], ALU.mult)
            P.flush()

    def attention(l, s, qT):
        with ExitStack() as st:
            kTn = sb("kTn", [128, 4, T], BF16, st)
            vtok = sb("vtok", [128, 16, 512], BF16, st)
            with ExitStack() as st2:
                ws = [sb("wqk%d" % i, [128, 8, 512], BF16, st2) for i in range(2)]
                load_wslice(ws[0], DI.w_in[l], 1024, 512)
                load_wslice(ws[1], DI.w_in[l], 1536, 512)
                proj_fm(ws[0], lambda cc, tsl: qT[:, cc, tsl],
                        lambda o, p_, i: P.copy("act" if i % 2 else "dve", o, p_))
                proj_fm(ws[1], lambda cc, tsl: kTn[:, cc, tsl],
                        lambda o, p_, i: (P.act(o, p_, AF.Copy, scale=-0.125) if i % 2 else
                                          P.ts("dve", o, p_, -0.125, None, ALU.mult)))
                P.flush(barrier=False)
                load_wslice(ws[0], DI.w_in[l], 2048, 512)
                for n in range(16):
                    b = nb()
                    for k in range(8):
                        P.mm(ps[:, b, :], hT[:, k, n * 128:(n + 1) * 128], ws[0][:, k, :], start=(k == 0), stop=(k == 7))
                    P.copy("act" if n % 2 else "dve", vtok[:, n, :], ps[:, b, :])
                P.flush()
            tap("qT_%d_%d" % (l, s), qT[:], [128, 4, T], BF16)
            tap("kTn_%d_%d" % (l, s), kTn[:], [128, 4, T], BF16)
            tap("vtok_%d_%d" % (l, s), vtok[:], [128, 16, 512], BF16)
            with ExitStack() as st2:
                NB_ = 3
                E = [sb("aE%d" % i, [128, 512], F32, st2) for i in range(NB_)]
                Lf = [sb("aLf%d" % i, [128, 512], F32, st2) for i in range(2)]
                Lb = [sb("aLb%d" % i, [128, 512], BF16, st2) for i in range(NB_)]
                SL = sb("aSL", [128, 512], F32, st2)
                SLb = [sb("aSLb%d" % i, [128, 512], BF16, st2) for i in range(NB_)]
                af = [sb("aaf%d" % i, [128, 512], F32, st2) for i in range(2)]
                aa = [sb("aa%d" % i, [128, 512], BF16, st2) for i in range(NB_)]
                blocks = []
                gi = 0
                for h in range(8):
                    for qb in range(4):
                        nkb = 4 * (qb + 1)
                        for kb in reversed(range(nkb)):
                            blocks.append((h, qb, kb, kb == nkb - 1, kb == 0, 5 + gi % 2))
                        gi += 1
                nd = [0]

                def S1(i):
                    h, qb, kb, first, last, bo = blocks[i]
                    cc = h // 2
                    rows = slice((h % 2) * 64, (h % 2) * 64 + 64)
                    tsl = slice(qb * 512, (qb + 1) * 512)
                    ksl = slice(kb * 128, (kb + 1) * 128)
                    r = kb - 4 * qb
                    bz = i % 2
                    e_, lb_ = E[i % NB_], Lb[i % NB_]
                    P.mm(ps[:, bz, :], kTn[rows, cc, ksl], qT[rows, cc, tsl])
                    for _ in range(NWARM):
                        P.mm(ps[:, 7, :], ones_b, cb[:, 0:512])
                    P.act(e_[:], ps[:, bz, :], AF.Exp, scale=-1.0)
                    if r >= 0:
                        am = cst[:, C_AM + r * 512:C_AM + (r + 1) * 512]
                        lf_ = Lf[nd[0] % 2]
                        nd[0] += 1
                        P.act(lf_[:], e_[:], AF.Ln, bias=one_c)
                        P.tt("pool", lb_[:], lf_[:], am, ALU.mult)
                    else:
                        P.act(lb_[:], e_[:], AF.Ln, bias=one_c)
                    if not last:
                        if first:
                            P.copy("dve", SL[:], lb_[:])
                            P.copy("dve", SLb[i % NB_][:], lb_[:])
                        else:
                            P.tt("dve", SL[:], SL[:], lb_[:], ALU.add)
                            P.copy("dve", SLb[i % NB_][:], SL[:])

                def S2(i):
                    h, qb, kb, first, last, bo = blocks[i]
                    cc = h // 2
                    rows = slice((h % 2) * 64, (h % 2) * 64 + 64)
                    tsl = slice(qb * 512, (qb + 1) * 512)
                    ksl = slice(kb * 128, (kb + 1) * 128)
                    r = kb - 4 * qb
                    bc_ = 2 + i % 3
                    lb_, a_ = Lb[i % NB_], aa[i % NB_]
                    P.mm(ps[:, bc_, :], tri_b, lb_[:], start=True, stop=False)
                    if not first:
                        P.mm(ps[:, bc_, :], ones_b, SLb[(i - 1) % NB_][:], start=False, stop=False)
                    P.mm(ps[:, bc_, :], kTn[rows, cc, ksl], qT[rows, cc, tsl], start=False, stop=True)
                    if r >= 0:
                        am = cst[:, C_AM + r * 512:C_AM + (r + 1) * 512]
                        af_ = af[i % 2]
                        P.act(af_[:], ps[:, bc_, :], AF.Exp, scale=-1.0)
                        P.tt("pool", a_[:], af_[:], am, ALU.mult)
                    else:
                        P.act(a_[:], ps[:, bc_, :], AF.Exp, scale=-1.0)

                def S3(i):
                    h, qb, kb, first, last, bo = blocks[i]
                    cc = h // 2
                    rows = slice((h % 2) * 64, (h % 2) * 64 + 64)
                    tsl = slice(qb * 512, (qb + 1) * 512)
                    P.mm(ps[:, bo, :], vtok[:, kb, cc * 128:(cc + 1) * 128], aa[i % NB_][:], start=first, stop=last)
                    if last:
                        P.copy("dve", qT[rows, cc, tsl], ps[rows, bo, :])

                nblk = len(blocks)
                for n in range(nblk + 2):
                    if n < nblk:
                        S1(n)
                    if 0 <= n - 1 < nblk:
                        S2(n - 1)
                    if 0 <= n - 2 < nblk:
                        S3(n - 2)
                    if n % 16 == 15:
                        P.flush(barrier=False)
                P.flush()

    def out_proj(l, s, yaT, ybT):
        with ExitStack() as st:
            wo = sb("wo", [128, 8, D], BF16, st)
            load_wslice(wo, DI.w_out[l], 0, D)
            for dc in range(8):
                for tq in range(4):
                    tsl = slice(tq * 512, (tq + 1) * 512)
                    b = nb()
                    for c8 in range(8):
                        src = yaT[:, c8, tsl] if c8 < 4 else ybT[:, c8 - 4, tsl]
                        P.mm(ps[:, b, :], wo[:, c8, dc * 128:(dc + 1) * 128], src, start=(c8 == 0), stop=(c8 == 7))
                    P.stt("dve", xT[:, dc, tsl], ps[:, b, :], modT[:, l, 2, dc, s:s + 1], xT[:, dc, tsl], ALU.mult, ALU.add)
            P.flush()

    Wd = nc.dram_tensor("Wd_scratch", [32, 128, 128, 64], BF16, kind="Internal").ap()
    GA = 4
    NEG = 32

    def peer_wbuild(l, s):
        with ExitStack() as st:
            wq = sb("wq", [128, 8, D], BF16, st)
            kbd = sb("kbd", [128, 256], BF16, st)
            qp = sb("qp", [128, 8, 512], BF16, st)
            S = sb("pS", [128, 8, 256], F32, st)
            V12 = sb("pV12", [128, 8, 2, 16], F32, st)
            I12 = sb("pI12", [128, 8, 2, 16], U32, st)
            I12f = sb("pI12f", [128, 2, 8, 16], F32, st)
            wk = sb("pwk", [128, 16, 128], F32, st)
            cand = sb("pcand", [128, 4, 256], F32, st)
            c8 = sb("pc8", [128, 8, 16], F32, st)
            cI = sb("pcI", [128, 8, 16], U32, st)
            pf = sb("ppf", [128, 3, 8, 16], F32, st)
            cIj = sb("pcIj", [128, 2, 8, 16], U32, st)
            oh = sb("poh", [128, 4, 16, 16], F32, st)
            ge = sb("pge", [128, 8, 16], F32, st)
            zs = sb("pzs", [128, 16], F32, st)
            abg = sb("pabg", [128, 3, 8, 16], F32, st)
            abgTs = [sb("pabgT%d" % i, [128, 3, 128], BF16, st) for i in range(2)]
            NSUB = 8
            iom = sb("piom", [128, 128, NSUB], BF16, st)
            As = [sb("pA%d" % i, [128, 128, NSUB], BF16, st) for i in range(2)]
            Bs = [sb("pB%d" % i, [128, 128, NSUB], BF16, st) for i in range(2)]
            Wst = sb("pWst", [128, 128, 64], BF16, st)
            iota16 = cst[:, C_IOTA:C_IOTA + 16]
            load_wslice(wq, DI.peer_wq[l], 0, D)
            P.memset("dve", kbd[:], 0.0)
            P.dma("pool", kbd[0:64, 0:128], DI.k1T[:, l, :])
            P.dma("pool", kbd[64:128, 128:256], DI.k2T[:, l, :])
            P.copy("dve", iom[:], bcast(iota_b, [[1, 128], [0, NSUB]]))

            def qproj(tb4):
                tsl = slice(tb4 * 512, (tb4 + 1) * 512)
                for h in range(8):
                    b = nb()
                    for k in range(8):
                        P.mm(ps[:, b, :], wq[:, k, h * 128:(h + 1) * 128], hT[:, k, tsl], start=(k == 0), stop=(k == 7))
                    P.copy("act", qp[:, h, :], ps[:, b, :])

            def qs(tb):
                if tb % 4 == 0:
                    qproj(tb // 4)
                o = (tb % 4) * 128
                for h2 in range(4):
                    b = nb()
                    for h1 in range(2):
                        P.mm(ps[:, b, h1 * 256:(h1 + 1) * 256], qp[:, 2 * h2 + h1, o:o + 128], kbd[:])
                    P.copy("act", S[:, h2 * 2:h2 * 2 + 2, :], ps[:, b, :].rearrange("p (a c) -> p a c", a=2))

            def topk(tb):
                abgT = abgTs[tb % 2]
                grp = [(h, hf) for h in range(8) for hf in range(2)]
                for h, hf in grp:
                    P.vmax(V12[:, h, hf, 0:8], S[:, h, hf * 128:(hf + 1) * 128])
                    if hf: yield
                for h, hf in grp:
                    P.vmax_index(I12[:, h, hf, 0:8], V12[:, h, hf, 0:8], S[:, h, hf * 128:(hf + 1) * 128])
                    if hf and h % 2: yield
                for gi, (h, hf) in enumerate(grp):
                    P.vmatch_replace(wk[:, gi, :], V12[:, h, hf, 0:8], S[:, h, hf * 128:(hf + 1) * 128], -1e30)
                    if hf: yield
                for gi, (h, hf) in enumerate(grp):
                    P.vmax(V12[:, h, hf, 8:16], wk[:, gi, :])
                    if hf: yield
                for gi, (h, hf) in enumerate(grp):
                    P.vmax_index(I12[:, h, hf, 8:16], V12[:, h, hf, 8:16], wk[:, gi, :])
                    if hf and h % 2: yield
                for hf in range(2):
                    P.copy("pool", I12f[:, hf, :, :], I12[:, :, hf, :])
                wk2 = wk[:].rearrange("p (a c) d -> p a (c d)", c=2)
                for hh in range(2):
                    hs = slice(4 * hh, 4 * hh + 4)
                    v1 = V12[:, hs, 0, :]
                    v2 = V12[:, hs, 1, :]
                    cand4 = cand[:].rearrange("p h (i j) -> p h i j", i=16)
                    P.tt("dve", cand4, bcast(v1, [[32, 4], [1, 16], [0, 16]]), bcast(v2, [[32, 4], [0, 16], [1, 16]]), ALU.add)
                    yield
                    for hl in range(4):
                        P.vmax(c8[:, 4 * hh + hl, 0:8], cand[:, hl, :])
                    yield
                    for hl in range(4):
                        P.vmax_index(cI[:, 4 * hh + hl, 0:8], c8[:, 4 * hh + hl, 0:8], cand[:, hl, :])
                    for hl in range(4):
                        P.vmatch_replace(wk2[:, hl, :], c8[:, 4 * hh + hl, 0:8], cand[:, hl, :], -1e30)
                    yield
                    for hl in range(4):
                        P.vmax(c8[:, 4 * hh + hl, 8:16], wk2[:, hl, :])
                    for hl in range(4):
                        P.vmax_index(cI[:, 4 * hh + hl, 8:16], c8[:, 4 * hh + hl, 8:16], wk2[:, hl, :])
                    yield
                P.tt("dve", ge[:], c8[:], bcast(c8[:, :, 0:1], [[16, 8], [0, 16]]), ALU.subtract)
                P.act(ge[:], ge[:], AF.Exp)
                P.reduce("dve", zs[:, 0:8], ge[:], ALU.add)
                P.recip(zs[:, 8:16], zs[:, 0:8])
                P.tt("pool", abg[:, 2, :, :], ge[:], bcast(zs[:, 8:16], [[1, 8], [0, 16]]), ALU.mult)
                yield
                P.ts("dve", cIj[:, 0, :, :], cI[:], 15, None, ALU.bitwise_and)
                P.ts("dve", cIj[:, 1, :, :], cI[:], 4, None, ALU.logical_shift_right)
                P.copy("dve", pf[:, 1, :, :], cIj[:, 0, :, :])
                P.copy("dve", pf[:, 2, :, :], cIj[:, 1, :, :])
                yield
                for hh in range(2):
                    hs = slice(4 * hh, 4 * hh + 4)
                    for w_, (pi, hf) in enumerate(((2, 0), (1, 1))):
                        P.tt("dve", oh[:], bcast(iota16, [[0, 4], [0, 16], [1, 16]]),
                             bcast(pf[:, pi, hs, :], [[16, 4], [1, 16], [0, 16]]), ALU.is_equal)
                        P.tt("pool", oh[:], oh[:], bcast(I12f[:, hf, hs, :], [[16, 4], [0, 16], [1, 16]]), ALU.mult)
                        P.reduce("dve", abg[:, w_, hs, :], oh[:], ALU.add)
                        yield
                for w_ in range(3):
                    b = nb()
                    P.tr(ps[:, b, 0:128], abg[:, w_, :, :].rearrange("p h k -> p (h k)"), ident)
                    P.copy("act", abgT[:, w_, :], ps[:, b, 0:128])
                yield

            ctr = {"sub": 0, "ev": 0}

            def pertoken(tb, bg=None):
                abgT = abgTs[tb % 2]
                for half in range(2):
                    for sub in range(64 // NSUB):
                        t0 = half * 64 + sub * NSUB
                        sbi = ctr["sub"] % 2
                        ctr["sub"] += 1
                        A_, B_ = As[sbi], Bs[sbi]
                        P.tt("dve", A_[:], iom[:], bcast(abgT[:, 0, t0:t0 + NSUB], [[0, 128], [1, NSUB]]), ALU.is_equal)
                        P.tt("dve", B_[:], iom[:], bcast(abgT[:, 1, t0:t0 + NSUB], [[0, 128], [1, NSUB]]), ALU.is_equal)
                        P.tt("pool", B_[:], B_[:], bcast(abgT[:, 2, t0:t0 + NSUB], [[0, 128], [1, NSUB]]), ALU.mult)
                        for t4 in range(NSUB // 4):
                            bW = ctr["ev"] % 8
                            ctr["ev"] += 1
                            pw = ps[:, bW, :]
                            for tt_ in range(4):
                                tl = t4 * 4 + tt_
                                P.mm(bass.AP(pw.tensor, pw.offset + tt_, [list(pw.ap[0]), [4, 128]]), B_[:, :, tl], A_[:, :, tl])
                            tw = sub * NSUB + t4 * 4
                            P.copy("act", bcast(Wst[:, 0, tw:tw + 4], [[64, 128], [1, 4]]),
                                   pw.rearrange("p (a t) -> p a t", t=4))
                        if bg is not None:
                            for _ in range(4):
                                if next(bg, "end") == "end":
                                    bg = None
                                    break
                    P.dma("sp", Wd[tb * 2 + half], Wst[:])
                    P.flush(barrier=False)
                if bg is not None:
                    for _ in bg:
                        pass

            qs(0)
            for _ in topk(0):
                pass
            for tb in range(16):
                if tb + 1 < 16:
                    qs(tb + 1)
                pertoken(tb, topk(tb + 1) if tb + 1 < 16 else None)
            P.flush()

    def peer_dense(l, s):
        with ExitStack() as st:
            UTs = [sb("dU%d" % i, [128, 8, GA * 128], BF16, st) for i in range(2)]
            Vs = [sb("dV%d" % i, [128, GA, D], BF16, st) for i in range(2)]
            Wt = [sb("dW%d" % i, [128, 8, GA, 64], BF16, st) for i in range(2)]
            actT = [sb("dA%d" % i, [128, 512], BF16, st) for i in range(2)]
            Zt = [sb("dZ%d" % i, [128, GA, 512], BF16, st) for i in range(2)]

            def load_tables(eg):
                a0 = eg * GA
                for k in range(8):
                    P.dma("pool", UTs[eg % 2][:, k, :], DI.peer_uT[l, k * 128:(k + 1) * 128, a0 * 128:(a0 + GA) * 128])
                for ga in range(GA):
                    P.dma("pool", Vs[eg % 2][:, ga, :], DI.peer_v[l, (a0 + ga) * 128:(a0 + ga + 1) * 128, :])

            steps = [(eg, tq) for eg in range(NEG) for tq in range(4)]
            ai = [0]
            yi = [0]

            def stageA(i):
                eg, tq = steps[i]
                tsl = slice(tq * 512, (tq + 1) * 512)
                a0 = eg * GA
                if tq == 1 and eg + 1 < NEG:
                    load_tables(eg + 1)
                w_ = Wt[i % 2]
                P.dma("sp", w_[:], Wd[tq * 8:(tq + 1) * 8, :, a0:a0 + GA, :].rearrange("k b a t -> b k a t"))
                for ga in range(GA):
                    b = ai[0] % 2
                    a_ = actT[ai[0] % 2]
                    ai[0] += 1
                    for k in range(8):
                        P.mm(ps[:, b, :], UTs[eg % 2][:, k, ga * 128:(ga + 1) * 128], hT[:, k, tsl], start=(k == 0), stop=(k == 7))
                    P.act(a_[:], ps[:, b, :], AF.Gelu_apprx_tanh)
                    P.tt("dve", Zt[i % 2][:, ga, :].rearrange("p (k t) -> p k t", k=8),
                         a_[:].rearrange("p (k t) -> p k t", k=8), w_[:, :, ga, :], ALU.mult)

            def stageY(i):
                eg, tq = steps[i]
                tsl = slice(tq * 512, (tq + 1) * 512)
                for dc in range(8):
                    b = 2 + yi[0] % 6
                    yi[0] += 1
                    for ga in range(GA):
                        P.mm(ps[:, b, :], Vs[eg % 2][:, ga, dc * 128:(dc + 1) * 128], Zt[i % 2][:, ga, :],
                             start=(ga == 0), stop=(ga == GA - 1))
                    P.stt("dve", xT[:, dc, tsl], ps[:, b, :], modT[:, l, 5, dc, s:s + 1], xT[:, dc, tsl], ALU.mult, ALU.add)

            load_tables(0)
            stageA(0)
            for i in range(len(steps)):
                if i + 1 < len(steps):
                    stageA(i + 1)
                stageY(i)
                if i % 4 == 3:
                    P.flush(barrier=False)
            P.flush()

    def final_out(s):
        with ExitStack() as st:
            fg = sb("fg", [128, 8], F32, st)
            sq = [sb("fsq%d" % i, [128, 512], F32, st) for i in range(2)]
            rs = [sb("frs%d" % i, [128, 512], F32, st) for i in range(2)]
            tmp = [sb("ftm%d" % i, [128, 512], F32, st) for i in range(2)]
            yk = sb("fyk", [128, 8, 512], F32, st)
            ot = [sb("fot%d" % i, [128, D], F32, st) for i in range(2)]
            P.dma("sp", fg[:], DI.fgT[:, :])
            io = 0
            for tq in range(4):
                tsl = slice(tq * 512, (tq + 1) * 512)
                b = nb()
                for k in range(8):
                    q = sq[k % 2]
                    P.tt("pool", q[:], xT[:, k, tsl], xT[:, k, tsl], ALU.mult)
                    P.mm(ps[:, b, :], ones_f, q[:], start=(k == 0), stop=(k == 7))
                r = rs[tq % 2]
                P.act(r[:], ps[:, b, :], AF.Sqrt, bias=eps_c, scale=1.0 / D)
                P.recip(r[:], r[:])
                for k in range(8):
                    tm = tmp[k % 2]
                    P.tt("dve", tm[:], xT[:, k, tsl], r[:], ALU.mult)
                    P.act(yk[:, k, :], tm[:], AF.Identity, scale=fg[:, k:k + 1])
                for t4 in range(4):
                    o = ot[io % 2]
                    io += 1
                    for kh in range(2):
                        b = nb()
                        for kk in range(4):
                            k = kh * 4 + kk
                            P.tr(ps[:, b, kk * 128:(kk + 1) * 128], yk[:, k, t4 * 128:(t4 + 1) * 128], ident)
                        P.copy("act" if kh else "dve", o[:, kh * 512:(kh + 1) * 512], ps[:, b, :])
                    P.dma("sp", out_d[s, tq * 512 + t4 * 128:tq * 512 + (t4 + 1) * 128, :], o[:])
            P.flush()

    ona = sb("ona", [128, L, 4])
    onb = sb("onb", [128, L, 4])
    P.dma("sp", ona[:], DI.onaT[:, :, :])
    P.dma("sp", onb[:], DI.onbT[:, :, :])

    done = False
    for s in range(nseq):
        load_x(s)
        tap("xT%d" % s, xT[:], [128, 8, T], F32)
        for l in range(L):
            if not stages.startswith("peeronly"):
                norm_mod(l, s, 0)
                tap("h1_%d_%d" % (l, s), hT[:], [128, 8, T], BF16)
                if stages == "norm1":
                    done = True
                    break
                with ExitStack() as stl:
                    yaT = sb("yaT", [128, 4, T], BF16, stl)
                    qT = sb("qT", [128, 4, T], BF16, stl)
                    gmlp(l, s, yaT)
                    tap("ya_%d_%d" % (l, s), yaT[:], [128, 4, T], BF16)
                    if stages == "gmlp":
                        done = True
                        P.flush()
                        break
                    attention(l, s, qT)
                    tap("yb_%d_%d" % (l, s), qT[:], [128, 4, T], BF16)
                    if stages == "attn":
                        done = True
                        P.flush()
                        break
                    group_norm(yaT, ona, l)
                    group_norm(qT, onb, l)
                    tap("yan_%d_%d" % (l, s), yaT[:], [128, 4, T], BF16)
                    tap("ybn_%d_%d" % (l, s), qT[:], [128, 4, T], BF16)
                    out_proj(l, s, yaT, qT)
                    tap("xmix_%d_%d" % (l, s), xT[:], [128, 8, T], F32)
                    P.flush()
                if stages == "mix":
                    done = True
                    break
            norm_mod(l, s, 1)
            tap("h2_%d_%d" % (l, s), hT[:], [128, 8, T], BF16)
            peer_wbuild(l, s)
            if "Wd_%d_%d" % (l, s) in dbg:
                o = nc.dram_tensor("dbg_Wd_%d_%d" % (l, s), [32, 128, 128, 64], BF16, kind="ExternalOutput").ap()
                P.dma("sp", o, Wd)
                P.flush()
            if stages in ("wbuild", "peeronly_wbuild"):
                done = True
                break
            peer_dense(l, s)
            tap("xl_%d_%d" % (l, s), xT[:], [128, 8, T], F32)
            if stages in ("layer0", "peeronly"):
                done = True
                break
        if done:
            break
        final_out(s)

    P.flush()
    return nc, P, dbg_out, list(dram_in.keys())


def host_inputs(inputs, core):
    f = np.float32
    b0 = 2 * core
    m = {}
    m["x"] = np.ascontiguousarray(inputs["x"][b0:b0 + 2])
    c = inputs["c"][b0:b0 + 2]
    m["cT"] = np.ascontiguousarray(c.reshape(2, 8, 128).transpose(2, 1, 0))
    m["ada_w"] = inputs["ada_w"]
    m["ada_bT"] = np.ascontiguousarray(inputs["ada_b"].reshape(L, 48, 128).transpose(2, 0, 1))
    m["n1gT"] = np.ascontiguousarray(inputs["norm1_g"].reshape(L, 8, 128).transpose(2, 0, 1))
    m["n2gT"] = np.ascontiguousarray(inputs["norm2_g"].reshape(L, 8, 128).transpose(2, 0, 1))
    m["w_in"] = inputs["w_in"]
    m["sgu_wT"] = np.ascontiguousarray(inputs["sgu_w"].transpose(3, 0, 1, 2))
    sb_ = inputs["sgu_b"]
    m["sgu_bB"] = np.ascontiguousarray(np.repeat(sb_.reshape(L, 4, 2, 1, 128), 64, axis=3)
                                       .reshape(L, 4, 128, 128).transpose(2, 0, 1, 3))
    m["onaT"] = np.ascontiguousarray(inputs["out_norm_a"].reshape(L, 4, 128).transpose(2, 0, 1))
    m["onbT"] = np.ascontiguousarray(inputs["out_norm_b"].reshape(L, 4, 128).transpose(2, 0, 1))
    m["w_out"] = inputs["w_out"]
    m["peer_wq"] = inputs["peer_wq"]
    m["k1T"] = np.ascontiguousarray(inputs["peer_k1"].transpose(2, 0, 1))
    m["k2T"] = np.ascontiguousarray(inputs["peer_k2"].transpose(2, 0, 1))
    m["peer_uT"] = inputs["_peer_uT"]
    m["peer_v"] = inputs["peer_v"]
    m["fgT"] = np.ascontiguousarray(inputs["final_g"].reshape(8, 128).T)
    m["consts"] = inputs["_consts"]
    return {k: np.asarray(v, dtype=f) for k, v in m.items()}


def kernel(**inputs):
    inputs = {k: np.asarray(v) for k, v in inputs.items()}
    inputs["_peer_uT"] = np.ascontiguousarray(inputs["peer_u"].transpose(0, 2, 1))
    inputs["_consts"] = _consts()
    nc, P, _, used = build()
    in_maps = [{k: v for k, v in host_inputs(inputs, c).items() if k in used} for c in range(NCORES)]
    res = run_bass_kernel_spmd(nc, in_maps, core_ids=list(range(NCORES)))
    out = np.concatenate([r["out"] for r in res.results], axis=0)
    return out.astype(np.float32)
```

```python
import numpy as np
from contextlib import ExitStack
import concourse.bass as bass
import concourse.mybir as mybir
from concourse.bass_utils import run_bass_kernel_spmd

F32 = mybir.dt.float32
BF16 = mybir.dt.bfloat16
U32 = mybir.dt.uint32
ALU = mybir.AluOpType
AF = mybir.ActivationFunctionType
AX = mybir.AxisListType

NCORES = 8
T = 2048
D = 1024
L = 2
EPS = 1e-6
_ESZ = {F32: 4, BF16: 2, U32: 4}


def _esz(dt):
    return _ESZ.get(dt, 4)


class _Op:
    __slots__ = ("eng", "fn", "acc", "dma", "deps", "ms", "semval", "dsem", "dval", "prevd")


class Prog:
    ENGS = ("pe", "act", "dve", "pool", "sp")

    def __init__(self, nc, es):
        self.nc = nc
        self.e = {"pe": nc.tensor, "act": nc.scalar, "dve": nc.vector, "pool": nc.gpsimd, "sp": nc.sync}
        self.sem = {k: es.enter_context(nc.semaphore("pg_" + k)) for k in self.ENGS}
        self.cnt = {k: 0 for k in self.ENGS}
        self.NDS = 8
        self.dsems = {q: [es.enter_context(nc.semaphore("dq_%s%d" % (q, i))) for i in range(self.NDS)]
                      for q in ("sp", "act", "pool")}
        self.dcnt = {q: 0 for q in self.dsems}
        self.dlast = {q: [None] * self.NDS for q in self.dsems}
        self.waited = {k: {} for k in self.ENGS}
        self.pending = []
        self.hist = {}
        self.nins = 0

    @staticmethod
    def regions(ap):
        name = ap.tensor.name
        esz = _esz(ap.dtype)
        apl = ap.ap
        off = ap.offset
        sp = str(ap.space)
        if "SB" not in sp and "PSUM" not in sp:
            ext = sum((c - 1) * abs(s) for s, c in apl) + 1
            return [(name, 0, 1, off * esz, (off + ext) * esz)]
        pstep, pcnt = apl[0]
        if pstep > 0:
            p0 = off // pstep
            f0 = off % pstep
        else:
            p0, f0 = 0, off
        p1 = p0 + pcnt
        dims = [(abs(s), c) for s, c in apl[1:] if c > 1]
        if not dims:
            return [(name, p0, p1, f0 * esz, (f0 + 1) * esz)]
        dims.sort(key=lambda sc: -sc[0])
        s0, c0 = dims[0]
        inner = sum((c - 1) * s for s, c in dims[1:]) + 1
        if len(dims) > 1 and s0 >= inner and c0 <= 32:
            return [(name, p0, p1, (f0 + i * s0) * esz, (f0 + i * s0 + inner) * esz) for i in range(c0)]
        ext = sum((c - 1) * s for s, c in dims) + 1
        return [(name, p0, p1, f0 * esz, (f0 + ext) * esz)]

    def op(self, eng, fn, reads=(), writes=(), dma=False):
        o = _Op()
        o.eng = eng
        o.fn = fn
        o.dma = dma
        acc = []
        for ap in reads:
            if ap is None or isinstance(ap, (int, float)):
                continue
            for r in self.regions(ap):
                acc.append((r, False))
        for ap in writes:
            for r in self.regions(ap):
                acc.append((r, True))
        o.acc = acc
        o.deps = []
        o.ms = False
        o.semval = None
        o.dsem = None
        o.dval = None
        o.prevd = None
        self.pending.append(o)
        return o

    def flush(self, barrier=True):
        hist = self.hist
        pend = self.pending
        for o in pend:
            deps = set()
            for (name, p0, p1, lo, hi), w in o.acc:
                lst = hist.get(name)
                if not lst:
                    continue
                for rec in lst:
                    if rec[0] < p1 and p0 < rec[1] and rec[2] < hi and lo < rec[3]:
                        if w or rec[4]:
                            deps.add(rec[5])
            deps.discard(o)
            for d in deps:
                if (not d.dma) and d.eng == o.eng and o.eng == "pe" and not o.dma:
                    continue
                o.deps.append(d)
                if not d.dma:
                    d.ms = True
            if o.dma:
                q = o.eng
                k = self.dcnt[q]
                self.dcnt[q] += 1
                slot = k % self.NDS
                o.dsem = self.dsems[q][slot]
                o.dval = 16 * (k // self.NDS + 1)
                o.prevd = self.dlast[q][slot]
                self.dlast[q][slot] = o
            for (name, p0, p1, lo, hi), w in o.acc:
                lst = hist.setdefault(name, [])
                if w:
                    lst[:] = [r for r in lst if not (p0 <= r[0] and r[1] <= p1 and lo <= r[2] and r[3] <= hi)]
                    lst.append((p0, p1, lo, hi, True, o))
                else:
                    found = False
                    if not o.dma:
                        for i, r in enumerate(lst):
                            if (not r[4]) and r[0] == p0 and r[1] == p1 and r[2] == lo and r[3] == hi \
                                    and (not r[5].dma) and r[5].eng == o.eng:
                                lst[i] = (p0, p1, lo, hi, False, o)
                                found = True
                                break
                    if not found:
                        lst.append((p0, p1, lo, hi, False, o))
        if barrier:
            last = {}
            for o in pend:
                if not o.dma:
                    last[o.eng] = o
            for o in last.values():
                o.ms = True
        else:
            for lst in hist.values():
                for r in lst:
                    if (not r[5].dma) and r[5].semval is None:
                        r[5].ms = True
        for o in pend:
            eng = self.e[o.eng]
            wd = self.waited[o.eng]
            need = {}
            if o.dma and o.prevd is not None:
                need[id(o.prevd.dsem)] = (o.prevd.dsem, o.prevd.dval)
            for d in o.deps:
                if d.dma:
                    s, v = d.dsem, d.dval
                else:
                    s, v = self.sem[d.eng], d.semval
                cur = need.get(id(s))
                if cur is None or cur[1] < v:
                    need[id(s)] = (s, v)
            for sid, (s, v) in need.items():
                if wd.get(sid, 0) < v:
                    eng.wait_ge(s, v)
                    wd[sid] = v
                    self.nins += 1
            ins = o.fn()
            self.nins += 1
            if o.dma:
                ins.then_inc(o.dsem, 16)
            elif o.ms:
                self.cnt[o.eng] += 1
                o.semval = self.cnt[o.eng]
                ins.then_inc(self.sem[o.eng], 1)
            o.fn = None
            o.acc = None
        self.pending = []
        if barrier:
            for k in self.ENGS:
                eng = self.e[k]
                wd = self.waited[k]
                for k2 in self.ENGS:
                    if k2 == k:
                        continue
                    s = self.sem[k2]
                    v = self.cnt[k2]
                    if wd.get(id(s), 0) < v:
                        eng.wait_ge(s, v)
                        wd[id(s)] = v
                        self.nins += 1
                for q in self.dsems:
                    for slot in range(self.NDS):
                        d = self.dlast[q][slot]
                        if d is None:
                            continue
                        if wd.get(id(d.dsem), 0) < d.dval:
                            eng.wait_ge(d.dsem, d.dval)
                            wd[id(d.dsem)] = d.dval
                            self.nins += 1
            self.hist = {}

    def mm(self, out, lhsT, rhs, start=True, stop=True):
        nc = self.nc
        rd = [lhsT, rhs] + ([] if start else [out])
        return self.op("pe", lambda: nc.tensor.matmul(out, lhsT, rhs, start=start, stop=stop), rd, [out])

    def tr(self, out, in_, ident):
        nc = self.nc
        return self.op("pe", lambda: nc.tensor.transpose(out, in_, ident), [in_, ident], [out])

    def act(self, out, in_, func, bias=None, scale=None):
        nc = self.nc
        kw = {}
        if bias is not None:
            kw["bias"] = bias
        if scale is not None:
            kw["scale"] = scale
        rd = [in_] + [a for a in (bias, scale) if a is not None and not isinstance(a, (int, float))]
        return self.op("act", lambda: nc.scalar.activation(out=out, in_=in_, func=func, **kw), rd, [out])

    def tt(self, eng, out, in0, in1, op):
        e = self.e[eng]
        return self.op(eng, lambda: e.tensor_tensor(out=out, in0=in0, in1=in1, op=op), [in0, in1], [out])

    def ts(self, eng, out, in0, s1, s2, op0, op1=None):
        e = self.e[eng]
        rd = [in0] + [a for a in (s1, s2) if a is not None and not isinstance(a, (int, float))]
        if op1 is None:
            return self.op(eng, lambda: e.tensor_scalar(out=out, in0=in0, scalar1=s1, scalar2=None, op0=op0), rd, [out])
        return self.op(eng, lambda: e.tensor_scalar(out=out, in0=in0, scalar1=s1, scalar2=s2, op0=op0, op1=op1), rd, [out])

    def stt(self, eng, out, in0, scalar, in1, op0, op1):
        e = self.e[eng]
        rd = [in0, in1] + ([] if isinstance(scalar, (int, float)) else [scalar])
        return self.op(eng, lambda: e.scalar_tensor_tensor(out=out, in0=in0, scalar=scalar, in1=in1, op0=op0, op1=op1),
                       rd, [out])

    def copy(self, eng, out, in_):
        if eng == "act":
            return self.act(out, in_, AF.Copy)
        e = self.e[eng]
        return self.op(eng, lambda: e.tensor_copy(out=out, in_=in_), [in_], [out])

    def memset(self, eng, ap, val):
        e = self.e[eng]
        return self.op(eng, lambda: e.memset(ap, val), [], [ap])

    def reduce(self, eng, out, in_, op):
        e = self.e[eng]
        return self.op(eng, lambda: e.tensor_reduce(out=out, in_=in_, axis=AX.X, op=op), [in_], [out])

    def recip(self, out, in_):
        nc = self.nc
        return self.op("dve", lambda: nc.vector.reciprocal(out=out, in_=in_), [in_], [out])

    def vmax(self, out, in_):
        nc = self.nc
        return self.op("dve", lambda: nc.vector.max(out=out, in_=in_), [in_], [out])

    def vmax_index(self, out, in_max, in_values):
        nc = self.nc
        return self.op("dve", lambda: nc.vector.max_index(out=out, in_max=in_max, in_values=in_values),
                       [in_max, in_values], [out])

    def vmatch_replace(self, out, in_to_replace, in_values, imm):
        nc = self.nc
        return self.op("dve", lambda: nc.vector.match_replace(out=out, in_to_replace=in_to_replace,
                                                              in_values=in_values, imm_value=imm),
                       [in_to_replace, in_values], [out])

    def dma(self, q, out, in_):
        e = self.e[q]
        return self.op(q, lambda: e.dma_start(out=out, in_=in_), [in_], [out], dma=True)


C_ID = 0
C_TRI = 128
C_ONE = 256
C_BD = 384
C_SGM = 512
C_IOTA = 640
C_BM = 768
C_EPS = 776
C_NH = 784
C_AM = 792
NCST = C_AM + 4 * 512


def _consts():
    c = np.zeros((128, NCST), np.float32)
    i = np.arange(128)
    c[:, C_ID:C_ID + 128] = np.eye(128)
    c[:, C_TRI:C_TRI + 128] = (i[:, None] >= i[None, :])
    c[:, C_ONE:C_ONE + 128] = 1.0
    c[:, C_BD:C_BD + 128] = (i[:, None] // 64 == i[None, :] // 64)
    c[:, C_SGM:C_SGM + 128] = (i[:, None] <= i[None, :])
    c[:, C_IOTA:C_IOTA + 128] = i[None, :]
    c[:, C_BM:C_BM + 8] = (i[:, None] // 16 == np.arange(8)[None, :])
    c[:, C_EPS] = EPS
    c[:, C_NH:C_NH + 8] = -0.5
    t = np.arange(512)
    for r in range(4):
        c[:, C_AM + r * 512:C_AM + (r + 1) * 512] = (t[None, :] > r * 128 + i[:, None])
    return c


NWARM = 3
ADEP = 1
NWARM2 = 0


def build(dbg=(), stages="all", nseq=2):
    nc = bass.Bass("TRN2", target_bir_lowering=False)
    es = ExitStack()
    P = Prog(nc, es)
    dram_in = {}

    def din(name, shape, dt=F32):
        dram_in[name] = nc.dram_tensor(name, list(shape), dt, kind="ExternalInput").ap()
        return dram_in[name]

    _shapes = {"x": [2, T, D], "cT": [128, 8, 2], "ada_w": [L, D, 6 * D], "ada_bT": [128, L, 48],
               "n1gT": [128, L, 8], "n2gT": [128, L, 8], "w_in": [L, D, 2560], "sgu_wT": [128, L, 8, 128],
               "sgu_bB": [128, L, 4, 128], "onaT": [128, L, 4], "onbT": [128, L, 4], "w_out": [L, D, D],
               "peer_wq": [L, D, D], "k1T": [64, L, 128], "k2T": [64, L, 128], "peer_uT": [L, D, 16384],
               "peer_v": [L, 16384, D], "fgT": [128, 8], "consts": [128, NCST]}

    class _DI:
        def __getattr__(self, name):
            if name not in dram_in:
                din(name, _shapes[name])
            return dram_in[name]
    DI = _DI()
    out_d = nc.dram_tensor("out", [2, T, D], F32, kind="ExternalOutput").ap()
    dbg_out = {}

    _uid = [0]

    def sb(name, shape, dt=F32, stack=es):
        _uid[0] += 1
        return stack.enter_context(nc.sbuf_tensor("s%d_%s" % (_uid[0], name), list(shape), dt))

    def tap(name, ap, shape, dt):
        if name in dbg:
            o = nc.dram_tensor("dbg_" + name, list(shape), dt, kind="ExternalOutput").ap()
            dbg_out[name] = o
            P.dma("sp", o, ap)

    ps = es.enter_context(nc.psum_tensor("ps", [128, 8, 512], F32))
    psb = ps[:].bitcast(BF16)
    bank_i = [0]

    def nb():
        b = bank_i[0]
        bank_i[0] = (b + 1) % 8
        return b

    cst = sb("cst", [128, NCST])
    cb = sb("cb", [128, 784], BF16)
    xT = sb("xT", [128, 8, T])
    hT = sb("hT", [128, 8, T], BF16)
    modT = sb("modT", [128, L, 6, 8, 2])
    gsc = sb("gsc", [128, L, 2, 2, 8])
    small = sb("small", [128, 64])
    P.dma("sp", cst[:], DI.consts[:, :])
    ident = cst[:, C_ID:C_ID + 128]
    ones_f = cst[:, C_ONE:C_ONE + 128]
    eps_c = cst[:, C_EPS:C_EPS + 1]
    P.copy("dve", cb[:, 0:784], cst[:, 0:784])
    ident_b = cb[:, 0:128]
    tri_b = cb[:, 128:256]
    ones_b = cb[:, 256:384]
    bd_b = cb[:, 384:512]
    iota_b = cb[:, C_IOTA:C_IOTA + 128]
    bm_b = cb[:, C_BM:C_BM + 8]

    with ExitStack() as st:
        cT = sb("cT", [128, 8, 2], F32, st)
        cact = sb("cact", [128, 8, 2], F32, st)
        adab = sb("adab", [128, L, 48], F32, st)
        n1g = sb("n1g", [128, L, 8], F32, st)
        n2g = sb("n2g", [128, L, 8], F32, st)
        wada = [sb("wada%d" % i, [128, 8, 1024], F32, st) for i in range(2)]
        P.dma("sp", cT[:], DI.cT[:, :, :])
        P.dma("sp", adab[:], DI.ada_bT[:, :, :])
        P.dma("sp", n1g[:], DI.n1gT[:, :, :])
        P.dma("sp", n2g[:], DI.n2gT[:, :, :])
        P.act(cact[:], cT[:], AF.Silu)
        it = 0
        for l in range(L):
            for js in range(6):
                w = wada[it % 2]
                it += 1
                for k in range(8):
                    P.dma("sp", w[:, k, :], DI.ada_w[l, k * 128:(k + 1) * 128, js * 1024:(js + 1) * 1024])
                for ec in range(8):
                    b = nb()
                    for k in range(8):
                        P.mm(ps[:, b, 0:2], w[:, k, ec * 128:(ec + 1) * 128], cact[:, k, :], start=(k == 0), stop=(k == 7))
                    P.ts("dve", modT[:, l, js, ec, :], ps[:, b, 0:2], adab[:, l, js * 8 + ec:js * 8 + ec + 1], None, ALU.add)
        for l in range(L):
            for s in range(2):
                for w, (j, g) in enumerate(((1, n1g), (4, n2g))):
                    P.ts("dve", small[:, 0:8], modT[:, l, j, :, s], 1.0, None, ALU.add)
                    P.tt("dve", gsc[:, l, w, s, :], small[:, 0:8], g[:, l, :], ALU.mult)
        tap("modT", modT[:], [128, L, 6, 8, 2], F32)
        P.flush()

    def load_x(s):
        with ExitStack() as st:
            xin = [sb("xin%d" % i, [128, D], F32, st) for i in range(2)]
            for tb in range(16):
                xi = xin[tb % 2]
                P.dma("sp", xi[:], DI.x[s, tb * 128:(tb + 1) * 128, :])
                for kh in range(2):
                    b = nb()
                    for kk in range(4):
                        k = kh * 4 + kk
                        P.tr(ps[:, b, kk * 128:(kk + 1) * 128], xi[:, k * 128:(k + 1) * 128], ident)
                    P.copy("act" if kh == 0 else "dve", xT[:, kh * 4:(kh + 1) * 4, tb * 128:(tb + 1) * 128],
                           ps[:, b, :].rearrange("p (a b) -> p a b", a=4))
            P.flush()

    def rms_bcast(st, tq, tagsrc):
        sq = [sb("sq%d" % i, [128, 512], F32, st) for i in range(2)]
        rs = sb("rs", [128, 512], F32, st)
        return sq, rs

    def norm_mod(l, s, w):
        jsh = 0 if w == 0 else 3
        with ExitStack() as st:
            sq = [sb("sq%d" % i, [128, 512], F32, st) for i in range(2)]
            rs = [sb("rs%d" % i, [128, 512], F32, st) for i in range(2)]
            tmp = [sb("nt%d" % i, [128, 512], F32, st) for i in range(2)]
            for tq in range(4):
                tsl = slice(tq * 512, (tq + 1) * 512)
                b = nb()
                for k in range(8):
                    q = sq[k % 2]
                    P.tt("pool", q[:], xT[:, k, tsl], xT[:, k, tsl], ALU.mult)
                    P.mm(ps[:, b, :], ones_f, q[:], start=(k == 0), stop=(k == 7))
                r = rs[tq % 2]
                P.act(r[:], ps[:, b, :], AF.Sqrt, bias=eps_c, scale=1.0 / D)
                P.recip(r[:], r[:])
                for k in range(8):
                    tm = tmp[k % 2]
                    P.tt("dve", tm[:], xT[:, k, tsl], r[:], ALU.mult)
                    P.act(hT[:, k, tsl], tm[:], AF.Identity, bias=modT[:, l, jsh, k, s:s + 1],
                          scale=gsc[:, l, w, s, k:k + 1])
            P.flush()

    def bcast(ap, dims):
        return bass.AP(ap.tensor, ap.offset, [list(ap.ap[0])] + [list(d) for d in dims])

    one_c = cst[:, C_ONE:C_ONE + 1]

    def load_wslice(dst, src2d, c0, ncols):
        for k in range(8):
            P.dma("pool", dst[:, k, :], src2d[k * 128:(k + 1) * 128, c0:c0 + ncols])

    def proj_fm(w, dstfn, evac):
        for cc in range(4):
            for tq in range(4):
                tsl = slice(tq * 512, (tq + 1) * 512)
                b = nb()
                for k in range(8):
                    P.mm(ps[:, b, :], w[:, k, cc * 128:(cc + 1) * 128], hT[:, k, tsl], start=(k == 0), stop=(k == 7))
                evac(dstfn(cc, tsl), ps[:, b, :], cc * 4 + tq)

    def group_norm(yT, gain, l):
        with ExitStack() as st:
            sqb = [sb("gq%d" % i, [128, 512], BF16, st) for i in range(2)]
            rs = [sb("gr%d" % i, [128, 512], F32, st) for i in range(2)]
            tf = [sb("gt%d" % i, [128, 512], F32, st) for i in range(2)]
            i = 0
            for cc in range(4):
                for tq in range(4):
                    tsl = slice(tq * 512, (tq + 1) * 512)
                    q, r, t_ = sqb[i % 2], rs[i % 2], tf[i % 2]
                    i += 1
                    P.tt("pool", q[:], yT[:, cc, tsl], yT[:, cc, tsl], ALU.mult)
                    b = nb()
                    P.mm(ps[:, b, :], bd_b, q[:])
                    P.act(r[:], ps[:, b, :], AF.Sqrt, bias=eps_c, scale=1.0 / 64)
                    P.recip(r[:], r[:])
                    P.tt("dve", t_[:], yT[:, cc, tsl], r[:], ALU.mult)
                    P.act(yT[:, cc, tsl], t_[:], AF.Identity, scale=gain[:, l, cc:cc + 1])
            P.flush()

    def gmlp(l, s, yaT):
        with ExitStack() as st:
            wu = sb("wu", [128, 8, 512], BF16, st)
            wv = sb("wv", [128, 8, 512], BF16, st)
            sw = sb("sw", [128, 8, 128], F32, st)
            swb = sb("swb", [128, 8, 128], BF16, st)
            bsB = sb("bsB", [128, 4, 128], F32, st)
            st8 = [sb("st8%d" % i, [128, 40], F32, st) for i in range(2)]
            vv = [sb("vv%d" % i, [128, 512], F32, st) for i in range(3)]
            cen = [sb("cen%d" % i, [128, 512], F32, st) for i in range(2)]
            sqv = [sb("sqv%d" % i, [128, 512], F32, st) for i in range(2)]
            vn = [sb("vn%d" % i, [128, 512], BF16, st) for i in range(3)]
            tmp = [sb("gtmp%d" % i, [128, 128], F32, st) for i in range(4)]
            load_wslice(wu, DI.w_in[l], 0, 512)
            load_wslice(wv, DI.w_in[l], 512, 512)
            P.dma("sp", sw[:], DI.sgu_wT[:, l, :, :])
            P.dma("sp", bsB[:], DI.sgu_bB[:, l, :, :])
            for g in range(8):
                P.tt("dve", swb[:, g, :], sw[:, g, :], cst[:, C_SGM:C_SGM + 128], ALU.mult)
            proj_fm(wu, lambda cc, tsl: yaT[:, cc, tsl], lambda o, p_, i: P.act(o, p_, AF.Gelu_apprx_tanh))
            nhalf = cst[:, C_NH:C_NH + 8]
            it = [0]

            def stA(n):
                nsl = slice(n * 128, (n + 1) * 128)
                b = n % 2
                for k in range(8):
                    P.mm(ps[:, b, :], hT[:, k, nsl], wv[:, k, :], start=(k == 0), stop=(k == 7))
                P.act(vv[n % 3][:], ps[:, b, :], AF.Gelu_apprx_tanh)

            def stB(n):
                v, ce_, sq_, s8, vb_ = vv[n % 3], cen[n % 2], sqv[n % 2], st8[n % 2], vn[n % 3]
                v3 = v[:].rearrange("p (g d) -> p g d", g=8)
                cen3 = ce_[:].rearrange("p (g d) -> p g d", g=8)
                sq3 = sq_[:].rearrange("p (g d) -> p g d", g=8)
                P.reduce("dve", s8[:, 0:8], v3, ALU.add)
                P.ts("dve", s8[:, 8:16], s8[:, 0:8], -1.0 / 64, None, ALU.mult)
                P.tt("dve", cen3, v3, bcast(s8[:, 8:16], [[1, 8], [0, 64]]), ALU.add)
                P.tt("pool", sq_[:], ce_[:], ce_[:], ALU.mult)
                P.reduce("dve", s8[:, 16:24], sq3, ALU.add)
                P.ts("dve", s8[:, 24:32], s8[:, 16:24], 1.0 / 64, EPS, ALU.mult, ALU.add)
                P.tt("pool", s8[:, 32:40], s8[:, 24:32], nhalf, ALU.pow)
                P.tt("dve", vb_[:].rearrange("p (g d) -> p g d", g=8), cen3, bcast(s8[:, 32:40], [[1, 8], [0, 64]]), ALU.mult)

            def stC(n):
                nsl = slice(n * 128, (n + 1) * 128)
                vb_ = vn[n % 3]
                for cc in range(4):
                    for half in range(2):
                        g = 2 * cc + half
                        rows = slice(half * 64, half * 64 + 64)
                        b2 = 2 + it[0] % 6
                        P.mm(ps[:, b2, 0:128], vb_[:, cc * 128:(cc + 1) * 128], swb[:, g, :])
                        tm = tmp[it[0] % 4]
                        it[0] += 1
                        P.tt("dve", tm[rows, :], ps[rows, b2, 0:128], bsB[rows, cc, :], ALU.add)
                        P.tt("pool", yaT[rows, cc, nsl], tm[rows, :], yaT[rows, cc, nsl], ALU.mult)

            for n in range(16 + 2):
                if n < 16:
                    stA(n)
                if 1 <= n < 17:
                    stB(n - 1)
                if n >= 2:
                    stC(n - 2)
            P.flush()

    def attention(l, s, qT):
        with ExitStack() as st:
            kTn = sb("kTn", [128, 4, T], BF16, st)
            vtok = sb("vtok", [128, 16, 512], BF16, st)
            with ExitStack() as st2:
                ws = [sb("wqk%d" % i, [128, 8, 512], BF16, st2) for i in range(2)]
                load_wslice(ws[0], DI.w_in[l], 1024, 512)
                load_wslice(ws[1], DI.w_in[l], 1536, 512)
                proj_fm(ws[0], lambda cc, tsl: qT[:, cc, tsl],
                        lambda o, p_, i: P.copy("act" if i % 2 else "dve", o, p_))
                proj_fm(ws[1], lambda cc, tsl: kTn[:, cc, tsl],
                        lambda o, p_, i: (P.act(o, p_, AF.Copy, scale=-0.125) if i % 2 else
                                          P.ts("dve", o, p_, -0.125, None, ALU.mult)))
                P.flush(barrier=False)
                load_wslice(ws[0], DI.w_in[l], 2048, 512)
                for n in range(16):
                    b = nb()
                    for k in range(8):
                        P.mm(ps[:, b, :], hT[:, k, n * 128:(n + 1) * 128], ws[0][:, k, :], start=(k == 0), stop=(k == 7))
                    P.copy("act" if n % 2 else "dve", vtok[:, n, :], ps[:, b, :])
                P.flush()
            tap("qT_%d_%d" % (l, s), qT[:], [128, 4, T], BF16)
            tap("kTn_%d_%d" % (l, s), kTn[:], [128, 4, T], BF16)
            tap("vtok_%d_%d" % (l, s), vtok[:], [128, 16, 512], BF16)
            with ExitStack() as st2:
                NB_ = 4
                E = [sb("aE%d" % i, [128, 512], F32, st2) for i in range(NB_)]
                Lf = [sb("aLf%d" % i, [128, 512], F32, st2) for i in range(2)]
                Lb = [sb("aLb%d" % i, [128, 512], BF16, st2) for i in range(NB_)]
                SL = sb("aSL", [128, 512], F32, st2)
                SLb = [sb("aSLb%d" % i, [128, 512], BF16, st2) for i in range(NB_)]
                af = [sb("aaf%d" % i, [128, 512], F32, st2) for i in range(2)]
                aa = [sb("aa%d" % i, [128, 512], BF16, st2) for i in range(NB_)]
                blocks = []
                gi = 0
                for h in range(8):
                    for qb in range(4):
                        nkb = 4 * (qb + 1)
                        for kb in reversed(range(nkb)):
                            blocks.append((h, qb, kb, kb == nkb - 1, kb == 0, 5 + gi % 2))
                        gi += 1
                nd = [0]

                def S1(i):
                    h, qb, kb, first, last, bo = blocks[i]
                    cc = h // 2
                    rows = slice((h % 2) * 64, (h % 2) * 64 + 64)
                    tsl = slice(qb * 512, (qb + 1) * 512)
                    ksl = slice(kb * 128, (kb + 1) * 128)
                    r = kb - 4 * qb
                    bz = i % 5
                    e_, lb_ = E[i % NB_], Lb[i % NB_]
                    P.mm(ps[:, bz, :], kTn[rows, cc, ksl], qT[rows, cc, tsl], start=True, stop=False)
                    for _ in range(NWARM):
                        P.mm(ps[:, 7, :], ones_b, cb[:, 0:512])
                    P.act(e_[:], ps[:, bz, :], AF.Exp, scale=-1.0)
                    if r >= 0:
                        am = cst[:, C_AM + r * 512:C_AM + (r + 1) * 512]
                        lf_ = Lf[nd[0] % 2]
                        nd[0] += 1
                        P.act(lf_[:], e_[:], AF.Ln, bias=one_c)
                        P.tt("pool", lb_[:], lf_[:], am, ALU.mult)
                    else:
                        P.act(lb_[:], e_[:], AF.Ln, bias=one_c)
                    if not last:
                        if first:
                            P.copy("dve", SL[:], lb_[:])
                            P.copy("dve", SLb[i % NB_][:], lb_[:])
                        else:
                            P.tt("dve", SL[:], SL[:], lb_[:], ALU.add)
                            P.copy("dve", SLb[i % NB_][:], SL[:])

                def S2(i):
                    h, qb, kb, first, last, bo = blocks[i]
                    cc = h // 2
                    rows = slice((h % 2) * 64, (h % 2) * 64 + 64)
                    tsl = slice(qb * 512, (qb + 1) * 512)
                    ksl = slice(kb * 128, (kb + 1) * 128)
                    r = kb - 4 * qb
                    bc_ = i % 5
                    lb_, a_ = Lb[i % NB_], aa[i % NB_]
                    P.mm(ps[:, bc_, :], tri_b, lb_[:], start=False, stop=first)
                    if not first:
                        P.mm(ps[:, bc_, :], ones_b, SLb[(i - 1) % NB_][:], start=False, stop=True)
                    if r >= 0:
                        am = cst[:, C_AM + r * 512:C_AM + (r + 1) * 512]
                        af_ = af[i % 2]
                        P.act(af_[:], ps[:, bc_, :], AF.Exp, scale=-1.0)
                        P.tt("pool", a_[:], af_[:], am, ALU.mult)
                    else:
                        P.act(a_[:], ps[:, bc_, :], AF.Exp, scale=-1.0)

                def S3(i):
                    h, qb, kb, first, last, bo = blocks[i]
                    cc = h // 2
                    rows = slice((h % 2) * 64, (h % 2) * 64 + 64)
                    tsl = slice(qb * 512, (qb + 1) * 512)
                    P.mm(ps[:, bo, :], vtok[:, kb, cc * 128:(cc + 1) * 128], aa[i % NB_][:], start=first, stop=last)
                    if last:
                        P.copy("dve", qT[rows, cc, tsl], ps[rows, bo, :])

                nblk = len(blocks)
                for n in range(nblk + ADEP + 1):
                    if n < nblk:
                        S1(n)
                    if 0 <= n - ADEP < nblk:
                        S2(n - ADEP)
                    if 0 <= n - ADEP - 1 < nblk:
                        S3(n - ADEP - 1)
                    if n % 16 == 15:
                        P.flush(barrier=False)
                P.flush()

    def out_proj(l, s, yaT, ybT):
        with ExitStack() as st:
            wo = sb("wo", [128, 8, D], BF16, st)
            load_wslice(wo, DI.w_out[l], 0, D)
            for dc in range(8):
                for tq in range(4):
                    tsl = slice(tq * 512, (tq + 1) * 512)
                    b = nb()
                    for c8 in range(8):
                        src = yaT[:, c8, tsl] if c8 < 4 else ybT[:, c8 - 4, tsl]
                        P.mm(ps[:, b, :], wo[:, c8, dc * 128:(dc + 1) * 128], src, start=(c8 == 0), stop=(c8 == 7))
                    P.stt("dve", xT[:, dc, tsl], ps[:, b, :], modT[:, l, 2, dc, s:s + 1], xT[:, dc, tsl], ALU.mult, ALU.add)
            P.flush()

    Wd = nc.dram_tensor("Wd_scratch", [32, 128, 128, 64], BF16, kind="Internal").ap()
    GA = 4
    NEG = 32

    def peer_wbuild(l, s):
        with ExitStack() as st:
            wq = sb("wq", [128, 8, D], BF16, st)
            kbd = sb("kbd", [128, 256], BF16, st)
            qp = sb("qp", [128, 8, 512], BF16, st)
            S = sb("pS", [128, 8, 256], F32, st)
            V12 = sb("pV12", [128, 8, 2, 16], F32, st)
            I12 = sb("pI12", [128, 8, 2, 16], U32, st)
            I12f = sb("pI12f", [128, 2, 8, 16], F32, st)
            wk = sb("pwk", [128, 16, 128], F32, st)
            cand = sb("pcand", [128, 4, 256], F32, st)
            c8 = sb("pc8", [128, 8, 16], F32, st)
            cI = sb("pcI", [128, 8, 16], U32, st)
            pf = sb("ppf", [128, 3, 8, 16], F32, st)
            cIj = sb("pcIj", [128, 2, 8, 16], U32, st)
            oh = sb("poh", [128, 4, 16, 16], F32, st)
            ge = sb("pge", [128, 8, 16], F32, st)
            zs = sb("pzs", [128, 16], F32, st)
            abg = sb("pabg", [128, 3, 8, 16], F32, st)
            abgTs = [sb("pabgT%d" % i, [128, 3, 128], BF16, st) for i in range(2)]
            NSUB = 8
            iom = sb("piom", [128, 128, NSUB], BF16, st)
            As = [sb("pA%d" % i, [128, 128, NSUB], BF16, st) for i in range(3)]
            Bs = [sb("pB%d" % i, [128, 128, NSUB], BF16, st) for i in range(3)]
            Wst = sb("pWst", [128, 128, 64], BF16, st)
            iota16 = cst[:, C_IOTA:C_IOTA + 16]
            load_wslice(wq, DI.peer_wq[l], 0, D)
            P.memset("dve", kbd[:], 0.0)
            P.dma("pool", kbd[0:64, 0:128], DI.k1T[:, l, :])
            P.dma("pool", kbd[64:128, 128:256], DI.k2T[:, l, :])
            P.copy("dve", iom[:], bcast(iota_b, [[1, 128], [0, NSUB]]))

            def qproj(tb4):
                tsl = slice(tb4 * 512, (tb4 + 1) * 512)
                for h in range(8):
                    b = nb()
                    for k in range(8):
                        P.mm(ps[:, b, :], wq[:, k, h * 128:(h + 1) * 128], hT[:, k, tsl], start=(k == 0), stop=(k == 7))
                    P.copy("act", qp[:, h, :], ps[:, b, :])

            def qs(tb):
                if tb % 4 == 0:
                    qproj(tb // 4)
                o = (tb % 4) * 128
                for h2 in range(4):
                    b = nb()
                    for h1 in range(2):
                        P.mm(ps[:, b, h1 * 256:(h1 + 1) * 256], qp[:, 2 * h2 + h1, o:o + 128], kbd[:])
                    P.copy("act", S[:, h2 * 2:h2 * 2 + 2, :], ps[:, b, :].rearrange("p (a c) -> p a c", a=2))

            def topk(tb):
                abgT = abgTs[tb % 2]
                grp = [(h, hf) for h in range(8) for hf in range(2)]
                for h, hf in grp:
                    P.vmax(V12[:, h, hf, 0:8], S[:, h, hf * 128:(hf + 1) * 128])
                    if hf: yield
                for h, hf in grp:
                    P.vmax_index(I12[:, h, hf, 0:8], V12[:, h, hf, 0:8], S[:, h, hf * 128:(hf + 1) * 128])
                    if hf and h % 2: yield
                for gi, (h, hf) in enumerate(grp):
                    P.vmatch_replace(wk[:, gi, :], V12[:, h, hf, 0:8], S[:, h, hf * 128:(hf + 1) * 128], -1e30)
                    if hf: yield
                for gi, (h, hf) in enumerate(grp):
                    P.vmax(V12[:, h, hf, 8:16], wk[:, gi, :])
                    if hf: yield
                for gi, (h, hf) in enumerate(grp):
                    P.vmax_index(I12[:, h, hf, 8:16], V12[:, h, hf, 8:16], wk[:, gi, :])
                    if hf and h % 2: yield
                for hf in range(2):
                    P.copy("pool", I12f[:, hf, :, :], I12[:, :, hf, :])
                wk2 = wk[:].rearrange("p (a c) d -> p a (c d)", c=2)
                for hh in range(2):
                    hs = slice(4 * hh, 4 * hh + 4)
                    v1 = V12[:, hs, 0, :]
                    v2 = V12[:, hs, 1, :]
                    cand4 = cand[:].rearrange("p h (i j) -> p h i j", i=16)
                    P.tt("dve", cand4, bcast(v1, [[32, 4], [1, 16], [0, 16]]), bcast(v2, [[32, 4], [0, 16], [1, 16]]), ALU.add)
                    yield
                    for hl in range(4):
                        P.vmax(c8[:, 4 * hh + hl, 0:8], cand[:, hl, :])
                    yield
                    for hl in range(4):
                        P.vmax_index(cI[:, 4 * hh + hl, 0:8], c8[:, 4 * hh + hl, 0:8], cand[:, hl, :])
                    for hl in range(4):
                        P.vmatch_replace(wk2[:, hl, :], c8[:, 4 * hh + hl, 0:8], cand[:, hl, :], -1e30)
                    yield
                    for hl in range(4):
                        P.vmax(c8[:, 4 * hh + hl, 8:16], wk2[:, hl, :])
                    for hl in range(4):
                        P.vmax_index(cI[:, 4 * hh + hl, 8:16], c8[:, 4 * hh + hl, 8:16], wk2[:, hl, :])
                    yield
                P.tt("dve", ge[:], c8[:], bcast(c8[:, :, 0:1], [[16, 8], [0, 16]]), ALU.subtract)
                P.act(ge[:], ge[:], AF.Exp)
                P.reduce("dve", zs[:, 0:8], ge[:], ALU.add)
                P.recip(zs[:, 8:16], zs[:, 0:8])
                P.tt("pool", abg[:, 2, :, :], ge[:], bcast(zs[:, 8:16], [[1, 8], [0, 16]]), ALU.mult)
                yield
                P.ts("dve", cIj[:, 0, :, :], cI[:], 15, None, ALU.bitwise_and)
                P.ts("dve", cIj[:, 1, :, :], cI[:], 4, None, ALU.logical_shift_right)
                P.copy("dve", pf[:, 1, :, :], cIj[:, 0, :, :])
                P.copy("dve", pf[:, 2, :, :], cIj[:, 1, :, :])
                yield
                for hh in range(2):
                    hs = slice(4 * hh, 4 * hh + 4)
                    for w_, (pi, hf) in enumerate(((2, 0), (1, 1))):
                        P.tt("dve", oh[:], bcast(iota16, [[0, 4], [0, 16], [1, 16]]),
                             bcast(pf[:, pi, hs, :], [[16, 4], [1, 16], [0, 16]]), ALU.is_equal)
                        P.tt("dve", oh[:], oh[:], bcast(I12f[:, hf, hs, :], [[16, 4], [0, 16], [1, 16]]), ALU.mult)
                        P.reduce("dve", abg[:, w_, hs, :], oh[:], ALU.add)
                        yield
                for w_ in range(3):
                    b = nb()
                    P.tr(ps[:, b, 0:128], abg[:, w_, :, :].rearrange("p h k -> p (h k)"), ident)
                    P.copy("act", abgT[:, w_, :], ps[:, b, 0:128])
                yield

            ctr = {"sub": 0, "ev": 0}

            def pertoken(tb, bg=None):
                abgT = abgTs[tb % 2]
                for half in range(2):
                    for sub in range(64 // NSUB):
                        t0 = half * 64 + sub * NSUB
                        sbi = ctr["sub"] % 3
                        ctr["sub"] += 1
                        A_, B_ = As[sbi], Bs[sbi]
                        P.tt("dve", A_[:], iom[:], bcast(abgT[:, 0, t0:t0 + NSUB], [[0, 128], [1, NSUB]]), ALU.is_equal)
                        P.tt("dve", B_[:], iom[:], bcast(abgT[:, 1, t0:t0 + NSUB], [[0, 128], [1, NSUB]]), ALU.is_equal)
                        P.tt("pool", B_[:], B_[:], bcast(abgT[:, 2, t0:t0 + NSUB], [[0, 128], [1, NSUB]]), ALU.mult)
                        for t4 in range(NSUB // 4):
                            bW = ctr["ev"] % 8
                            ctr["ev"] += 1
                            pw = ps[:, bW, :]
                            for tt_ in range(4):
                                tl = t4 * 4 + tt_
                                P.mm(bass.AP(pw.tensor, pw.offset + tt_, [list(pw.ap[0]), [4, 128]]), B_[:, :, tl], A_[:, :, tl])
                            tw = sub * NSUB + t4 * 4
                            P.copy("act", bcast(Wst[:, 0, tw:tw + 4], [[64, 128], [1, 4]]),
                                   pw.rearrange("p (a t) -> p a t", t=4))
                        if bg is not None:
                            for _ in range(4):
                                if next(bg, "end") == "end":
                                    bg = None
                                    break
                    P.dma("sp", Wd[tb * 2 + half], Wst[:])
                    P.flush(barrier=False)
                if bg is not None:
                    for _ in bg:
                        pass

            qs(0)
            for _ in topk(0):
                pass
            for tb in range(16):
                if tb + 1 < 16:
                    qs(tb + 1)
                pertoken(tb, topk(tb + 1) if tb + 1 < 16 else None)
            P.flush()

    def peer_dense(l, s):
        with ExitStack() as st:
            UTs = [sb("dU%d" % i, [128, 8, GA * 128], BF16, st) for i in range(2)]
            Vs = [sb("dV%d" % i, [128, GA, D], BF16, st) for i in range(2)]
            Wt = [sb("dW%d" % i, [128, 8, GA, 64], BF16, st) for i in range(2)]
            actT = [sb("dA%d" % i, [128, 512], BF16, st) for i in range(2)]
            Zt = [sb("dZ%d" % i, [128, GA, 512], BF16, st) for i in range(2)]

            def load_tables(eg):
                a0 = eg * GA
                for k in range(8):
                    P.dma("pool", UTs[eg % 2][:, k, :], DI.peer_uT[l, k * 128:(k + 1) * 128, a0 * 128:(a0 + GA) * 128])
                for ga in range(GA):
                    P.dma("pool", Vs[eg % 2][:, ga, :], DI.peer_v[l, (a0 + ga) * 128:(a0 + ga + 1) * 128, :])

            steps = [(eg, tq) for eg in range(NEG) for tq in range(4)]
            ai = [0]
            yi = [0]

            def stageA(i):
                eg, tq = steps[i]
                tsl = slice(tq * 512, (tq + 1) * 512)
                a0 = eg * GA
                if tq == 1 and eg + 1 < NEG:
                    load_tables(eg + 1)
                w_ = Wt[i % 2]
                P.dma("sp", w_[:], Wd[tq * 8:(tq + 1) * 8, :, a0:a0 + GA, :].rearrange("k b a t -> b k a t"))
                for ga in range(GA):
                    b = ai[0] % 2
                    a_ = actT[ai[0] % 2]
                    ai[0] += 1
                    for k in range(8):
                        P.mm(ps[:, b, :], UTs[eg % 2][:, k, ga * 128:(ga + 1) * 128], hT[:, k, tsl], start=(k == 0), stop=(k == 7))
                    P.act(a_[:], ps[:, b, :], AF.Gelu_apprx_tanh)
                    P.tt("dve", Zt[i % 2][:, ga, :].rearrange("p (k t) -> p k t", k=8),
                         a_[:].rearrange("p (k t) -> p k t", k=8), w_[:, :, ga, :], ALU.mult)

            def stageY(i):
                eg, tq = steps[i]
                tsl = slice(tq * 512, (tq + 1) * 512)
                for dc in range(8):
                    b = 2 + yi[0] % 6
                    yi[0] += 1
                    for ga in range(GA):
                        P.mm(ps[:, b, :], Vs[eg % 2][:, ga, dc * 128:(dc + 1) * 128], Zt[i % 2][:, ga, :],
                             start=(ga == 0), stop=(ga == GA - 1))
                    P.stt("dve", xT[:, dc, tsl], ps[:, b, :], modT[:, l, 5, dc, s:s + 1], xT[:, dc, tsl], ALU.mult, ALU.add)

            load_tables(0)
            stageA(0)
            for i in range(len(steps)):
                if i + 1 < len(steps):
                    stageA(i + 1)
                stageY(i)
                if i % 4 == 3:
                    P.flush(barrier=False)
            P.flush()

    def final_out(s):
        with ExitStack() as st:
            fg = sb("fg", [128, 8], F32, st)
            sq = [sb("fsq%d" % i, [128, 512], F32, st) for i in range(2)]
            rs = [sb("frs%d" % i, [128, 512], F32, st) for i in range(2)]
            tmp = [sb("ftm%d" % i, [128, 512], F32, st) for i in range(2)]
            yk = sb("fyk", [128, 8, 512], F32, st)
            ot = [sb("fot%d" % i, [128, D], F32, st) for i in range(2)]
            P.dma("sp", fg[:], DI.fgT[:, :])
            io = 0
            for tq in range(4):
                tsl = slice(tq * 512, (tq + 1) * 512)
                b = nb()
                for k in range(8):
                    q = sq[k % 2]
                    P.tt("pool", q[:], xT[:, k, tsl], xT[:, k, tsl], ALU.mult)
                    P.mm(ps[:, b, :], ones_f, q[:], start=(k == 0), stop=(k == 7))
                r = rs[tq % 2]
                P.act(r[:], ps[:, b, :], AF.Sqrt, bias=eps_c, scale=1.0 / D)
                P.recip(r[:], r[:])
                for k in range(8):
                    tm = tmp[k % 2]
                    P.tt("dve", tm[:], xT[:, k, tsl], r[:], ALU.mult)
                    P.act(yk[:, k, :], tm[:], AF.Identity, scale=fg[:, k:k + 1])
                for t4 in range(4):
                    o = ot[io % 2]
                    io += 1
                    for kh in range(2):
                        b = nb()
                        for kk in range(4):
                            k = kh * 4 + kk
                            P.tr(ps[:, b, kk * 128:(kk + 1) * 128], yk[:, k, t4 * 128:(t4 + 1) * 128], ident)
                        P.copy("act" if kh else "dve", o[:, kh * 512:(kh + 1) * 512], ps[:, b, :])
                    P.dma("sp", out_d[s, tq * 512 + t4 * 128:tq * 512 + (t4 + 1) * 128, :], o[:])
            P.flush()

    ona = sb("ona", [128, L, 4])
    onb = sb("onb", [128, L, 4])
    P.dma("sp", ona[:], DI.onaT[:, :, :])
    P.dma("sp", onb[:], DI.onbT[:, :, :])

    done = False
    for s in range(nseq):
        load_x(s)
        tap("xT%d" % s, xT[:], [128, 8, T], F32)
        for l in range(L):
            if not stages.startswith("peeronly"):
                norm_mod(l, s, 0)
                tap("h1_%d_%d" % (l, s), hT[:], [128, 8, T], BF16)
                if stages == "norm1":
                    done = True
                    break
                with ExitStack() as stl:
                    yaT = sb("yaT", [128, 4, T], BF16, stl)
                    qT = sb("qT", [128, 4, T], BF16, stl)
                    gmlp(l, s, yaT)
                    tap("ya_%d_%d" % (l, s), yaT[:], [128, 4, T], BF16)
                    if stages == "gmlp":
                        done = True
                        P.flush()
                        break
                    attention(l, s, qT)
                    tap("yb_%d_%d" % (l, s), qT[:], [128, 4, T], BF16)
                    if stages == "attn":
                        done = True
                        P.flush()
                        break
                    group_norm(yaT, ona, l)
                    group_norm(qT, onb, l)
                    tap("yan_%d_%d" % (l, s), yaT[:], [128, 4, T], BF16)
                    tap("ybn_%d_%d" % (l, s), qT[:], [128, 4, T], BF16)
                    out_proj(l, s, yaT, qT)
                    tap("xmix_%d_%d" % (l, s), xT[:], [128, 8, T], F32)
                    P.flush()
                if stages == "mix":
                    done = True
                    break
            norm_mod(l, s, 1)
            tap("h2_%d_%d" % (l, s), hT[:], [128, 8, T], BF16)
            peer_wbuild(l, s)
            if "Wd_%d_%d" % (l, s) in dbg:
                o = nc.dram_tensor("dbg_Wd_%d_%d" % (l, s), [32, 128, 128, 64], BF16, kind="ExternalOutput").ap()
                P.dma("sp", o, Wd)
                P.flush()
            if stages in ("wbuild", "peeronly_wbuild"):
                done = True
                break
            peer_dense(l, s)
            tap("xl_%d_%d" % (l, s), xT[:], [128, 8, T], F32)
            if stages in ("layer0", "peeronly"):
                done = True
                break
        if done:
            break
        final_out(s)

    P.flush()
    return nc, P, dbg_out, list(dram_in.keys())


def host_inputs(inputs, core):
    f = np.float32
    b0 = 2 * core
    m = {}
    m["x"] = np.ascontiguousarray(inputs["x"][b0:b0 + 2])
    c = inputs["c"][b0:b0 + 2]
    m["cT"] = np.ascontiguousarray(c.reshape(2, 8, 128).transpose(2, 1, 0))
    m["ada_w"] = inputs["ada_w"]
    m["ada_bT"] = np.ascontiguousarray(inputs["ada_b"].reshape(L, 48, 128).transpose(2, 0, 1))
    m["n1gT"] = np.ascontiguousarray(inputs["norm1_g"].reshape(L, 8, 128).transpose(2, 0, 1))
    m["n2gT"] = np.ascontiguousarray(inputs["norm2_g"].reshape(L, 8, 128).transpose(2, 0, 1))
    m["w_in"] = inputs["w_in"]
    m["sgu_wT"] = np.ascontiguousarray(inputs["sgu_w"].transpose(3, 0, 1, 2))
    sb_ = inputs["sgu_b"]
    m["sgu_bB"] = np.ascontiguousarray(np.repeat(sb_.reshape(L, 4, 2, 1, 128), 64, axis=3)
                                       .reshape(L, 4, 128, 128).transpose(2, 0, 1, 3))
    m["onaT"] = np.ascontiguousarray(inputs["out_norm_a"].reshape(L, 4, 128).transpose(2, 0, 1))
    m["onbT"] = np.ascontiguousarray(inputs["out_norm_b"].reshape(L, 4, 128).transpose(2, 0, 1))
    m["w_out"] = inputs["w_out"]
    m["peer_wq"] = inputs["peer_wq"]
    m["k1T"] = np.ascontiguousarray(inputs["peer_k1"].transpose(2, 0, 1))
    m["k2T"] = np.ascontiguousarray(inputs["peer_k2"].transpose(2, 0, 1))
    m["peer_uT"] = inputs["_peer_uT"]
    m["peer_v"] = inputs["peer_v"]
    m["fgT"] = np.ascontiguousarray(inputs["final_g"].reshape(8, 128).T)
    m["consts"] = inputs["_consts"]
    return {k: np.asarray(v, dtype=f) for k, v in m.items()}


def kernel(**inputs):
    inputs = {k: np.asarray(v) for k, v in inputs.items()}
    inputs["_peer_uT"] = np.ascontiguousarray(inputs["peer_u"].transpose(0, 2, 1))
    inputs["_consts"] = _consts()
    nc, P, _, used = build()
    in_maps = [{k: v for k, v in host_inputs(inputs, c).items() if k in used} for c in range(NCORES)]
    res = run_bass_kernel_spmd(nc, in_maps, core_ids=list(range(NCORES)))
    out = np.concatenate([r["out"] for r in res.results], axis=0)
    return out.astype(np.float32)
```

```python
import numpy as np
from contextlib import ExitStack
import concourse.bass as bass
import concourse.mybir as mybir
from concourse.bass_utils import run_bass_kernel_spmd

F32 = mybir.dt.float32
BF16 = mybir.dt.bfloat16
U32 = mybir.dt.uint32
ALU = mybir.AluOpType
AF = mybir.ActivationFunctionType
AX = mybir.AxisListType

NCORES = 8
T = 2048
D = 1024
L = 2
EPS = 1e-6
_ESZ = {F32: 4, BF16: 2, U32: 4}


def _esz(dt):
    return _ESZ.get(dt, 4)


class _Op:
    __slots__ = ("eng", "fn", "acc", "dma", "deps", "ms", "semval", "dsem", "dval", "prevd")


class Prog:
    ENGS = ("pe", "act", "dve", "pool", "sp")

    def __init__(self, nc, es):
        self.nc = nc
        self.e = {"pe": nc.tensor, "act": nc.scalar, "dve": nc.vector, "pool": nc.gpsimd, "sp": nc.sync}
        self.sem = {k: es.enter_context(nc.semaphore("pg_" + k)) for k in self.ENGS}
        self.cnt = {k: 0 for k in self.ENGS}
        self.NDS = 8
        self.dsems = {q: [es.enter_context(nc.semaphore("dq_%s%d" % (q, i))) for i in range(self.NDS)]
                      for q in ("sp", "act", "pool")}
        self.dcnt = {q: 0 for q in self.dsems}
        self.dlast = {q: [None] * self.NDS for q in self.dsems}
        self.waited = {k: {} for k in self.ENGS}
        self.pending = []
        self.hist = {}
        self.nins = 0

    @staticmethod
    def regions(ap):
        name = ap.tensor.name
        esz = _esz(ap.dtype)
        apl = ap.ap
        off = ap.offset
        sp = str(ap.space)
        if "SB" not in sp and "PSUM" not in sp:
            ext = sum((c - 1) * abs(s) for s, c in apl) + 1
            return [(name, 0, 1, off * esz, (off + ext) * esz)]
        pstep, pcnt = apl[0]
        if pstep > 0:
            p0 = off // pstep
            f0 = off % pstep
        else:
            p0, f0 = 0, off
        p1 = p0 + pcnt
        dims = [(abs(s), c) for s, c in apl[1:] if c > 1]
        if not dims:
            return [(name, p0, p1, f0 * esz, (f0 + 1) * esz)]
        dims.sort(key=lambda sc: -sc[0])
        s0, c0 = dims[0]
        inner = sum((c - 1) * s for s, c in dims[1:]) + 1
        if len(dims) > 1 and s0 >= inner and c0 <= 32:
            return [(name, p0, p1, (f0 + i * s0) * esz, (f0 + i * s0 + inner) * esz) for i in range(c0)]
        ext = sum((c - 1) * s for s, c in dims) + 1
        return [(name, p0, p1, f0 * esz, (f0 + ext) * esz)]

    def op(self, eng, fn, reads=(), writes=(), dma=False):
        o = _Op()
        o.eng = eng
        o.fn = fn
        o.dma = dma
        acc = []
        for ap in reads:
            if ap is None or isinstance(ap, (int, float)):
                continue
            for r in self.regions(ap):
                acc.append((r, False))
        for ap in writes:
            for r in self.regions(ap):
                acc.append((r, True))
        o.acc = acc
        o.deps = []
        o.ms = False
        o.semval = None
        o.dsem = None
        o.dval = None
        o.prevd = None
        self.pending.append(o)
        return o

    def flush(self, barrier=True):
        hist = self.hist
        pend = self.pending
        for o in pend:
            deps = set()
            for (name, p0, p1, lo, hi), w in o.acc:
                lst = hist.get(name)
                if not lst:
                    continue
                for rec in lst:
                    if rec[0] < p1 and p0 < rec[1] and rec[2] < hi and lo < rec[3]:
                        if w or rec[4]:
                            deps.add(rec[5])
            deps.discard(o)
            for d in deps:
                if (not d.dma) and d.eng == o.eng and o.eng == "pe" and not o.dma:
                    continue
                o.deps.append(d)
                if not d.dma:
                    d.ms = True
            if o.dma:
                q = o.eng
                k = self.dcnt[q]
                self.dcnt[q] += 1
                slot = k % self.NDS
                o.dsem = self.dsems[q][slot]
                o.dval = 16 * (k // self.NDS + 1)
                o.prevd = self.dlast[q][slot]
                self.dlast[q][slot] = o
            for (name, p0, p1, lo, hi), w in o.acc:
                lst = hist.setdefault(name, [])
                if w:
                    lst[:] = [r for r in lst if not (p0 <= r[0] and r[1] <= p1 and lo <= r[2] and r[3] <= hi)]
                    lst.append((p0, p1, lo, hi, True, o))
                else:
                    found = False
                    if not o.dma:
                        for i, r in enumerate(lst):
                            if (not r[4]) and r[0] == p0 and r[1] == p1 and r[2] == lo and r[3] == hi \
                                    and (not r[5].dma) and r[5].eng == o.eng:
                                lst[i] = (p0, p1, lo, hi, False, o)
                                found = True
                                break
                    if not found:
                        lst.append((p0, p1, lo, hi, False, o))
        if barrier:
            last = {}
            for o in pend:
                if not o.dma:
                    last[o.eng] = o
            for o in last.values():
                o.ms = True
        else:
            for lst in hist.values():
                for r in lst:
                    if (not r[5].dma) and r[5].semval is None:
                        r[5].ms = True
        for o in pend:
            eng = self.e[o.eng]
            wd = self.waited[o.eng]
            need = {}
            if o.dma and o.prevd is not None:
                need[id(o.prevd.dsem)] = (o.prevd.dsem, o.prevd.dval)
            for d in o.deps:
                if d.dma:
                    s, v = d.dsem, d.dval
                else:
                    s, v = self.sem[d.eng], d.semval
                cur = need.get(id(s))
                if cur is None or cur[1] < v:
                    need[id(s)] = (s, v)
            for sid, (s, v) in need.items():
                if wd.get(sid, 0) < v:
                    eng.wait_ge(s, v)
                    wd[sid] = v
                    self.nins += 1
            ins = o.fn()
            self.nins += 1
            if o.dma:
                ins.then_inc(o.dsem, 16)
            elif o.ms:
                self.cnt[o.eng] += 1
                o.semval = self.cnt[o.eng]
                ins.then_inc(self.sem[o.eng], 1)
            o.fn = None
            o.acc = None
        self.pending = []
        if barrier:
            for k in self.ENGS:
                eng = self.e[k]
                wd = self.waited[k]
                for k2 in self.ENGS:
                    if k2 == k:
                        continue
                    s = self.sem[k2]
                    v = self.cnt[k2]
                    if wd.get(id(s), 0) < v:
                        eng.wait_ge(s, v)
                        wd[id(s)] = v
                        self.nins += 1
                for q in self.dsems:
                    for slot in range(self.NDS):
                        d = self.dlast[q][slot]
                        if d is None:
                            continue
                        if wd.get(id(d.dsem), 0) < d.dval:
                            eng.wait_ge(d.dsem, d.dval)
                            wd[id(d.dsem)] = d.dval
                            self.nins += 1
            self.hist = {}

    def mm(self, out, lhsT, rhs, start=True, stop=True):
        nc = self.nc
        rd = [lhsT, rhs] + ([] if start else [out])
        return self.op("pe", lambda: nc.tensor.matmul(out, lhsT, rhs, start=start, stop=stop), rd, [out])

    def tr(self, out, in_, ident):
        nc = self.nc
        return self.op("pe", lambda: nc.tensor.transpose(out, in_, ident), [in_, ident], [out])

    def act(self, out, in_, func, bias=None, scale=None):
        nc = self.nc
        kw = {}
        if bias is not None:
            kw["bias"] = bias
        if scale is not None:
            kw["scale"] = scale
        rd = [in_] + [a for a in (bias, scale) if a is not None and not isinstance(a, (int, float))]
        return self.op("act", lambda: nc.scalar.activation(out=out, in_=in_, func=func, **kw), rd, [out])

    def tt(self, eng, out, in0, in1, op):
        e = self.e[eng]
        return self.op(eng, lambda: e.tensor_tensor(out=out, in0=in0, in1=in1, op=op), [in0, in1], [out])

    def ts(self, eng, out, in0, s1, s2, op0, op1=None):
        e = self.e[eng]
        rd = [in0] + [a for a in (s1, s2) if a is not None and not isinstance(a, (int, float))]
        if op1 is None:
            return self.op(eng, lambda: e.tensor_scalar(out=out, in0=in0, scalar1=s1, scalar2=None, op0=op0), rd, [out])
        return self.op(eng, lambda: e.tensor_scalar(out=out, in0=in0, scalar1=s1, scalar2=s2, op0=op0, op1=op1), rd, [out])

    def stt(self, eng, out, in0, scalar, in1, op0, op1):
        e = self.e[eng]
        rd = [in0, in1] + ([] if isinstance(scalar, (int, float)) else [scalar])
        return self.op(eng, lambda: e.scalar_tensor_tensor(out=out, in0=in0, scalar=scalar, in1=in1, op0=op0, op1=op1),
                       rd, [out])

    def copy(self, eng, out, in_):
        if eng == "act":
            return self.act(out, in_, AF.Copy)
        e = self.e[eng]
        return self.op(eng, lambda: e.tensor_copy(out=out, in_=in_), [in_], [out])

    def memset(self, eng, ap, val):
        e = self.e[eng]
        return self.op(eng, lambda: e.memset(ap, val), [], [ap])

    def reduce(self, eng, out, in_, op):
        e = self.e[eng]
        return self.op(eng, lambda: e.tensor_reduce(out=out, in_=in_, axis=AX.X, op=op), [in_], [out])

    def recip(self, out, in_):
        nc = self.nc
        return self.op("dve", lambda: nc.vector.reciprocal(out=out, in_=in_), [in_], [out])

    def vmax(self, out, in_):
        nc = self.nc
        return self.op("dve", lambda: nc.vector.max(out=out, in_=in_), [in_], [out])

    def vmax_index(self, out, in_max, in_values):
        nc = self.nc
        return self.op("dve", lambda: nc.vector.max_index(out=out, in_max=in_max, in_values=in_values),
                       [in_max, in_values], [out])

    def vmatch_replace(self, out, in_to_replace, in_values, imm):
        nc = self.nc
        return self.op("dve", lambda: nc.vector.match_replace(out=out, in_to_replace=in_to_replace,
                                                              in_values=in_values, imm_value=imm),
                       [in_to_replace, in_values], [out])

    def dma(self, q, out, in_):
        e = self.e[q]
        return self.op(q, lambda: e.dma_start(out=out, in_=in_), [in_], [out], dma=True)


C_ID = 0
C_TRI = 128
C_ONE = 256
C_BD = 384
C_SGM = 512
C_IOTA = 640
C_BM = 768
C_EPS = 776
C_NH = 784
C_AM = 792
NCST = C_AM + 4 * 512


def _consts():
    c = np.zeros((128, NCST), np.float32)
    i = np.arange(128)
    c[:, C_ID:C_ID + 128] = np.eye(128)
    c[:, C_TRI:C_TRI + 128] = (i[:, None] >= i[None, :])
    c[:, C_ONE:C_ONE + 128] = 1.0
    c[:, C_BD:C_BD + 128] = (i[:, None] // 64 == i[None, :] // 64)
    c[:, C_SGM:C_SGM + 128] = (i[:, None] <= i[None, :])
    c[:, C_IOTA:C_IOTA + 128] = i[None, :]
    c[:, C_BM:C_BM + 8] = (i[:, None] // 16 == np.arange(8)[None, :])
    c[:, C_EPS] = EPS
    c[:, C_NH:C_NH + 8] = -0.5
    t = np.arange(512)
    for r in range(4):
        c[:, C_AM + r * 512:C_AM + (r + 1) * 512] = (t[None, :] > r * 128 + i[:, None])
    return c


NWARM = 3
ADEP = 1
NWARM2 = 0
NPULL = 3


def build(dbg=(), stages="all", nseq=2):
    nc = bass.Bass("TRN2", target_bir_lowering=False)
    es = ExitStack()
    P = Prog(nc, es)
    dram_in = {}

    def din(name, shape, dt=F32):
        dram_in[name] = nc.dram_tensor(name, list(shape), dt, kind="ExternalInput").ap()
        return dram_in[name]

    _shapes = {"x": [2, T, D], "cT": [128, 8, 2], "ada_w": [L, D, 6 * D], "ada_bT": [128, L, 48],
               "n1gT": [128, L, 8], "n2gT": [128, L, 8], "w_in": [L, D, 2560], "sgu_wT": [128, L, 8, 128],
               "sgu_bB": [128, L, 4, 128], "onaT": [128, L, 4], "onbT": [128, L, 4], "w_out": [L, D, D],
               "peer_wq": [L, D, D], "k1T": [64, L, 128], "k2T": [64, L, 128], "peer_uT": [L, D, 16384],
               "peer_v": [L, 16384, D], "fgT": [128, 8], "consts": [128, NCST]}

    class _DI:
        def __getattr__(self, name):
            if name not in dram_in:
                din(name, _shapes[name])
            return dram_in[name]
    DI = _DI()
    out_d = nc.dram_tensor("out", [2, T, D], F32, kind="ExternalOutput").ap()
    dbg_out = {}

    _uid = [0]

    def sb(name, shape, dt=F32, stack=es):
        _uid[0] += 1
        return stack.enter_context(nc.sbuf_tensor("s%d_%s" % (_uid[0], name), list(shape), dt))

    def tap(name, ap, shape, dt):
        if name in dbg:
            o = nc.dram_tensor("dbg_" + name, list(shape), dt, kind="ExternalOutput").ap()
            dbg_out[name] = o
            P.dma("sp", o, ap)

    ps = es.enter_context(nc.psum_tensor("ps", [128, 8, 512], F32))
    psb = ps[:].bitcast(BF16)
    bank_i = [0]

    def nb():
        b = bank_i[0]
        bank_i[0] = (b + 1) % 8
        return b

    cst = sb("cst", [128, NCST])
    cb = sb("cb", [128, 784], BF16)
    xT = sb("xT", [128, 8, T])
    hT = sb("hT", [128, 8, T], BF16)
    modT = sb("modT", [128, L, 6, 8, 2])
    gsc = sb("gsc", [128, L, 2, 2, 8])
    small = sb("small", [128, 64])
    P.dma("sp", cst[:], DI.consts[:, :])
    ident = cst[:, C_ID:C_ID + 128]
    ones_f = cst[:, C_ONE:C_ONE + 128]
    eps_c = cst[:, C_EPS:C_EPS + 1]
    P.copy("dve", cb[:, 0:784], cst[:, 0:784])
    ident_b = cb[:, 0:128]
    tri_b = cb[:, 128:256]
    ones_b = cb[:, 256:384]
    bd_b = cb[:, 384:512]
    iota_b = cb[:, C_IOTA:C_IOTA + 128]
    bm_b = cb[:, C_BM:C_BM + 8]

    with ExitStack() as st:
        cT = sb("cT", [128, 8, 2], F32, st)
        cact = sb("cact", [128, 8, 2], F32, st)
        adab = sb("adab", [128, L, 48], F32, st)
        n1g = sb("n1g", [128, L, 8], F32, st)
        n2g = sb("n2g", [128, L, 8], F32, st)
        wada = [sb("wada%d" % i, [128, 8, 1024], F32, st) for i in range(2)]
        P.dma("sp", cT[:], DI.cT[:, :, :])
        P.dma("sp", adab[:], DI.ada_bT[:, :, :])
        P.dma("sp", n1g[:], DI.n1gT[:, :, :])
        P.dma("sp", n2g[:], DI.n2gT[:, :, :])
        P.act(cact[:], cT[:], AF.Silu)
        it = 0
        for l in range(L):
            for js in range(6):
                w = wada[it % 2]
                it += 1
                for k in range(8):
                    P.dma("sp", w[:, k, :], DI.ada_w[l, k * 128:(k + 1) * 128, js * 1024:(js + 1) * 1024])
                for ec in range(8):
                    b = nb()
                    for k in range(8):
                        P.mm(ps[:, b, 0:2], w[:, k, ec * 128:(ec + 1) * 128], cact[:, k, :], start=(k == 0), stop=(k == 7))
                    P.ts("dve", modT[:, l, js, ec, :], ps[:, b, 0:2], adab[:, l, js * 8 + ec:js * 8 + ec + 1], None, ALU.add)
        for l in range(L):
            for s in range(2):
                for w, (j, g) in enumerate(((1, n1g), (4, n2g))):
                    P.ts("dve", small[:, 0:8], modT[:, l, j, :, s], 1.0, None, ALU.add)
                    P.tt("dve", gsc[:, l, w, s, :], small[:, 0:8], g[:, l, :], ALU.mult)
        tap("modT", modT[:], [128, L, 6, 8, 2], F32)
        P.flush()

    def load_x(s):
        with ExitStack() as st:
            xin = [sb("xin%d" % i, [128, D], F32, st) for i in range(2)]
            for tb in range(16):
                xi = xin[tb % 2]
                P.dma("sp", xi[:], DI.x[s, tb * 128:(tb + 1) * 128, :])
                for kh in range(2):
                    b = nb()
                    for kk in range(4):
                        k = kh * 4 + kk
                        P.tr(ps[:, b, kk * 128:(kk + 1) * 128], xi[:, k * 128:(k + 1) * 128], ident)
                    P.copy("act" if kh == 0 else "dve", xT[:, kh * 4:(kh + 1) * 4, tb * 128:(tb + 1) * 128],
                           ps[:, b, :].rearrange("p (a b) -> p a b", a=4))
            P.flush()

    def rms_bcast(st, tq, tagsrc):
        sq = [sb("sq%d" % i, [128, 512], F32, st) for i in range(2)]
        rs = sb("rs", [128, 512], F32, st)
        return sq, rs

    def norm_mod(l, s, w):
        jsh = 0 if w == 0 else 3
        with ExitStack() as st:
            sq = [sb("sq%d" % i, [128, 512], F32, st) for i in range(3)]
            rs = [sb("rs%d" % i, [128, 512], F32, st) for i in range(2)]
            tmp = [sb("nt%d" % i, [128, 512], F32, st) for i in range(3)]
            c = [0, 0]

            def s1(tq):
                tsl = slice(tq * 512, (tq + 1) * 512)
                b = nb()
                for k in range(8):
                    q = sq[c[0] % 3]
                    c[0] += 1
                    P.tt("pool", q[:], xT[:, k, tsl], xT[:, k, tsl], ALU.mult)
                    P.mm(ps[:, b, :], ones_f, q[:], start=(k == 0), stop=(k == 7))
                r = rs[tq % 2]
                P.act(r[:], ps[:, b, :], AF.Ln, bias=eps_c, scale=1.0 / D)
                P.act(r[:], r[:], AF.Exp, scale=-0.5)

            def s2(tq):
                tsl = slice(tq * 512, (tq + 1) * 512)
                r = rs[tq % 2]
                for k in range(8):
                    tm = tmp[c[1] % 3]
                    c[1] += 1
                    P.tt("dve", tm[:], xT[:, k, tsl], r[:], ALU.mult)
                    P.act(hT[:, k, tsl], tm[:], AF.Identity, bias=modT[:, l, jsh, k, s:s + 1],
                          scale=gsc[:, l, w, s, k:k + 1])

            for tq in range(5):
                if tq < 4:
                    s1(tq)
                if tq >= 1:
                    s2(tq - 1)
            P.flush()

    def bcast(ap, dims):
        return bass.AP(ap.tensor, ap.offset, [list(ap.ap[0])] + [list(d) for d in dims])

    one_c = cst[:, C_ONE:C_ONE + 1]

    def load_wslice(dst, src2d, c0, ncols):
        for k in range(8):
            P.dma("pool", dst[:, k, :], src2d[k * 128:(k + 1) * 128, c0:c0 + ncols])

    def proj_fm(w, dstfn, evac):
        for cc in range(4):
            for tq in range(4):
                tsl = slice(tq * 512, (tq + 1) * 512)
                b = nb()
                for k in range(8):
                    P.mm(ps[:, b, :], w[:, k, cc * 128:(cc + 1) * 128], hT[:, k, tsl], start=(k == 0), stop=(k == 7))
                evac(dstfn(cc, tsl), ps[:, b, :], cc * 4 + tq)

    def group_norm(yT, gain, l):
        with ExitStack() as st:
            sqb = [sb("gq%d" % i, [128, 512], BF16, st) for i in range(3)]
            rs = [sb("gr%d" % i, [128, 512], F32, st) for i in range(3)]
            tf = [sb("gt%d" % i, [128, 512], F32, st) for i in range(3)]
            blks = [(cc, tq) for cc in range(4) for tq in range(4)]

            def s1(i):
                cc, tq = blks[i]
                tsl = slice(tq * 512, (tq + 1) * 512)
                q, r = sqb[i % 3], rs[i % 3]
                P.tt("pool", q[:], yT[:, cc, tsl], yT[:, cc, tsl], ALU.mult)
                b = nb()
                P.mm(ps[:, b, :], bd_b, q[:])
                P.act(r[:], ps[:, b, :], AF.Ln, bias=eps_c, scale=1.0 / 64)
                P.act(r[:], r[:], AF.Exp, scale=-0.5)

            def s2(i):
                cc, tq = blks[i]
                tsl = slice(tq * 512, (tq + 1) * 512)
                r, t_ = rs[i % 3], tf[i % 3]
                P.tt("dve", t_[:], yT[:, cc, tsl], r[:], ALU.mult)
                P.act(yT[:, cc, tsl], t_[:], AF.Identity, scale=gain[:, l, cc:cc + 1])

            for i in range(len(blks) + 1):
                if i < len(blks):
                    s1(i)
                if i >= 1:
                    s2(i - 1)
            P.flush()

    def gmlp(l, s, yaT):
        with ExitStack() as st:
            wu = sb("wu", [128, 8, 512], BF16, st)
            wv = sb("wv", [128, 8, 512], BF16, st)
            sw = sb("sw", [128, 8, 128], F32, st)
            swb = sb("swb", [128, 8, 128], BF16, st)
            bsB = sb("bsB", [128, 4, 128], F32, st)
            st8 = [sb("st8%d" % i, [128, 40], F32, st) for i in range(2)]
            vv = [sb("vv%d" % i, [128, 512], F32, st) for i in range(3)]
            cen = [sb("cen%d" % i, [128, 512], F32, st) for i in range(2)]
            sqv = [sb("sqv%d" % i, [128, 512], F32, st) for i in range(2)]
            vn = [sb("vn%d" % i, [128, 512], BF16, st) for i in range(3)]
            tmp = [sb("gtmp%d" % i, [128, 128], F32, st) for i in range(4)]
            load_wslice(wu, DI.w_in[l], 0, 512)
            load_wslice(wv, DI.w_in[l], 512, 512)
            P.dma("sp", sw[:], DI.sgu_wT[:, l, :, :])
            P.dma("sp", bsB[:], DI.sgu_bB[:, l, :, :])
            for g in range(8):
                P.tt("dve", swb[:, g, :], sw[:, g, :], cst[:, C_SGM:C_SGM + 128], ALU.mult)
            proj_fm(wu, lambda cc, tsl: yaT[:, cc, tsl], lambda o, p_, i: P.act(o, p_, AF.Gelu_apprx_tanh))
            nhalf = cst[:, C_NH:C_NH + 8]
            it = [0]

            def stA(n):
                nsl = slice(n * 128, (n + 1) * 128)
                b = n % 2
                for k in range(8):
                    P.mm(ps[:, b, :], hT[:, k, nsl], wv[:, k, :], start=(k == 0), stop=(k == 7))
                P.act(vv[n % 3][:], ps[:, b, :], AF.Gelu_apprx_tanh)

            def stB(n):
                v, ce_, sq_, s8, vb_ = vv[n % 3], cen[n % 2], sqv[n % 2], st8[n % 2], vn[n % 3]
                v3 = v[:].rearrange("p (g d) -> p g d", g=8)
                cen3 = ce_[:].rearrange("p (g d) -> p g d", g=8)
                sq3 = sq_[:].rearrange("p (g d) -> p g d", g=8)
                P.reduce("dve", s8[:, 0:8], v3, ALU.add)
                P.ts("dve", s8[:, 8:16], s8[:, 0:8], -1.0 / 64, None, ALU.mult)
                P.tt("dve", cen3, v3, bcast(s8[:, 8:16], [[1, 8], [0, 64]]), ALU.add)
                P.tt("pool", sq_[:], ce_[:], ce_[:], ALU.mult)
                P.reduce("dve", s8[:, 16:24], sq3, ALU.add)
                P.ts("dve", s8[:, 24:32], s8[:, 16:24], 1.0 / 64, EPS, ALU.mult, ALU.add)
                P.tt("pool", s8[:, 32:40], s8[:, 24:32], nhalf, ALU.pow)
                P.tt("dve", vb_[:].rearrange("p (g d) -> p g d", g=8), cen3, bcast(s8[:, 32:40], [[1, 8], [0, 64]]), ALU.mult)

            def stC(n):
                nsl = slice(n * 128, (n + 1) * 128)
                vb_ = vn[n % 3]
                for cc in range(4):
                    for half in range(2):
                        g = 2 * cc + half
                        rows = slice(half * 64, half * 64 + 64)
                        b2 = 2 + it[0] % 6
                        P.mm(ps[:, b2, 0:128], vb_[:, cc * 128:(cc + 1) * 128], swb[:, g, :])
                        tm = tmp[it[0] % 4]
                        it[0] += 1
                        P.tt("dve", tm[rows, :], ps[rows, b2, 0:128], bsB[rows, cc, :], ALU.add)
                        P.tt("pool", yaT[rows, cc, nsl], tm[rows, :], yaT[rows, cc, nsl], ALU.mult)

            for n in range(16 + 2):
                if n < 16:
                    stA(n)
                if 1 <= n < 17:
                    stB(n - 1)
                if n >= 2:
                    stC(n - 2)
            P.flush()

    def attention(l, s, qT):
        with ExitStack() as st:
            kTn = sb("kTn", [128, 4, T], BF16, st)
            vtok = sb("vtok", [128, 16, 512], BF16, st)
            with ExitStack() as st2:
                ws = [sb("wqk%d" % i, [128, 8, 512], BF16, st2) for i in range(2)]
                load_wslice(ws[0], DI.w_in[l], 1024, 512)
                load_wslice(ws[1], DI.w_in[l], 1536, 512)
                proj_fm(ws[0], lambda cc, tsl: qT[:, cc, tsl],
                        lambda o, p_, i: P.copy("act" if i % 2 else "dve", o, p_))
                proj_fm(ws[1], lambda cc, tsl: kTn[:, cc, tsl],
                        lambda o, p_, i: (P.act(o, p_, AF.Copy, scale=-0.125) if i % 2 else
                                          P.ts("dve", o, p_, -0.125, None, ALU.mult)))
                P.flush(barrier=False)
                load_wslice(ws[0], DI.w_in[l], 2048, 512)
                for n in range(16):
                    b = nb()
                    for k in range(8):
                        P.mm(ps[:, b, :], hT[:, k, n * 128:(n + 1) * 128], ws[0][:, k, :], start=(k == 0), stop=(k == 7))
                    P.copy("act" if n % 2 else "dve", vtok[:, n, :], ps[:, b, :])
                P.flush()
            tap("qT_%d_%d" % (l, s), qT[:], [128, 4, T], BF16)
            tap("kTn_%d_%d" % (l, s), kTn[:], [128, 4, T], BF16)
            tap("vtok_%d_%d" % (l, s), vtok[:], [128, 16, 512], BF16)
            with ExitStack() as st2:
                NB_ = 4
                E = [sb("aE%d" % i, [128, 512], F32, st2) for i in range(NB_)]
                Lf = [sb("aLf%d" % i, [128, 512], F32, st2) for i in range(2)]
                Lb = [sb("aLb%d" % i, [128, 512], BF16, st2) for i in range(NB_)]
                SL = sb("aSL", [128, 512], F32, st2)
                SLb = [sb("aSLb%d" % i, [128, 512], BF16, st2) for i in range(NB_)]
                af = [sb("aaf%d" % i, [128, 512], F32, st2) for i in range(2)]
                aa = [sb("aa%d" % i, [128, 512], BF16, st2) for i in range(NB_)]
                blocks = []
                gi = 0
                for h in range(8):
                    for qb in range(4):
                        nkb = 4 * (qb + 1)
                        for kb in reversed(range(nkb)):
                            blocks.append((h, qb, kb, kb == nkb - 1, kb == 0, 5 + gi % 2))
                        gi += 1
                nd = [0]

                def S1(i):
                    h, qb, kb, first, last, bo = blocks[i]
                    cc = h // 2
                    rows = slice((h % 2) * 64, (h % 2) * 64 + 64)
                    tsl = slice(qb * 512, (qb + 1) * 512)
                    ksl = slice(kb * 128, (kb + 1) * 128)
                    r = kb - 4 * qb
                    bz = i % 5
                    e_, lb_ = E[i % NB_], Lb[i % NB_]
                    P.mm(ps[:, bz, :], kTn[rows, cc, ksl], qT[rows, cc, tsl], start=True, stop=True)
                    for _ in range(NWARM):
                        P.mm(ps[:, 7, :], ones_b, cb[:, 0:512])
                    P.act(e_[:], ps[:, bz, :], AF.Exp, scale=-1.0)
                    if r >= 0:
                        am = cst[:, C_AM + r * 512:C_AM + (r + 1) * 512]
                        lf_ = Lf[nd[0] % 2]
                        nd[0] += 1
                        P.act(lf_[:], e_[:], AF.Ln, bias=one_c)
                        P.tt("pool", lb_[:], lf_[:], am, ALU.mult)
                    else:
                        P.act(lb_[:], e_[:], AF.Ln, bias=one_c)
                    if not last:
                        if first:
                            P.copy("dve", SL[:], lb_[:])
                            P.copy("dve", SLb[i % NB_][:], lb_[:])
                        else:
                            P.tt("dve", SL[:], SL[:], lb_[:], ALU.add)
                            P.copy("dve", SLb[i % NB_][:], SL[:])

                def S2(i):
                    h, qb, kb, first, last, bo = blocks[i]
                    cc = h // 2
                    rows = slice((h % 2) * 64, (h % 2) * 64 + 64)
                    tsl = slice(qb * 512, (qb + 1) * 512)
                    ksl = slice(kb * 128, (kb + 1) * 128)
                    r = kb - 4 * qb
                    bc_ = i % 5
                    lb_, a_ = Lb[i % NB_], aa[i % NB_]
                    P.mm(ps[:, bc_, :], tri_b, lb_[:], start=False, stop=first)
                    if not first:
                        P.mm(ps[:, bc_, :], ones_b, SLb[(i - 1) % NB_][:], start=False, stop=True)
                    if r >= 0:
                        am = cst[:, C_AM + r * 512:C_AM + (r + 1) * 512]
                        af_ = af[i % 2]
                        P.act(af_[:], ps[:, bc_, :], AF.Exp, scale=-1.0)
                        P.tt("pool", a_[:], af_[:], am, ALU.mult)
                    else:
                        P.act(a_[:], ps[:, bc_, :], AF.Exp, scale=-1.0)

                def S3(i):
                    h, qb, kb, first, last, bo = blocks[i]
                    cc = h // 2
                    rows = slice((h % 2) * 64, (h % 2) * 64 + 64)
                    tsl = slice(qb * 512, (qb + 1) * 512)
                    P.mm(ps[:, bo, :], vtok[:, kb, cc * 128:(cc + 1) * 128], aa[i % NB_][:], start=first, stop=last)
                    if last:
                        P.copy("dve", qT[rows, cc, tsl], ps[rows, bo, :])

                nblk = len(blocks)
                for n in range(nblk + ADEP + 1):
                    if n < nblk:
                        S1(n)
                    if 0 <= n - ADEP < nblk:
                        S2(n - ADEP)
                    if 0 <= n - ADEP - 1 < nblk:
                        S3(n - ADEP - 1)
                    if n % 16 == 15:
                        P.flush(barrier=False)
                P.flush()

    def out_proj(l, s, yaT, ybT):
        with ExitStack() as st:
            wo = sb("wo", [128, 8, D], BF16, st)
            load_wslice(wo, DI.w_out[l], 0, D)
            for dc in range(8):
                for tq in range(4):
                    tsl = slice(tq * 512, (tq + 1) * 512)
                    b = nb()
                    for c8 in range(8):
                        src = yaT[:, c8, tsl] if c8 < 4 else ybT[:, c8 - 4, tsl]
                        P.mm(ps[:, b, :], wo[:, c8, dc * 128:(dc + 1) * 128], src, start=(c8 == 0), stop=(c8 == 7))
                    P.stt("dve", xT[:, dc, tsl], ps[:, b, :], modT[:, l, 2, dc, s:s + 1], xT[:, dc, tsl], ALU.mult, ALU.add)
            P.flush()

    Wd = nc.dram_tensor("Wd_scratch", [32, 128, 128, 64], BF16, kind="Internal").ap()
    GA = 4
    NEG = 32

    def peer_wbuild(l, s):
        with ExitStack() as st:
            wq = sb("wq", [128, 8, D], BF16, st)
            kbd = sb("kbd", [128, 256], BF16, st)
            qp = sb("qp", [128, 8, 512], BF16, st)
            S = sb("pS", [128, 8, 256], F32, st)
            V12 = sb("pV12", [128, 8, 2, 16], F32, st)
            I12 = sb("pI12", [128, 8, 2, 16], U32, st)
            I12f = sb("pI12f", [128, 2, 8, 16], F32, st)
            wk = sb("pwk", [128, 16, 128], F32, st)
            cand = sb("pcand", [128, 4, 256], F32, st)
            c8 = sb("pc8", [128, 8, 16], F32, st)
            cI = sb("pcI", [128, 8, 16], U32, st)
            pf = sb("ppf", [128, 3, 8, 16], F32, st)
            cIj = sb("pcIj", [128, 2, 8, 16], U32, st)
            ohs = [sb("poh%d" % i, [128, 4, 16, 16], F32, st) for i in range(2)]
            ge = sb("pge", [128, 8, 16], F32, st)
            zs = sb("pzs", [128, 16], F32, st)
            abg = sb("pabg", [128, 3, 8, 16], F32, st)
            abgTs = [sb("pabgT%d" % i, [128, 3, 128], BF16, st) for i in range(2)]
            NSUB = 8
            iom = sb("piom", [128, 128, NSUB], BF16, st)
            As = [sb("pA%d" % i, [128, 128, NSUB], BF16, st) for i in range(3)]
            Bs = [sb("pB%d" % i, [128, 128, NSUB], BF16, st) for i in range(3)]
            Wst = sb("pWst", [128, 128, 64], BF16, st)
            iota16 = cst[:, C_IOTA:C_IOTA + 16]
            load_wslice(wq, DI.peer_wq[l], 0, D)
            P.memset("dve", kbd[:], 0.0)
            P.dma("pool", kbd[0:64, 0:128], DI.k1T[:, l, :])
            P.dma("pool", kbd[64:128, 128:256], DI.k2T[:, l, :])
            P.copy("dve", iom[:], bcast(iota_b, [[1, 128], [0, NSUB]]))

            def qproj(tb4):
                tsl = slice(tb4 * 512, (tb4 + 1) * 512)
                for h in range(8):
                    b = nb()
                    for k in range(8):
                        P.mm(ps[:, b, :], wq[:, k, h * 128:(h + 1) * 128], hT[:, k, tsl], start=(k == 0), stop=(k == 7))
                    P.copy("act", qp[:, h, :], ps[:, b, :])

            def qs(tb):
                if tb % 4 == 0:
                    qproj(tb // 4)
                o = (tb % 4) * 128
                for h2 in range(4):
                    b = nb()
                    for h1 in range(2):
                        P.mm(ps[:, b, h1 * 256:(h1 + 1) * 256], qp[:, 2 * h2 + h1, o:o + 128], kbd[:])
                    P.copy("act", S[:, h2 * 2:h2 * 2 + 2, :], ps[:, b, :].rearrange("p (a c) -> p a c", a=2))

            def topk(tb):
                abgT = abgTs[tb % 2]
                grp = [(h, hf) for h in range(8) for hf in range(2)]
                for h, hf in grp:
                    P.vmax(V12[:, h, hf, 0:8], S[:, h, hf * 128:(hf + 1) * 128])
                    if hf: yield
                for h, hf in grp:
                    P.vmax_index(I12[:, h, hf, 0:8], V12[:, h, hf, 0:8], S[:, h, hf * 128:(hf + 1) * 128])
                    if hf and h % 2: yield
                for gi, (h, hf) in enumerate(grp):
                    P.vmatch_replace(wk[:, gi, :], V12[:, h, hf, 0:8], S[:, h, hf * 128:(hf + 1) * 128], -1e30)
                    if hf: yield
                for gi, (h, hf) in enumerate(grp):
                    P.vmax(V12[:, h, hf, 8:16], wk[:, gi, :])
                    if hf: yield
                for gi, (h, hf) in enumerate(grp):
                    P.vmax_index(I12[:, h, hf, 8:16], V12[:, h, hf, 8:16], wk[:, gi, :])
                    if hf and h % 2: yield
                for hf in range(2):
                    P.copy("pool", I12f[:, hf, :, :], I12[:, :, hf, :])
                wk2 = wk[:].rearrange("p (a c) d -> p a (c d)", c=2)
                for hh in range(2):
                    hs = slice(4 * hh, 4 * hh + 4)
                    v1 = V12[:, hs, 0, :]
                    v2 = V12[:, hs, 1, :]
                    cand4 = cand[:].rearrange("p h (i j) -> p h i j", i=16)
                    P.tt("dve", cand4, bcast(v1, [[32, 4], [1, 16], [0, 16]]), bcast(v2, [[32, 4], [0, 16], [1, 16]]), ALU.add)
                    yield
                    for hl in range(4):
                        P.vmax(c8[:, 4 * hh + hl, 0:8], cand[:, hl, :])
                    yield
                    for hl in range(4):
                        P.vmax_index(cI[:, 4 * hh + hl, 0:8], c8[:, 4 * hh + hl, 0:8], cand[:, hl, :])
                    for hl in range(4):
                        P.vmatch_replace(wk2[:, hl, :], c8[:, 4 * hh + hl, 0:8], cand[:, hl, :], -1e30)
                    yield
                    for hl in range(4):
                        P.vmax(c8[:, 4 * hh + hl, 8:16], wk2[:, hl, :])
                    for hl in range(4):
                        P.vmax_index(cI[:, 4 * hh + hl, 8:16], c8[:, 4 * hh + hl, 8:16], wk2[:, hl, :])
                    yield
                P.tt("dve", ge[:], c8[:], bcast(c8[:, :, 0:1], [[16, 8], [0, 16]]), ALU.subtract)
                P.act(ge[:], ge[:], AF.Exp)
                P.reduce("dve", zs[:, 0:8], ge[:], ALU.add)
                P.recip(zs[:, 8:16], zs[:, 0:8])
                P.tt("pool", abg[:, 2, :, :], ge[:], bcast(zs[:, 8:16], [[1, 8], [0, 16]]), ALU.mult)
                yield
                P.ts("dve", cIj[:, 0, :, :], cI[:], 15, None, ALU.bitwise_and)
                P.ts("dve", cIj[:, 1, :, :], cI[:], 4, None, ALU.logical_shift_right)
                P.copy("dve", pf[:, 1, :, :], cIj[:, 0, :, :])
                P.copy("dve", pf[:, 2, :, :], cIj[:, 1, :, :])
                yield
                for hh in range(2):
                    hs = slice(4 * hh, 4 * hh + 4)
                    for w_, (pi, hf) in enumerate(((2, 0), (1, 1))):
                        oh_ = ohs[(2 * hh + w_) % 2]
                        P.tt("dve", oh_[:], bcast(iota16, [[0, 4], [0, 16], [1, 16]]),
                             bcast(pf[:, pi, hs, :], [[16, 4], [1, 16], [0, 16]]), ALU.is_equal)
                        P.tt("dve", oh_[:], oh_[:], bcast(I12f[:, hf, hs, :], [[16, 4], [0, 16], [1, 16]]), ALU.mult)
                        P.reduce("dve", abg[:, w_, hs, :], oh_[:], ALU.add)
                        yield
                for w_ in range(3):
                    b = nb()
                    P.tr(ps[:, b, 0:128], abg[:, w_, :, :].rearrange("p h k -> p (h k)"), ident)
                    P.copy("act", abgT[:, w_, :], ps[:, b, 0:128])
                yield

            ctr = {"sub": 0, "ev": 0}

            def pertoken(tb, bg=None):
                abgT = abgTs[tb % 2]
                for half in range(2):
                    for sub in range(64 // NSUB):
                        t0 = half * 64 + sub * NSUB
                        sbi = ctr["sub"] % 3
                        ctr["sub"] += 1
                        A_, B_ = As[sbi], Bs[sbi]
                        P.tt("dve", A_[:], iom[:], bcast(abgT[:, 0, t0:t0 + NSUB], [[0, 128], [1, NSUB]]), ALU.is_equal)
                        P.tt("dve", B_[:], iom[:], bcast(abgT[:, 1, t0:t0 + NSUB], [[0, 128], [1, NSUB]]), ALU.is_equal)
                        P.tt("pool", B_[:], B_[:], bcast(abgT[:, 2, t0:t0 + NSUB], [[0, 128], [1, NSUB]]), ALU.mult)
                        for t4 in range(NSUB // 4):
                            bW = ctr["ev"] % 8
                            ctr["ev"] += 1
                            pw = ps[:, bW, :]
                            for tt_ in range(4):
                                tl = t4 * 4 + tt_
                                P.mm(bass.AP(pw.tensor, pw.offset + tt_, [list(pw.ap[0]), [4, 128]]), B_[:, :, tl], A_[:, :, tl])
                            tw = sub * NSUB + t4 * 4
                            P.copy("act", bcast(Wst[:, 0, tw:tw + 4], [[64, 128], [1, 4]]),
                                   pw.rearrange("p (a t) -> p a t", t=4))
                        if bg is not None:
                            for _ in range(NPULL):
                                if next(bg, "end") == "end":
                                    bg = None
                                    break
                    P.dma("sp", Wd[tb * 2 + half], Wst[:])
                    P.flush(barrier=False)
                if bg is not None:
                    for _ in bg:
                        pass

            qs(0)
            for _ in topk(0):
                pass
            for tb in range(16):
                if tb + 1 < 16:
                    qs(tb + 1)
                pertoken(tb, topk(tb + 1) if tb + 1 < 16 else None)
            P.flush()

    def peer_dense(l, s):
        with ExitStack() as st:
            UTs = [sb("dU%d" % i, [128, 8, GA * 128], BF16, st) for i in range(2)]
            Vs = [sb("dV%d" % i, [128, GA, D], BF16, st) for i in range(2)]
            Wt = [sb("dW%d" % i, [128, 8, GA, 64], BF16, st) for i in range(2)]
            actT = [sb("dA%d" % i, [128, 512], BF16, st) for i in range(2)]
            Zt = [sb("dZ%d" % i, [128, GA, 512], BF16, st) for i in range(2)]

            def load_tables(eg):
                a0 = eg * GA
                for k in range(8):
                    P.dma("pool", UTs[eg % 2][:, k, :], DI.peer_uT[l, k * 128:(k + 1) * 128, a0 * 128:(a0 + GA) * 128])
                for ga in range(GA):
                    P.dma("pool", Vs[eg % 2][:, ga, :], DI.peer_v[l, (a0 + ga) * 128:(a0 + ga + 1) * 128, :])

            steps = [(eg, tq) for eg in range(NEG) for tq in range(4)]
            ai = [0]
            yi = [0]

            def stageA(i):
                eg, tq = steps[i]
                tsl = slice(tq * 512, (tq + 1) * 512)
                a0 = eg * GA
                if tq == 1 and eg + 1 < NEG:
                    load_tables(eg + 1)
                w_ = Wt[i % 2]
                P.dma("sp", w_[:], Wd[tq * 8:(tq + 1) * 8, :, a0:a0 + GA, :].rearrange("k b a t -> b k a t"))
                for ga in range(GA):
                    b = ai[0] % 2
                    a_ = actT[ai[0] % 2]
                    ai[0] += 1
                    for k in range(8):
                        P.mm(ps[:, b, :], UTs[eg % 2][:, k, ga * 128:(ga + 1) * 128], hT[:, k, tsl], start=(k == 0), stop=(k == 7))
                    P.act(a_[:], ps[:, b, :], AF.Gelu_apprx_tanh)
                    P.tt("dve", Zt[i % 2][:, ga, :].rearrange("p (k t) -> p k t", k=8),
                         a_[:].rearrange("p (k t) -> p k t", k=8), w_[:, :, ga, :], ALU.mult)

            def stageY(i):
                eg, tq = steps[i]
                tsl = slice(tq * 512, (tq + 1) * 512)
                for dc in range(8):
                    b = 2 + yi[0] % 6
                    yi[0] += 1
                    for ga in range(GA):
                        P.mm(ps[:, b, :], Vs[eg % 2][:, ga, dc * 128:(dc + 1) * 128], Zt[i % 2][:, ga, :],
                             start=(ga == 0), stop=(ga == GA - 1))
                    P.stt("dve", xT[:, dc, tsl], ps[:, b, :], modT[:, l, 5, dc, s:s + 1], xT[:, dc, tsl], ALU.mult, ALU.add)

            load_tables(0)
            stageA(0)
            for i in range(len(steps)):
                if i + 1 < len(steps):
                    stageA(i + 1)
                stageY(i)
                if i % 4 == 3:
                    P.flush(barrier=False)
            P.flush()

    def final_out(s):
        with ExitStack() as st:
            fg = sb("fg", [128, 8], F32, st)
            sq = [sb("fsq%d" % i, [128, 512], F32, st) for i in range(2)]
            rs = [sb("frs%d" % i, [128, 512], F32, st) for i in range(2)]
            tmp = [sb("ftm%d" % i, [128, 512], F32, st) for i in range(2)]
            yk = sb("fyk", [128, 8, 512], F32, st)
            ot = [sb("fot%d" % i, [128, D], F32, st) for i in range(2)]
            P.dma("sp", fg[:], DI.fgT[:, :])
            io = 0
            for tq in range(4):
                tsl = slice(tq * 512, (tq + 1) * 512)
                b = nb()
                for k in range(8):
                    q = sq[k % 2]
                    P.tt("pool", q[:], xT[:, k, tsl], xT[:, k, tsl], ALU.mult)
                    P.mm(ps[:, b, :], ones_f, q[:], start=(k == 0), stop=(k == 7))
                r = rs[tq % 2]
                P.act(r[:], ps[:, b, :], AF.Ln, bias=eps_c, scale=1.0 / D)
                P.act(r[:], r[:], AF.Exp, scale=-0.5)
                for k in range(8):
                    tm = tmp[k % 2]
                    P.tt("dve", tm[:], xT[:, k, tsl], r[:], ALU.mult)
                    P.act(yk[:, k, :], tm[:], AF.Identity, scale=fg[:, k:k + 1])
                for t4 in range(4):
                    o = ot[io % 2]
                    io += 1
                    for kh in range(2):
                        b = nb()
                        for kk in range(4):
                            k = kh * 4 + kk
                            P.tr(ps[:, b, kk * 128:(kk + 1) * 128], yk[:, k, t4 * 128:(t4 + 1) * 128], ident)
                        P.copy("act" if kh else "dve", o[:, kh * 512:(kh + 1) * 512], ps[:, b, :])
                    P.dma("sp", out_d[s, tq * 512 + t4 * 128:tq * 512 + (t4 + 1) * 128, :], o[:])
            P.flush()

    ona = sb("ona", [128, L, 4])
    onb = sb("onb", [128, L, 4])
    P.dma("sp", ona[:], DI.onaT[:, :, :])
    P.dma("sp", onb[:], DI.onbT[:, :, :])

    done = False
    for s in range(nseq):
        load_x(s)
        tap("xT%d" % s, xT[:], [128, 8, T], F32)
        for l in range(L):
            if not stages.startswith("peeronly"):
                norm_mod(l, s, 0)
                tap("h1_%d_%d" % (l, s), hT[:], [128, 8, T], BF16)
                if stages == "norm1":
                    done = True
                    break
                with ExitStack() as stl:
                    yaT = sb("yaT", [128, 4, T], BF16, stl)
                    qT = sb("qT", [128, 4, T], BF16, stl)
                    gmlp(l, s, yaT)
                    tap("ya_%d_%d" % (l, s), yaT[:], [128, 4, T], BF16)
                    if stages == "gmlp":
                        done = True
                        P.flush()
                        break
                    attention(l, s, qT)
                    tap("yb_%d_%d" % (l, s), qT[:], [128, 4, T], BF16)
                    if stages == "attn":
                        done = True
                        P.flush()
                        break
                    group_norm(yaT, ona, l)
                    group_norm(qT, onb, l)
                    tap("yan_%d_%d" % (l, s), yaT[:], [128, 4, T], BF16)
                    tap("ybn_%d_%d" % (l, s), qT[:], [128, 4, T], BF16)
                    out_proj(l, s, yaT, qT)
                    tap("xmix_%d_%d" % (l, s), xT[:], [128, 8, T], F32)
                    P.flush()
                if stages == "mix":
                    done = True
                    break
            norm_mod(l, s, 1)
            tap("h2_%d_%d" % (l, s), hT[:], [128, 8, T], BF16)
            peer_wbuild(l, s)
            if "Wd_%d_%d" % (l, s) in dbg:
                o = nc.dram_tensor("dbg_Wd_%d_%d" % (l, s), [32, 128, 128, 64], BF16, kind="ExternalOutput").ap()
                P.dma("sp", o, Wd)
                P.flush()
            if stages in ("wbuild", "peeronly_wbuild"):
                done = True
                break
            peer_dense(l, s)
            tap("xl_%d_%d" % (l, s), xT[:], [128, 8, T], F32)
            if stages in ("layer0", "peeronly"):
                done = True
                break
        if done:
            break
        final_out(s)

    P.flush()
    return nc, P, dbg_out, list(dram_in.keys())


def host_inputs(inputs, core):
    f = np.float32
    b0 = 2 * core
    m = {}
    m["x"] = np.ascontiguousarray(inputs["x"][b0:b0 + 2])
    c = inputs["c"][b0:b0 + 2]
    m["cT"] = np.ascontiguousarray(c.reshape(2, 8, 128).transpose(2, 1, 0))
    m["ada_w"] = inputs["ada_w"]
    m["ada_bT"] = np.ascontiguousarray(inputs["ada_b"].reshape(L, 48, 128).transpose(2, 0, 1))
    m["n1gT"] = np.ascontiguousarray(inputs["norm1_g"].reshape(L, 8, 128).transpose(2, 0, 1))
    m["n2gT"] = np.ascontiguousarray(inputs["norm2_g"].reshape(L, 8, 128).transpose(2, 0, 1))
    m["w_in"] = inputs["w_in"]
    m["sgu_wT"] = np.ascontiguousarray(inputs["sgu_w"].transpose(3, 0, 1, 2))
    sb_ = inputs["sgu_b"]
    m["sgu_bB"] = np.ascontiguousarray(np.repeat(sb_.reshape(L, 4, 2, 1, 128), 64, axis=3)
                                       .reshape(L, 4, 128, 128).transpose(2, 0, 1, 3))
    m["onaT"] = np.ascontiguousarray(inputs["out_norm_a"].reshape(L, 4, 128).transpose(2, 0, 1))
    m["onbT"] = np.ascontiguousarray(inputs["out_norm_b"].reshape(L, 4, 128).transpose(2, 0, 1))
    m["w_out"] = inputs["w_out"]
    m["peer_wq"] = inputs["peer_wq"]
    m["k1T"] = np.ascontiguousarray(inputs["peer_k1"].transpose(2, 0, 1))
    m["k2T"] = np.ascontiguousarray(inputs["peer_k2"].transpose(2, 0, 1))
    m["peer_uT"] = inputs["_peer_uT"]
    m["peer_v"] = inputs["peer_v"]
    m["fgT"] = np.ascontiguousarray(inputs["final_g"].reshape(8, 128).T)
    m["consts"] = inputs["_consts"]
    return {k: np.asarray(v, dtype=f) for k, v in m.items()}


def kernel(**inputs):
    inputs = {k: np.asarray(v) for k, v in inputs.items()}
    inputs["_peer_uT"] = np.ascontiguousarray(inputs["peer_u"].transpose(0, 2, 1))
    inputs["_consts"] = _consts()
    nc, P, _, used = build()
    in_maps = [{k: v for k, v in host_inputs(inputs, c).items() if k in used} for c in range(NCORES)]
    res = run_bass_kernel_spmd(nc, in_maps, core_ids=list(range(NCORES)))
    out = np.concatenate([r["out"] for r in res.results], axis=0)
    return out.astype(np.float32)
```
